# Optimizing a Trainium2 kernel written in Bass

```python
import jax, jax.numpy as jnp
from jax import lax
import numpy as np

D_MODEL = 1024
BATCH = 16
SEQ = 256
DEPTH = 2
DEC_BATCH = 4
DEC_SEQ = 4096
PAST_LEN = 256

GRID_W = 64
EXPAND = 2
D_INNER = EXPAND * D_MODEL
N_MIXERS = 2
N_FGROUPS = 8
FG_DIM = D_INNER // N_FGROUPS
N_LRU_HEADS = 8
LRU_HEAD_DIM = D_INNER // N_LRU_HEADS
LRU_C = 8.0
CONV_W = 4
CONV_LEFT = 2
EPS = 1e-6
POS_BASE = 10000.0

kernel_name = "hybrid_fnet_rglru_diffusion_step"


def _rmsnorm(x, g):
    xf = x.astype(jnp.float32)
    y = xf * lax.rsqrt(jnp.mean(xf * xf, axis=-1, keepdims=True) + EPS)
    return (y * g.astype(jnp.float32)).astype(x.dtype)


def _ada(cond, w_ada, b_ada):
    m = jax.nn.silu(cond) @ w_ada + b_ada
    shift, scale, gate = jnp.split(m, 3, axis=-1)
    return shift[..., None, :], scale[..., None, :], gate[..., None, :]


def _grid_pos_embed(n_tokens, dim, dtype):
    rows = n_tokens // GRID_W
    r = jnp.broadcast_to(jnp.arange(rows, dtype=jnp.float32)[:, None], (rows, GRID_W)).reshape(-1)
    col = jnp.broadcast_to(jnp.arange(GRID_W, dtype=jnp.float32)[None, :], (rows, GRID_W)).reshape(-1)
    quarter = dim // 4
    omega = 1.0 / (POS_BASE ** (jnp.arange(quarter, dtype=jnp.float32) / quarter))
    er = r[:, None] * omega
    ec = col[:, None] * omega
    return jnp.concatenate([jnp.sin(er), jnp.cos(er), jnp.sin(ec), jnp.cos(ec)], axis=-1).astype(dtype)


def _fourier_mix(u, w_fmix, b_fmix):
    B, S, _ = u.shape
    ug = u.astype(jnp.float32).reshape(B, S, N_FGROUPS, FG_DIM)
    f = jnp.real(jnp.fft.fft2(ug, axes=(1, 3), norm="ortho")).astype(u.dtype)
    y = jnp.einsum("bsgi,gij->bsgj", f, w_fmix).reshape(B, S, D_INNER)
    return y + b_fmix


def _centred_dwconv(u, w, b):
    S = u.shape[1]
    up = jnp.pad(u, ((0, 0), (CONV_LEFT, CONV_W - 1 - CONV_LEFT), (0, 0)))
    out = b
    for k in range(CONV_W):
        out = out + up[:, k:k + S] * w[k]
    return out


def _blockdiag(u, w, b):
    B, S, _ = u.shape
    y = jnp.einsum("bshi,hij->bshj", u.reshape(B, S, N_LRU_HEADS, LRU_HEAD_DIM), w)
    return y.reshape(B, S, D_INNER) + b


def _lin_combine(e1, e2):
    a1, b1 = e1
    a2, b2 = e2
    return a1 * a2, a2 * b1 + b2


def _rglru_dir(u, w_r, b_r, w_i, b_i, lam, h0, reverse):
    S = u.shape[1]
    f32 = jnp.float32
    r = jax.nn.sigmoid(_blockdiag(u, w_r.astype(f32), b_r.astype(f32)))
    i = jax.nn.sigmoid(_blockdiag(u, w_i.astype(f32), b_i.astype(f32)))
    log_a = -LRU_C * r * jax.nn.softplus(-lam.astype(f32))
    a = jnp.exp(log_a)
    bterm = jnp.sqrt(-jnp.expm1(2.0 * log_a)) * (i * u)
    first = S - 1 if reverse else 0
    last = 0 if reverse else S - 1
    bterm = bterm.at[:, first].add(a[:, first] * h0)
    _, h = lax.associative_scan(_lin_combine, (a, bterm), reverse=reverse, axis=1)
    return h, h[:, last]


def _layer(x, cond, p, kind, h0):
    norm_g, w_ada, b_ada, w_in, w_out = p["norm_g"], p["w_ada"], p["b_ada"], p["w_in"], p["w_out"]
    shift, scale, gate = _ada(cond, w_ada, b_ada)
    h = _rmsnorm(x, norm_g) * (1.0 + scale) + shift
    uz = h @ w_in
    u, z = jnp.split(uz, 2, axis=-1)
    state = None
    if kind == 0:
        y = _fourier_mix(u, p["w_fmix"], p["b_fmix"])
    else:
        uc = _centred_dwconv(u, p["conv_w"], p["conv_b"]).astype(jnp.float32)
        yf, hf = _rglru_dir(uc, p["w_rgate"][0], p["b_rgate"][0], p["w_igate"][0], p["b_igate"][0], p["lam"][0], h0[:, 0], False)
        yb, hb = _rglru_dir(uc, p["w_rgate"][1], p["b_rgate"][1], p["w_igate"][1], p["b_igate"][1], p["lam"][1], h0[:, 1], True)
        y = (yf + yb).astype(x.dtype)
        state = jnp.stack([hf, hb], axis=1)
    out = (y * jax.nn.silu(z)) @ w_out
    return x + gate * out, state


def setup_inputs(seed: int = 0) -> dict:
    key = jax.random.key(seed)
    ks = iter(jax.random.split(key, 40))
    nrm = lambda shape, s: jax.random.normal(next(ks), shape, jnp.float32) * s
    D, DI = D_MODEL, D_INNER

    def lam_init():
        u = jax.random.uniform(next(ks), (2, DI), jnp.float32, 0.9, 0.999)
        p = u ** (1.0 / LRU_C)
        return jnp.log(p) - jnp.log1p(-p)

    return {
        "x_prompt": nrm((BATCH, SEQ, D), 1.0),
        "x_sample": nrm((DEC_BATCH, DEC_SEQ, D), 1.0),
        "state_lru_1": nrm((DEC_BATCH, 2, DI), 0.5),
        "c": nrm((DEC_BATCH, D), 1.0),
        "c_ctx": nrm((D,), 1.0),
        "norm_g_0": 1.0 + nrm((D,), 0.02),
        "w_ada_0": nrm((D, 3 * D), 0.5 * D ** -0.5),
        "b_ada_0": nrm((3 * D,), 0.02),
        "w_in_0": nrm((D, 2 * DI), D ** -0.5),
        "w_fmix_0": nrm((N_FGROUPS, FG_DIM, FG_DIM), FG_DIM ** -0.5),
        "b_fmix_0": nrm((DI,), 0.02),
        "w_out_0": nrm((DI, D), DI ** -0.5),
        "norm_g_1": 1.0 + nrm((D,), 0.02),
        "w_ada_1": nrm((D, 3 * D), 0.5 * D ** -0.5),
        "b_ada_1": nrm((3 * D,), 0.02),
        "w_in_1": nrm((D, 2 * DI), D ** -0.5),
        "conv_w_1": nrm((CONV_W, DI), CONV_W ** -0.5),
        "conv_b_1": nrm((DI,), 0.02),
        "w_rgate_1": nrm((2, N_LRU_HEADS, LRU_HEAD_DIM, LRU_HEAD_DIM), LRU_HEAD_DIM ** -0.5),
        "b_rgate_1": nrm((2, DI), 0.02),
        "w_igate_1": nrm((2, N_LRU_HEADS, LRU_HEAD_DIM, LRU_HEAD_DIM), LRU_HEAD_DIM ** -0.5),
        "b_igate_1": nrm((2, DI), 0.02),
        "lam_1": lam_init(),
        "w_out_1": nrm((DI, D), DI ** -0.5),
        "final_g": 1.0 + nrm((D,), 0.02),
    }


def reference(x_prompt, x_sample, state_lru_1, c, c_ctx,
              norm_g_0, w_ada_0, b_ada_0, w_in_0, w_fmix_0, b_fmix_0, w_out_0,
              norm_g_1, w_ada_1, b_ada_1, w_in_1, conv_w_1, conv_b_1,
              w_rgate_1, b_rgate_1, w_igate_1, b_igate_1, lam_1, w_out_1,
              final_g):
    layer_params = [
        {"norm_g": norm_g_0, "w_ada": w_ada_0, "b_ada": b_ada_0, "w_in": w_in_0,
         "w_fmix": w_fmix_0, "b_fmix": b_fmix_0, "w_out": w_out_0},
        {"norm_g": norm_g_1, "w_ada": w_ada_1, "b_ada": b_ada_1, "w_in": w_in_1,
         "conv_w": conv_w_1, "conv_b": conv_b_1, "w_rgate": w_rgate_1, "b_rgate": b_rgate_1,
         "w_igate": w_igate_1, "b_igate": b_igate_1, "lam": lam_1, "w_out": w_out_1},
    ]
    cached_states = {1: state_lru_1}

    xc = x_prompt
    new_states = {}
    zero_h = jnp.zeros((x_prompt.shape[0], 2, D_INNER), jnp.float32)
    for i in range(DEPTH):
        kind = i % N_MIXERS
        xc, st = _layer(xc, c_ctx, layer_params[i], kind, zero_h)
        if kind == 1:
            new_states[i] = st
    y_prompt = _rmsnorm(xc, final_g)
    new_state_lru_1 = new_states[1].astype(x_prompt.dtype)

    xs = x_sample + _grid_pos_embed(x_sample.shape[1], D_MODEL, x_sample.dtype)[None]
    dummy_h = jnp.zeros((x_sample.shape[0], 2, D_INNER), jnp.float32)
    for i in range(DEPTH):
        kind = i % N_MIXERS
        h0 = cached_states[i].astype(jnp.float32) if kind == 1 else dummy_h
        xs, _ = _layer(xs, c, layer_params[i], kind, h0)
    y_sample = _rmsnorm(xs, final_g)

    return (y_prompt, y_sample, new_state_lru_1)
```

```python
import numpy as np
from contextlib import ExitStack
import concourse.bass as bass
import concourse.mybir as mybir
from concourse.bass_utils import run_bass_kernel_spmd

F32 = mybir.dt.float32
BF16 = mybir.dt.bfloat16
AF = mybir.ActivationFunctionType
ALU = mybir.AluOpType

D = 1024
DI = 2048
EPS = 1e-6
GRID_W = 64
POS_BASE = 10000.0
NCORES = 8
TS = 2048
TP = 256
NW = 53100
BLK = 64


def pos_embed(n_tokens, dim):
    rows = n_tokens // GRID_W
    r = np.broadcast_to(np.arange(rows, dtype=np.float32)[:, None], (rows, GRID_W)).reshape(-1)
    col = np.broadcast_to(np.arange(GRID_W, dtype=np.float32)[None, :], (rows, GRID_W)).reshape(-1)
    quarter = dim // 4
    omega = (1.0 / (np.float32(POS_BASE) ** (np.arange(quarter, dtype=np.float32) / np.float32(quarter)))).astype(np.float32)
    er = (r[:, None] * omega).astype(np.float32)
    ec = (col[:, None] * omega).astype(np.float32)
    return np.concatenate([np.sin(er), np.cos(er), np.sin(ec), np.cos(ec)], axis=-1).astype(np.float32)


def s1_table(S, N2, alpha, beta, gamma, delta):
    m1 = np.arange(128)[:, None]
    k1 = np.arange(128)[None, :]
    out = np.zeros((N2, 128, 256), np.float64)
    for m2 in range(N2):
        m = N2 * m1 + m2
        ph = -2 * np.pi * (((alpha * m + beta) * (gamma * k1 + delta)) % S) / S
        out[m2, :, :128] = np.cos(ph)
        out[m2, :, 128:] = np.sin(ph)
    return out / np.sqrt(128.0)


def s2_kron(S2c, N2, flip):
    NI = 128 // N2
    Kr = np.zeros((128, 128))
    Ki = np.zeros((128, 128))
    for m2 in range(N2):
        for k2 in range(N2):
            for i in range(NI):
                ip = (NI - 1 - i) if flip else i
                Kr[m2 * NI + i, k2 * NI + ip] = S2c[m2, k2].real
                Ki[m2 * NI + i, k2 * NI + ip] = -S2c[m2, k2].imag
    return Kr, Ki


def dft_tables_prompt(rev):
    S = 256
    N2 = 2
    if rev:
        alpha, beta, gamma, delta = -1, 255, -1, 255
    else:
        alpha, beta, gamma, delta = 1, 0, 1, 0
    s1 = s1_table(S, N2, alpha, beta, gamma, delta)
    m2 = np.arange(N2)[:, None]
    k2 = np.arange(N2)[None, :]
    S2c = np.exp(-2j * np.pi * ((alpha * gamma * 128 * m2 * k2 + beta * gamma * 128 * k2) % S) / S) / np.sqrt(S / 128.0)
    Kr, Ki = s2_kron(S2c, N2, False)
    return s1, Kr, Ki


def dft_tables_sample(sig):
    S = 4096
    N2 = 16
    s1 = s1_table(S, N2, 2, sig, 1, 0)
    m2 = np.arange(N2)[:, None]
    k2 = np.arange(N2)[None, :]
    nrm = np.sqrt(S / 128.0)
    S2A = np.exp(-2j * np.pi * (((2 * m2 + sig) * k2) % 32) / 32) / nrm
    S2B = np.exp(-2j * np.pi * (((2 * m2 + sig) * (31 - k2)) % 32) / 32) / nrm
    KrA, KiA = s2_kron(S2A, N2, False)
    KrB, KiB = s2_kron(S2B, N2, True)
    return s1, KrA, KiA, KrB, KiB


def ch_tables():
    c = np.arange(256)[:, None]
    l = np.arange(256)[None, :]
    ph = 2 * np.pi * ((c * l) % 256) / 256
    C = np.cos(ph) / 16.0
    S_ = np.sin(ph) / 16.0
    return np.stack([np.concatenate([C, -S_], 1), np.concatenate([S_, C], 1)], 0)


class Region:
    def __init__(self, ctx, off, n):
        self.ctx, self.off, self.n = ctx, off, n

    def f32(self):
        return self.ctx.arena[:, self.off:self.off + self.n]

    def bf(self):
        return self.ctx.arena[:, self.off:self.off + self.n].bitcast(BF16)

    def sub(self, lo, n):
        assert lo >= 0 and lo + n <= self.n, (lo, n, self.n)
        return Region(self.ctx, self.off + lo, n)

    def keys(self):
        return [("A", b) for b in range(self.off // BLK, (self.off + self.n - 1) // BLK + 1)]


def K(*items):
    out = []
    for it in items:
        if isinstance(it, Region):
            out.extend(it.keys())
        elif isinstance(it, (list, tuple)) and len(it) > 0 and isinstance(it[0], (Region, list)):
            out.extend(K(*it))
        elif isinstance(it, list):
            out.extend(it)
        else:
            out.append(it)
    return out


class Ctx:
    COMPUTE = ("pe", "act", "dve", "pool")

    def __init__(self, nc, stack, needed=None):
        self.nc = nc
        self.stack = stack
        self.dry = needed is None
        self.needed = needed
        self.need_out = {e: set() for e in self.COMPUTE}
        self.eng = {"pe": nc.tensor, "act": nc.scalar, "dve": nc.vector, "pool": nc.gpsimd, "sp": nc.sync}
        self.csem = {e: stack.enter_context(nc.semaphore("c_" + e)) for e in self.COMPUTE}
        self.ccnt = {e: 0 for e in self.COMPUTE}
        self.rank = None
        if needed is not None:
            self.rank = {}
            for e in self.COMPUTE:
                self.rank[e] = {idx: i + 1 for i, idx in enumerate(sorted(needed[e]))}
        self.dsem = {}
        self.dcnt = {}
        self.waited = {e: {} for e in self.eng}
        self.lastw = {}
        self.readers = {}
        self.nwaits = 0
        self.ninst = {e: 0 for e in self.eng}
        self.arena = stack.enter_context(nc.sbuf_tensor("arena", [128, NW], F32))
        self.top = 0
        self.bankrr = 0
        self.drrq = {}
        self.ptrr = 0

    def alloc(self, n):
        n_al = (n + BLK - 1) // BLK * BLK
        r = Region(self, self.top, n)
        self.top += n_al
        assert self.top <= NW, f"arena overflow {self.top} > {NW}"
        return r

    def mark(self):
        return self.top

    def release(self, m):
        self.top = m

    def _deps(self, e, reads, writes):
        deps = {}

        def add(t):
            if t is None:
                return
            s, v, src = t
            if src == "pe" and e == "pe":
                return
            if s not in deps or deps[s] < v:
                deps[s] = v
        for k in reads:
            add(self.lastw.get(k))
        for k in writes:
            add(self.lastw.get(k))
            for t in self.readers.get(k, ()):
                add(t)
        w = self.waited[e]
        for s, v in deps.items():
            if w.get(s, 0) >= v:
                continue
            w[s] = v
            self.nwaits += 1
            if s in self.COMPUTE:
                self.need_out[s].add(v)
                if not self.dry:
                    self.eng[e].wait_ge(self.csem[s], self.rank[s][v])
            else:
                if not self.dry:
                    self.eng[e].wait_ge(self.dsem[s], v)

    def _record(self, t, reads, writes):
        for k in writes:
            self.lastw[k] = t
            self.readers[k] = []
        for k in reads:
            self.readers.setdefault(k, []).append(t)

    def op(self, e, fn, reads=(), writes=()):
        reads = K(*reads)
        writes = K(*writes)
        self._deps(e, reads, writes)
        self.ccnt[e] += 1
        self.ninst[e] += 1
        idx = self.ccnt[e]
        if not self.dry:
            inst = fn()
            if idx in self.rank[e]:
                inst.then_inc(self.csem[e], 1)
        self._record((e, idx, e), reads, writes)

    NPOOL = 28

    def _dsem(self, name):
        if name not in self.dsem:
            self.dsem[name] = self.stack.enter_context(self.nc.semaphore("d_" + name))
            self.dcnt[name] = 0

    def dma(self, q, semname, fn, reads=(), writes=()):
        reads = K(*reads)
        writes = K(*writes)
        self._deps(q, reads, writes)
        rr = self.drrq.setdefault(q, 0)
        self.drrq[q] = rr + 1
        name = "%s%d" % (q, rr % self.NPOOL)
        self._dsem(name)
        prev = self.dcnt[name]
        if prev > 0 and self.waited[q].get(name, 0) < prev:
            self.waited[q][name] = prev
            self.nwaits += 1
            if not self.dry:
                self.eng[q].wait_ge(self.dsem[name], prev)
        self.dcnt[name] += 16
        self.ninst[q] += 1
        if not self.dry:
            fn().then_inc(self.dsem[name], 16)
        self._record((name, self.dcnt[name], "dma"), reads, writes)

    def cc(self, semname, fn, reads=(), writes=()):
        reads = K(*reads)
        writes = K(*writes)
        self._deps("pool", reads, writes)
        self._dsem(semname)
        self.dcnt[semname] += 1
        if not self.dry:
            fn().then_inc(self.dsem[semname], 1)
        self._record((semname, self.dcnt[semname], "cc"), reads, writes)

    def finish(self, e="sp"):
        allk = list(set(self.lastw) | set(self.readers))
        self._deps(e, allk, allk)


STOP = None
DEBUG = ()


class _Stop(Exception):
    pass


def _emit(nc, c):
    try:
        _emit_body(nc, c)
    except _Stop:
        pass
    c.finish("sp")


def _emit_body(nc, c):
    def chk(name):
        if STOP == name:
            raise _Stop()

    def dump(name, ap_fn, shape, reads, dt=F32):
        if name not in DEBUG:
            return
        d = nc.dram_tensor("dbg_" + name, list(shape), dt, kind="ExternalOutput").ap()
        c.dma("sp", "dbg_" + name, lambda: nc.sync.dma_start(out=d, in_=ap_fn()), reads=reads, writes=[("dbg", name)])
    eng = c.eng
    pe, act, dve, pool, sp = nc.tensor, nc.scalar, nc.vector, nc.gpsimd, nc.sync

    def din(name, shape, dt=F32):
        return nc.dram_tensor(name, list(shape), dt, kind="ExternalInput").ap()

    def dout(name, shape, dt=F32):
        return nc.dram_tensor(name, list(shape), dt, kind="ExternalOutput").ap()

    def dint(name, shape, dt):
        return nc.dram_tensor(name, list(shape), dt, kind="Internal").ap()

    xs_in = din("xs_in", [TS, D]); pos_in = din("pos_in", [TS, D])
    xs_res = din("xs_res", [TS, D]); pos_res = din("pos_res", [TS, D])
    xp = din("xp", [2 * TP, D])
    condT = din("condT", [128, 16]); sel_d = din("sel", [128, 2]); st0_d = din("st0", [128, 16])
    w_ada = [din("w_ada_0", [D, 3 * D]), din("w_ada_1", [D, 3 * D])]
    badar_d = [din("badar_0", [1, 3072]), din("badar_1", [1, 3072])]
    ngr_d = [din("ngr_0", [1, 1024]), din("ngr_1", [1, 1024])]
    fgr_d = din("fgr", [1, 1024])
    w_in = [din("w_in_0", [D, 2 * DI]), din("w_in_1", [D, 2 * DI])]
    w_fmix = din("w_fmix", [8, 256, 256]); bfm_d = din("bfm", [128, 16])
    w_out = [din("w_out_0", [DI, D]), din("w_out_1", [DI, D])]
    convw_d = din("convw", [128, 80]); convb_d = din("convb", [128, 16])
    wr_d = din("wr", [2, 8, 256, 256]); wi_d = din("wi", [2, 8, 256, 256])
    br_d = din("br", [128, 32]); bi_d = din("bi", [128, 32]); lam_d = din("lam", [128, 32])
    ident_d = din("ident", [128, 128])
    s1s_d = din("s1s", [16, 128, 256]); s1p_d = din("s1p", [2, 128, 256])
    kmat_d = din("kmat", [6, 128, 128]); chtab_d = din("chtab", [2, 256, 512])
    ys = dout("ys", [TS, D]); yp = dout("yp", [2 * TP, D]); stp = dout("stp", [128, 64])
    rs_in = [dint(f"rs_in{g}", [512, TS], BF16) for g in range(8)]
    rs_out = [dint(f"rs_out{g}", [256, TS], BF16) for g in range(8)]
    halo_in = dint("halo_in", [128, 32], F32); halo_out = dint("halo_out", [256, 32], F32)
    car_in = [dint(f"car_in{h}", [128, 2], F32) for h in range(8)]
    car_out = [dint(f"car_out{h}", [256, 2], F32) for h in range(8)]
    RG = [[0, 1], [2, 3], [4, 5], [6, 7]]

    psb = [c.stack.enter_context(nc.psum_tensor(f"ps{i}", [128, 512], F32)) for i in range(6)]
    ptb = [c.stack.enter_context(nc.psum_tensor(f"pt{i}", [128, 1024], BF16)) for i in range(2)]

    def bank():
        b = c.bankrr
        c.bankrr = (b + 1) % 6
        return b, psb[b], ("ps", b)

    def tbank():
        b = c.ptrr
        c.ptrr = (b + 1) % 2
        return ptb[b], ("pt", b)

    evac_rr = [0]

    def evac_copy(out_ap, in_ap, reads, writes, flip=True):
        if flip:
            evac_rr[0] ^= 1
        if evac_rr[0]:
            c.op("act", lambda: act.copy(out=out_ap, in_=in_ap), reads, writes)
        else:
            c.op("dve", lambda: dve.tensor_copy(out=out_ap, in_=in_ap), reads, writes)

    x1 = c.alloc(16 * 1024)
    hT = c.alloc(8 * 1024)
    smalls = c.alloc(1280)
    so = [0]

    def S(n):
        r = smalls.sub(so[0], n)
        so[0] += n
        return r
    identb = S(64)
    ss = S(32); rstd = S(32)
    scond = S(16); selr = S(2); st0 = S(16); zcol = S(1)
    bfm = S(16); convw = S(80); convb = S(16)
    brc = S(32); bic = S(32); cl = S(32); cl2 = S(32)
    halo = S(32); hg = S(64); c1 = S(2); c2 = S(2); cg = S(4); ctmp = S(32)
    stpo = S(64)
    diagb = [S(64) for _ in range(5)]
    hTv = hT.bf().rearrange("p (k t) -> p k t", k=8)
    x1v = x1.f32().rearrange("p (t d) -> p t d", t=16)

    def x1t(t):
        return x1.sub(t * 1024, 1024)

    def hTk(kc, c0, n):
        return hT.sub(kc * 1024 + c0 // 2, (n + 1) // 2)

    def hT_cols(c0, n):
        return [hTk(kc, c0, n) for kc in range(8)]

    def ld(r, src, sem="small"):
        c.dma("sp", sem, lambda: sp.dma_start(out=r.f32(), in_=src), writes=[r])
    ld(scond, condT); ld(selr, sel_d); ld(st0, st0_d)
    ld(bfm, bfm_d); ld(convw, convw_d); ld(convb, convb_d)
    ld(brc, br_d); ld(bic, bi_d); ld(cl, lam_d)
    c.dma("pool", "identb", lambda: pool.dma_start(out=identb.bf(), in_=ident_d), writes=[identb])
    c.op("dve", lambda: dve.memset(zcol.f32(), 0.0), writes=[zcol])
    c.op("act", lambda: act.activation(out=scond.f32(), in_=scond.f32(), func=AF.Silu), reads=[scond], writes=[scond])
    c.op("act", lambda: act.activation(out=cl.f32(), in_=cl.f32(), func=AF.Exp, scale=-1.0), reads=[cl], writes=[cl])
    c.op("act", lambda: act.activation(out=cl.f32(), in_=cl.f32(), func=AF.Ln, bias=1.0), reads=[cl], writes=[cl])
    c.op("dve", lambda: dve.tensor_scalar(out=cl2.f32(), in0=cl.f32(), scalar1=-4.0, scalar2=None, op0=ALU.mult), reads=[cl], writes=[cl2])
    c.op("dve", lambda: dve.tensor_scalar(out=cl.f32(), in0=cl.f32(), scalar1=-8.0, scalar2=None, op0=ALU.mult), reads=[cl], writes=[cl])
    c.op("dve", lambda: dve.tensor_scalar(out=brc.f32(), in0=brc.f32(), scalar1=0.5, scalar2=None, op0=ALU.mult), reads=[brc], writes=[brc])
    c.op("dve", lambda: dve.tensor_scalar(out=bic.f32(), in0=bic.f32(), scalar1=0.5, scalar2=None, op0=ALU.mult), reads=[bic], writes=[bic])

    ada_dram = [dint(f"ada_rows{l}", [2, 3072], F32) for l in range(2)]

    def ada_layer(l):
        mk = c.mark()
        NS = 3
        wst = [c.alloc(3072) for _ in range(NS)]
        arow = c.alloc(3072); brow = c.alloc(3072); grow = c.alloc(1024)
        c.dma("sp", "brow", lambda: sp.dma_start(out=brow.f32()[0:2, :], in_=badar_d[l][0, :].partition_broadcast(2)), writes=[brow])
        c.dma("sp", "grow", lambda: sp.dma_start(out=grow.f32()[0:2, :], in_=ngr_d[l][0, :].partition_broadcast(2)), writes=[grow])
        for kc in range(8):
            r = wst[kc % NS]
            c.dma("sp", "wada", lambda kc=kc, r=r: sp.dma_start(out=r.f32(), in_=w_ada[l][kc * 128:(kc + 1) * 128, :]), writes=[r])
            for n in range(6):
                c.op("pe", lambda kc=kc, n=n, r=r: pe.matmul(psb[n][0:2, :], lhsT=scond.f32()[:, kc * 2:kc * 2 + 2], rhs=r.f32()[:, n * 512:(n + 1) * 512], start=(kc == 0), stop=(kc == 7)),
                     reads=[r, scond], writes=[("ps", n)])
        for n in range(6):
            c.op("dve", lambda n=n: dve.tensor_tensor(out=arow.f32()[0:2, n * 512:(n + 1) * 512], in0=psb[n][0:2, :], in1=brow.f32()[0:2, n * 512:(n + 1) * 512], op=ALU.add),
                 reads=[("ps", n), brow], writes=[arow.sub(n * 512, 512)])
        c.op("dve", lambda: dve.scalar_tensor_tensor(out=arow.f32()[0:2, 1024:2048], in0=arow.f32()[0:2, 1024:2048], scalar=1.0, in1=grow.f32()[0:2, :], op0=ALU.add, op1=ALU.mult),
             reads=[arow, grow], writes=[arow.sub(1024, 1024)])
        c.dma("sp", "adaout", lambda: sp.dma_start(out=ada_dram[l], in_=arow.f32()[0:2, :]), reads=[arow], writes=[("ada", l)])
        c.release(mk)

    WHICH = {"shift": 0, "m": 1, "gate": 2}

    def bcast_ada(l, j, which, dst):
        w = WHICH[which]
        c.dma("sp", "bc", lambda: sp.dma_start(out=dst.f32(), in_=ada_dram[l][j, w * 1024:(w + 1) * 1024].partition_broadcast(128)), reads=[("ada", l)], writes=[dst])

    ada_layer(0)

    def front_end(ntiles, col0, l, j, xsrc=None, possrc=None, xdst=None):
        mk = c.mark()
        m_bc = c.alloc(1024); sh_bc = c.alloc(1024)
        NSL = 3
        tmp = [c.alloc(1024) for _ in range(NSL)]
        hb = [c.alloc(512) for _ in range(NSL)]
        xt = [c.alloc(1024) for _ in range(NSL)] if xdst is None else None
        posb = [c.alloc(1024) for _ in range(NSL)] if possrc is not None else None
        bcast_ada(l, j, "m", m_bc)
        bcast_ada(l, j, "shift", sh_bc)
        for t in range(ntiles):
            s = t % NSL
            xr = xdst(t) if xdst is not None else xt[s]
            if xsrc is not None:
                c.dma("sp", f"x{s}", lambda xr=xr, t=t: sp.dma_start(out=xr.f32(), in_=xsrc[t * 128:(t + 1) * 128, :]), writes=[xr])
                if possrc is not None:
                    c.dma("sp", f"p{s}", lambda s=s, t=t: sp.dma_start(out=posb[s].f32(), in_=possrc[t * 128:(t + 1) * 128, :]), writes=[posb[s]])
                    c.op("dve", lambda xr=xr, s=s: dve.tensor_tensor(out=xr.f32(), in0=xr.f32(), in1=posb[s].f32(), op=ALU.add), reads=[xr, posb[s]], writes=[xr])
            ssr = ss.sub(t, 1); rsr = rstd.sub(t, 1)
            c.op("act", lambda xr=xr, s=s, ssr=ssr: act.activation(out=tmp[s].f32(), in_=xr.f32(), func=AF.Square, accum_out=ssr.f32()),
                 reads=[xr], writes=[tmp[s], ssr])
            c.op("act", lambda ssr=ssr, rsr=rsr: act.activation(out=rsr.f32(), in_=ssr.f32(), func=AF.Sqrt, scale=1.0 / D, bias=EPS),
                 reads=[ssr], writes=[rsr])
            c.op("dve", lambda rsr=rsr: dve.reciprocal(out=rsr.f32(), in_=rsr.f32()), reads=[rsr], writes=[rsr])
            c.op("dve", lambda xr=xr, s=s, rsr=rsr: dve.scalar_tensor_tensor(out=tmp[s].f32(), in0=xr.f32(), scalar=rsr.f32(), in1=m_bc.f32(), op0=ALU.mult, op1=ALU.mult),
                 reads=[xr, rsr, m_bc], writes=[tmp[s]])
            c.op("dve", lambda s=s: dve.tensor_tensor(out=hb[s].bf(), in0=tmp[s].f32(), in1=sh_bc.f32(), op=ALU.add),
                 reads=[tmp[s], sh_bc], writes=[hb[s]])
            pt, ptk = tbank()
            for kc in range(8):
                c.op("pe", lambda kc=kc, s=s, pt=pt: pe.transpose(pt[:, kc * 128:(kc + 1) * 128], hb[s].bf()[:, kc * 128:(kc + 1) * 128], identb.bf()),
                     reads=[hb[s], identb], writes=[ptk])
            cc0 = col0 + t * 128
            c.op("act", lambda pt=pt, cc0=cc0: act.copy(out=hTv[:, :, cc0:cc0 + 128], in_=pt[:, :].rearrange("p (k t) -> p k t", k=8)),
                 reads=[ptk], writes=hT_cols(cc0, 128))
        c.release(mk)

    def load_w_in(l, colstart, dst, sem):
        src = w_in[l].rearrange("(k p) n -> p k n", p=128)[:, :, colstart:colstart + 256]
        c.dma("pool", sem, lambda: pool.dma_start(out=dst.bf().rearrange("p (k n) -> p k n", k=8), in_=src), writes=[dst])

    def pieces_of(seqs):
        out = []
        for (c0, T) in seqs:
            for o in range(0, T, 512):
                out.append((c0 + o, min(512, T - o)))
        return out

    def l0_pre_seq(g, wu, col0, T, N2, s1r, slots, dest, u_tm, Yt, Qt):
        NI = 128 // N2
        wuv = wu.bf().rearrange("p (k n) -> p k n", k=8)
        utv = u_tm.bf().rearrange("p (m n) -> p m n", n=256)
        s1v = s1r.bf().rearrange("p (m n) -> p m n", n=256)
        Ytv = Yt.bf().rearrange("p (c q j m i) -> p c q j m i", c=2, q=2, j=N2, m=N2)
        for m2 in range(N2):
            b, ps, pk = bank()
            for kc in range(8):
                c.op("pe", lambda kc=kc, m2=m2: pe.matmul(ps[:, 0:256], lhsT=hTv[:, kc, col0 + m2:col0 + T:N2], rhs=wuv[:, kc, :], start=(kc == 0), stop=(kc == 7)),
                     reads=[hTk(kc, col0, T), wu], writes=[pk])
            ur = u_tm.sub(m2 * 128, 128)
            evac_copy(utv[:, m2, :], ps[:, 0:256], [pk], [ur])
        chk("pre_u")
        for m2 in range(N2):
            ur = u_tm.sub(m2 * 128, 128)
            for cc in range(2):
                b, ps, pk = bank()
                c.op("pe", lambda m2=m2, cc=cc: pe.matmul(ps[:, 0:256], lhsT=utv[:, m2, cc * 128:(cc + 1) * 128], rhs=s1v[:, m2, :], start=True, stop=True),
                     reads=[ur, s1r.sub(m2 * 128, 128)], writes=[pk])
                evac_copy(Ytv[:, cc, :, :, m2, :], ps[:, 0:256].rearrange("p (q j i) -> p q j i", q=2, j=N2), [pk], [Yt])
        chk("pre_s1")
        chv = chtab.bf().rearrange("p (q c n) -> p q c n", q=2, c=2)
        def s2_stage(j, q):
            b2, ps2, pk2 = bank()
            for si, (Kr, Ki, flip) in enumerate(slots):
                for lc in range(2):
                    o = ps2[:, (si * 2 + lc) * 128:(si * 2 + lc + 1) * 128]
                    c.op("pe", lambda o=o, lc=lc, Kr=Kr: pe.matmul(o, lhsT=q.bf()[:, lc * 128:(lc + 1) * 128], rhs=Kr.bf(), start=True, stop=False),
                         reads=[q, Kr], writes=[pk2])
                    c.op("pe", lambda o=o, lc=lc, Ki=Ki: pe.matmul(o, lhsT=q.bf()[:, 256 + lc * 128:256 + (lc + 1) * 128], rhs=Ki.bf(), start=False, stop=True),
                         reads=[q, Ki], writes=[pk2])
            for si, (Kr, Ki, flip) in enumerate(slots):
                jb = (N2 - 1 - j) if flip else j
                for lc in range(2):
                    o = ps2[:, (si * 2 + lc) * 128:(si * 2 + lc + 1) * 128].rearrange("p (k i) -> p k i", i=NI)
                    dap, dreg = dest(si, lc)
                    d = dap.rearrange("p (k c) -> p k c", c=128)[:, :, jb * NI:(jb + 1) * NI]
                    evac_copy(d, o, [pk2], [dreg], flip=(si == 0 and lc == 0))

        prev = None
        for j in range(N2):
            b, ps, pk = bank()
            n = 0
            for cc in range(2):
                for comp in range(2):
                    c.op("pe", lambda cc=cc, comp=comp, n=n: pe.matmul(ps[:, :], lhsT=Ytv[:, cc, comp, j, :, :].rearrange("p m i -> p (m i)"), rhs=chv[:, comp, cc, :], start=(n == 0), stop=(n == 3)),
                         reads=[Yt, chtab], writes=[pk])
                    n += 1
            q = Qt[j % len(Qt)]
            evac_copy(q.bf(), ps[:, :], [pk], [q])
            if prev is not None:
                s2_stage(*prev)
            prev = (j, q)
        s2_stage(*prev)

    def wout_load(l, hd, gate_bc, wstg, wob):
        src = w_out[l][hd * 256:(hd + 1) * 256, :].rearrange("(l p) d -> p l d", p=128)
        if wstg is not None:
            c.dma("sp", "wo", lambda: sp.dma_start(out=wstg.f32().rearrange("p (l d) -> p l d", l=2), in_=src), writes=[wstg])
            for lc in range(2):
                c.op("pool", lambda lc=lc: pool.tensor_tensor(out=wob.bf()[:, lc * 1024:(lc + 1) * 1024], in0=wstg.f32()[:, lc * 1024:(lc + 1) * 1024], in1=gate_bc.f32(), op=ALU.mult),
                     reads=[wstg.sub(lc * 1024, 1024), gate_bc], writes=[wob.sub(lc * 512, 512)])
        else:
            c.dma("pool", "wo", lambda: pool.dma_start(out=wob.bf().rearrange("p (l d) -> p l d", l=2), in_=src), writes=[wob])
            for lc in range(2):
                c.op("pool", lambda lc=lc: pool.tensor_tensor(out=wob.bf()[:, lc * 1024:(lc + 1) * 1024], in0=wob.bf()[:, lc * 1024:(lc + 1) * 1024], in1=gate_bc.f32(), op=ALU.mult),
                     reads=[wob.sub(lc * 512, 512), gate_bc], writes=[wob.sub(lc * 512, 512)])

    def wout_update(l, hd, gT, gate_bc, wstg, wob, seqs_tiles, load=True):
        if load:
            wout_load(l, hd, gate_bc, wstg, wob)
        gv = gT.bf().rearrange("p (l t) -> p l t", l=2)
        Tt = gT.n
        for (tile, col) in seqs_tiles:
            for half in range(2):
                b, ps, pk = bank()
                for lc in range(2):
                    c.op("pe", lambda lc=lc, col=col, half=half: pe.matmul(ps[:, :], lhsT=gv[:, lc, col:col + 128], rhs=wob.bf()[:, lc * 1024 + half * 512:lc * 1024 + (half + 1) * 512], start=(lc == 0), stop=(lc == 1)),
                         reads=[gT.sub(lc * (Tt // 2) + col // 2, 64), wob.sub(lc * 512 + half * 256, 256)], writes=[pk])
                xr = x1t(tile).sub(half * 512, 512)
                c.op("dve", lambda xr=xr: dve.tensor_tensor(out=xr.f32(), in0=ps[:, :], in1=xr.f32(), op=ALU.add), reads=[pk, xr], writes=[xr])

    def l0_post_group(g, Ttot, seqs, fT, wz, sz, wf, gT, gate_bc, wstg, wob, tile0):
        wzv = wz.bf().rearrange("p (k n) -> p k n", k=8)
        szv = sz.bf().rearrange("p (l t) -> p l t", l=2)
        fv = fT.bf().rearrange("p (l t) -> p l t", l=2)
        gv = gT.bf().rearrange("p (l t) -> p l t", l=2)
        wfv = wf.bf().rearrange("p (l n) -> p l n", l=2)
        pcs = pieces_of(seqs)
        for jc in range(2):
            for (c0, n) in pcs:
                b, ps, pk = bank()
                for kc in range(8):
                    c.op("pe", lambda kc=kc, jc=jc, c0=c0, n=n: pe.matmul(ps[:, 0:n], lhsT=wzv[:, kc, jc * 128:(jc + 1) * 128], rhs=hTv[:, kc, c0:c0 + n], start=(kc == 0), stop=(kc == 7)),
                         reads=[wz, hTk(kc, c0, n)], writes=[pk])
                r = sz.sub(jc * (Ttot // 2) + c0 // 2, n // 2)
                c.op("act", lambda jc=jc, c0=c0, n=n: act.activation(out=szv[:, jc, c0:c0 + n], in_=ps[:, 0:n], func=AF.Silu), reads=[pk], writes=[r])
        for jc in range(2):
            for (c0, n) in pcs:
                b, ps, pk = bank()
                for lc in range(2):
                    c.op("pe", lambda lc=lc, jc=jc, c0=c0, n=n: pe.matmul(ps[:, 0:n], lhsT=wfv[:, lc, jc * 128:(jc + 1) * 128], rhs=fv[:, lc, c0:c0 + n], start=(lc == 0), stop=(lc == 1)),
                         reads=[wf, fT.sub(lc * (Ttot // 2) + c0 // 2, n // 2)], writes=[pk])
                rs_ = sz.sub(jc * (Ttot // 2) + c0 // 2, n // 2)
                rg = gT.sub(jc * (Ttot // 2) + c0 // 2, n // 2)
                bcol = bfm.f32()[:, g * 2 + jc:g * 2 + jc + 1]
                c.op("dve", lambda jc=jc, c0=c0, n=n, bcol=bcol: dve.scalar_tensor_tensor(out=gv[:, jc, c0:c0 + n], in0=ps[:, 0:n], scalar=bcol, in1=szv[:, jc, c0:c0 + n], op0=ALU.add, op1=ALU.mult),
                     reads=[pk, bfm, rs_], writes=[rg])
        tiles = [(tile0 + i, i * 128) for i in range(Ttot // 128)]
        wout_update(0, g, gT, gate_bc, wstg, wob, tiles)

    def load_tables():
        t = {}
        t["s1s"] = c.alloc(16 * 128); t["s1p"] = c.alloc(2 * 128)
        t["k"] = [c.alloc(64) for _ in range(6)]
        t["ch"] = c.alloc(2 * 2 * 256)
        c.dma("pool", "tab", lambda: pool.dma_start(out=t["s1s"].bf().rearrange("p (m n) -> p m n", m=16), in_=s1s_d.rearrange("m p n -> p m n")), writes=[t["s1s"]])
        c.dma("pool", "tab", lambda: pool.dma_start(out=t["s1p"].bf().rearrange("p (m n) -> p m n", m=2), in_=s1p_d.rearrange("m p n -> p m n")), writes=[t["s1p"]])
        for i in range(6):
            c.dma("pool", "tab", lambda i=i: pool.dma_start(out=t["k"][i].bf(), in_=kmat_d[i]), writes=[t["k"][i]])
        c.dma("pool", "tab", lambda: pool.dma_start(out=t["ch"].bf().rearrange("p (q c n) -> p q c n", q=2, c=2), in_=chtab_d.rearrange("q (c p) n -> p q c n", p=128)), writes=[t["ch"]])
        return t

    chk("setup")
    mkL0 = c.mark()
    tabs = load_tables()
    chtab = tabs["ch"]
    front_end(16, 0, 0, 0, xsrc=xs_in, possrc=pos_in, xdst=None)
    ada_layer(1)
    dump("hT_in", lambda: hT.bf(), [128, 8 * 2048], [hT], BF16)
    chk("fe_in")
    mk = c.mark()
    wus = [c.alloc(1024), c.alloc(1024)]
    u_tm = c.alloc(16 * 128); Yt = c.alloc(4096); Qt = [c.alloc(256) for _ in range(3)]
    send = [c.alloc(4096), c.alloc(4096)]
    slotsS = [(tabs["k"][0], tabs["k"][1], False), (tabs["k"][2], tabs["k"][3], True)]
    load_w_in(0, 0, wus[0], "wu0")
    for g in range(8):
        if g + 1 < 8:
            load_w_in(0, (g + 1) * 256, wus[(g + 1) % 2], f"wu{(g + 1) % 2}")
        sd = send[g % 2]
        sdv = sd.bf().rearrange("p (s l t) -> p s l t", s=2, l=2)

        def dest(si, lc, sd=sd, sdv=sdv):
            return sdv[:, si, lc, :], sd.sub((si * 2 + lc) * 1024, 1024)
        l0_pre_seq(g, wus[g % 2], 0, TS, 16, tabs["s1s"], slotsS, dest, u_tm, Yt, Qt)
        chk("pre_s2")
        c.dma("sp", f"send{g % 2}", lambda g=g, sdv=sdv: sp.dma_start(out=rs_in[g].rearrange("(s l p) t -> p s l t", s=2, l=2), in_=sdv),
              reads=[sd], writes=[("rs_in", g)])
        chk("pre_send")
        c.cc(f"rs{g}", lambda g=g: pool.collective_compute("ReduceScatter", ALU.add, replica_groups=RG, ins=[rs_in[g]], outs=[rs_out[g]]),
             reads=[("rs_in", g)], writes=[("rs_out", g)])
        chk("pre_rs")
    c.release(mk)
    chk("l0pre")
    front_end(16, 0, 0, 0, xsrc=xs_res, possrc=pos_res, xdst=x1t)
    chk("fe_res")
    mk = c.mark()
    gate_bc = c.alloc(1024); wstg = c.alloc(2048); wob = c.alloc(1024)
    wzs = [c.alloc(1024), c.alloc(1024)]
    sz = c.alloc(2048); fTs = [c.alloc(2048), c.alloc(2048)]; wf = [c.alloc(256), c.alloc(256)]; gT = c.alloc(2048)
    bcast_ada(0, 0, "gate", gate_bc)

    def load_post(g):
        load_w_in(0, DI + g * 256, wzs[g % 2], f"wz{g % 2}")
        c.dma("pool", f"wf{g % 2}", lambda: pool.dma_start(out=wf[g % 2].bf().rearrange("p (l n) -> p l n", l=2), in_=w_fmix[g].rearrange("(l p) n -> p l n", p=128)), writes=[wf[g % 2]])
        c.dma("sp", f"ft{g % 2}", lambda: sp.dma_start(out=fTs[g % 2].bf().rearrange("p (l t) -> p l t", l=2), in_=rs_out[g].rearrange("(l p) t -> p l t", p=128)),
              reads=[("rs_out", g)], writes=[fTs[g % 2]])
    load_post(0)
    for g in range(8):
        if g + 1 < 8:
            load_post(g + 1)
        l0_post_group(g, TS, [(0, TS)], fTs[g % 2], wzs[g % 2], sz, wf[g % 2], gT, gate_bc, wstg, wob, 0)
    c.release(mk)
    c.release(mkL0)
    dump("x1_l0", lambda: x1.f32(), [128, 16 * 1024], [x1])
    chk("l0post")

    def layer1(Ttot, seqs, j, is_sample, tile0):
        ntiles = Ttot // 128
        front_end(ntiles, 0, 1, j, xdst=lambda t: x1t(tile0 + t))
        mk = c.mark()
        RL = (sum(T + 4 for (_, T) in seqs) + BLK - 1) // BLK * BLK
        Rall = c.alloc(5 * RL)
        row = [Rall.sub(i * RL, RL) for i in range(5)]
        U = [row[0], row[1]]
        TA = [row[0].sub(0, Ttot), row[2].sub(0, Ttot)]
        TI = [row[1].sub(0, Ttot), row[3].sub(0, Ttot)]
        ts = row[4].sub(0, Ttot)
        h1 = c.alloc(2 * Ttot)
        gate_bc = c.alloc(1024); wob = c.alloc(1024)
        wu = c.alloc(1024); wz = c.alloc(1024)
        gw = [[c.alloc(256) for _ in range(4)] for _ in range(2)]
        ucbf = c.alloc(Ttot); sz = c.alloc(Ttot); gT = c.alloc(Ttot)
        bcast_ada(1, j, "gate", gate_bc)
        pcs = pieces_of(seqs)
        pads = []
        po = 0
        for (c0, T) in seqs:
            pads.append(po)
            po += T + 4

        if is_sample:
            wuh = [wu, wz]
            b, ps, pk = bank()
            for hd in range(8):
                load_w_in(1, hd * 256, wuh[hd % 2], f"wu{hd % 2}")
                wv = wuh[hd % 2].bf().rearrange("p (k n) -> p k n", k=8)
                for jc in range(2):
                    ch = hd * 2 + jc
                    for kc in range(8):
                        c.op("pe", lambda kc=kc, jc=jc, ch=ch, wv=wv: pe.matmul(ps[:, ch * 2:ch * 2 + 2], lhsT=wv[:, kc, jc * 128:(jc + 1) * 128], rhs=hTv[:, kc, Ttot - 2:Ttot], start=(kc == 0), stop=(kc == 7)),
                             reads=[wuh[hd % 2], hTk(kc, Ttot - 2, 2)], writes=[pk])
            c.op("dve", lambda: dve.tensor_copy(out=halo.f32(), in_=ps[:, 0:32]), reads=[pk], writes=[halo])
            c.dma("sp", "halo", lambda: sp.dma_start(out=halo_in, in_=halo.f32()), reads=[halo], writes=[("halo_in",)])
            c.cc("halocc", lambda: pool.collective_compute("AllGather", ALU.bypass, replica_groups=RG, ins=[halo_in], outs=[halo_out]),
                 reads=[("halo_in",)], writes=[("halo_out",)])
            c.dma("sp", "halo", lambda: sp.dma_start(out=hg.f32().rearrange("p (r n) -> p r n", r=2), in_=halo_out.rearrange("(r p) n -> p r n", p=128)),
                  reads=[("halo_out",)], writes=[hg])
            c.op("dve", lambda: dve.tensor_scalar(out=ctmp.f32(), in0=hg.f32()[:, 0:32], scalar1=selr.f32()[:, 0:1], scalar2=None, op0=ALU.mult), reads=[hg, selr], writes=[ctmp])
            c.op("dve", lambda: dve.scalar_tensor_tensor(out=halo.f32(), in0=hg.f32()[:, 32:64], scalar=selr.f32()[:, 1:2], in1=ctmp.f32(), op0=ALU.mult, op1=ALU.add),
                 reads=[hg, selr, ctmp], writes=[halo])

        wuv = wu.bf().rearrange("p (k n) -> p k n", k=8)
        wzv = wz.bf().rearrange("p (k n) -> p k n", k=8)
        ucv = ucbf.bf().rearrange("p (l t) -> p l t", l=2)
        szv = sz.bf().rearrange("p (l t) -> p l t", l=2)
        gv = gT.bf().rearrange("p (l t) -> p l t", l=2)
        h1v = h1.f32().rearrange("p (l t) -> p l t", l=2)

        def loads(hd):
            load_w_in(1, hd * 256, wu, "wu1h")
            load_w_in(1, DI + hd * 256, wz, "wz1h")
            gws = gw[hd % 2]
            for d in range(2):
                for gi, wsrc in enumerate((wr_d, wi_d)):
                    r = gws[d * 2 + gi]
                    c.dma("pool", f"gw{hd % 2}", lambda r=r, wsrc=wsrc, d=d: pool.dma_start(out=r.bf().rearrange("p (i n) -> p i n", i=2), in_=wsrc[d, hd].rearrange("(i p) n -> p i n", p=128)), writes=[r])

        def part_uz(hd):
            for jc in range(2):
                for si, (c0, T) in enumerate(seqs):
                    for o in range(0, T, 512):
                        n = min(512, T - o)
                        b, ps, pk = bank()
                        for kc in range(8):
                            c.op("pe", lambda kc=kc, jc=jc, c0=c0, o=o, n=n: pe.matmul(ps[:, 0:n], lhsT=wuv[:, kc, jc * 128:(jc + 1) * 128], rhs=hTv[:, kc, c0 + o:c0 + o + n], start=(kc == 0), stop=(kc == 7)),
                                 reads=[wu, hTk(kc, c0 + o, n)], writes=[pk])
                        dcol = pads[si] + 2 + o
                        evac_copy(U[jc].bf()[:, dcol:dcol + n], ps[:, 0:n], [pk], [U[jc].sub(dcol // 2, (n + 1) // 2 + 1)])
            for jc in range(2):
                for (c0, n) in pcs:
                    b, ps, pk = bank()
                    for kc in range(8):
                        c.op("pe", lambda kc=kc, jc=jc, c0=c0, n=n: pe.matmul(ps[:, 0:n], lhsT=wzv[:, kc, jc * 128:(jc + 1) * 128], rhs=hTv[:, kc, c0:c0 + n], start=(kc == 0), stop=(kc == 7)),
                             reads=[wz, hTk(kc, c0, n)], writes=[pk])
                    c.op("act", lambda jc=jc, c0=c0, n=n: act.activation(out=szv[:, jc, c0:c0 + n], in_=ps[:, 0:n], func=AF.Silu), reads=[pk], writes=[sz.sub(jc * (Ttot // 2) + c0 // 2, n // 2)])

        def part_conv(hd):
            for jc in range(2):
                ch = hd * 2 + jc
                Ub = U[jc].bf()
                for si, (c0, T) in enumerate(seqs):
                    p0 = pads[si]
                    c.op("dve", lambda Ub=Ub, p0=p0: dve.memset(Ub[:, p0:p0 + 2], 0.0), writes=[U[jc].sub(p0 // 2, 2)])
                    if not is_sample:
                        c.op("dve", lambda Ub=Ub, p0=p0, T=T: dve.memset(Ub[:, p0 + T + 2:p0 + T + 4], 0.0), writes=[U[jc].sub((p0 + T + 2) // 2, 2)])
                if is_sample:
                    T = seqs[0][1]
                    for e in range(2):
                        c.op("dve", lambda Ub=Ub, ch=ch, e=e, T=T: dve.tensor_copy(out=Ub[:, T + 2 + e:T + 3 + e], in_=halo.f32()[:, ch * 2 + 1 - e:ch * 2 + 2 - e]),
                             reads=[halo], writes=[U[jc].sub((T + 2 + e) // 2, 1)])
                plist = []
                for si, (c0, T) in enumerate(seqs):
                    for o in range(0, T, 512):
                        plist.append((pads[si] + o, c0 + o, min(512, T - o)))
                banks = [bank() for _ in plist]
                for dd in range(5):
                    dg = diagb[dd]
                    c.op("dve", lambda dg=dg, ch=ch, dd=dd: dve.tensor_scalar(out=dg.bf(), in0=identb.bf(), scalar1=convw.f32()[:, ch * 5 + dd:ch * 5 + dd + 1], scalar2=None, op0=ALU.mult),
                         reads=[identb, convw], writes=[dg])
                    for (pcol, c0o, n), (b, ps, pk) in zip(plist, banks):
                        c.op("pe", lambda dg=dg, Ub=Ub, pcol=pcol, n=n, dd=dd, ps=ps: pe.matmul(ps[:, 0:n], lhsT=dg.bf(), rhs=Ub[:, pcol + dd:pcol + dd + n], start=(dd == 0), stop=(dd == 4)),
                             reads=[dg, U[jc].sub((pcol + dd) // 2, (n + 1) // 2 + 1)], writes=[pk])
                for (pcol, c0o, n), (b, ps, pk) in zip(plist, banks):
                    c.op("act", lambda ps=ps, c0o=c0o, n=n, ch=ch, jc=jc: act.activation(out=ucv[:, jc, c0o:c0o + n], in_=ps[:, 0:n], func=AF.Identity, bias=convb.f32()[:, ch:ch + 1]),
                         reads=[pk, convb], writes=[ucbf.sub(jc * (Ttot // 2) + c0o // 2, n // 2)])

        def stage(hd, d, jc):
            gws = gw[hd % 2]
            ta, ti = TA[jc], TI[jc]
            ch = hd * 2 + jc
            for gi, (dst, bcol) in enumerate(((ta, brc), (ti, bic))):
                wv = gws[d * 2 + gi].bf().rearrange("p (i n) -> p i n", i=2)
                for (c0, n) in pcs:
                    b, ps, pk = bank()
                    for ic in range(2):
                        c.op("pe", lambda ic=ic, c0=c0, n=n, wv=wv: pe.matmul(ps[:, 0:n], lhsT=wv[:, ic, jc * 128:(jc + 1) * 128], rhs=ucv[:, ic, c0:c0 + n], start=(ic == 0), stop=(ic == 1)),
                             reads=[gws[d * 2 + gi], ucbf.sub(ic * (Ttot // 2) + c0 // 2, n // 2)], writes=[pk])
                    c.op("act", lambda c0=c0, n=n, dst=dst, bcol=bcol: act.activation(out=dst.f32()[:, c0:c0 + n], in_=ps[:, 0:n], func=AF.Tanh, scale=0.5, bias=bcol.f32()[:, d * 16 + ch:d * 16 + ch + 1]),
                         reads=[pk, bcol], writes=[dst.sub(c0, n)])
            colc = slice(d * 16 + ch, d * 16 + ch + 1)
            c.op("act", lambda: act.activation(out=ts.f32(), in_=ta.f32(), func=AF.Exp, scale=cl.f32()[:, colc], bias=cl.f32()[:, colc]), reads=[ta, cl], writes=[ts])
            c.op("act", lambda: act.activation(out=ta.f32(), in_=ta.f32(), func=AF.Exp, scale=cl2.f32()[:, colc], bias=cl2.f32()[:, colc]), reads=[ta, cl2], writes=[ta])
            c.op("act", lambda: act.activation(out=ts.f32(), in_=ts.f32(), func=AF.Sqrt, scale=-1.0, bias=1.0), reads=[ts], writes=[ts])
            c.op("dve", lambda: dve.scalar_tensor_tensor(out=ti.f32(), in0=ti.f32(), scalar=1.0, in1=ts.f32(), op0=ALU.add, op1=ALU.mult), reads=[ti, ts], writes=[ti])
            c.op("dve", lambda: dve.scalar_tensor_tensor(out=ti.f32(), in0=ti.f32(), scalar=0.5, in1=ucv[:, jc, :], op0=ALU.mult, op1=ALU.mult), reads=[ti, ucbf.sub(jc * (Ttot // 2), Ttot // 2)], writes=[ti])
            if d == 0:
                for si, (c0, T) in enumerate(seqs):
                    init = st0.f32()[:, ch:ch + 1] if is_sample else zcol.f32()
                    c.op("dve", lambda c0=c0, T=T, init=init: dve.tensor_tensor_scan(out=h1v[:, jc, c0:c0 + T], data0=ta.f32()[:, c0:c0 + T], data1=ti.f32()[:, c0:c0 + T], initial=init, op0=ALU.mult, op1=ALU.add),
                         reads=[ta, ti, st0, zcol], writes=[h1.sub(jc * Ttot + c0, T)])
                    if is_sample:
                        c.op("dve", lambda c0=c0, T=T: dve.tensor_copy(out=c1.f32()[:, jc:jc + 1], in_=h1v[:, jc, c0 + T - 1:c0 + T]), reads=[h1.sub(jc * Ttot + c0 + T - 1, 1)], writes=[c1])
                    else:
                        col = si * 32 + ch
                        c.op("dve", lambda c0=c0, T=T, col=col: dve.tensor_copy(out=stpo.f32()[:, col:col + 1], in_=h1v[:, jc, c0 + T - 1:c0 + T]), reads=[h1.sub(jc * Ttot + c0 + T - 1, 1)], writes=[stpo.sub(col, 1)])
            else:
                if is_sample and jc == 0:
                    c.op("dve", lambda: dve.tensor_scalar(out=ctmp.f32()[:, 0:2], in0=cg.f32()[:, 0:2], scalar1=selr.f32()[:, 0:1], scalar2=None, op0=ALU.mult), reads=[cg, selr], writes=[ctmp])
                    c.op("dve", lambda: dve.scalar_tensor_tensor(out=c2.f32(), in0=cg.f32()[:, 2:4], scalar=selr.f32()[:, 1:2], in1=ctmp.f32()[:, 0:2], op0=ALU.mult, op1=ALU.add),
                         reads=[cg, selr, ctmp], writes=[c2])
                for si, (c0, T) in enumerate(seqs):
                    init = c2.f32()[:, jc:jc + 1] if is_sample else zcol.f32()
                    c.op("dve", lambda c0=c0, T=T, init=init: dve.tensor_tensor_scan(out=ti.f32()[:, c0:c0 + T][:, ::-1], data0=ta.f32()[:, c0:c0 + T][:, ::-1], data1=ti.f32()[:, c0:c0 + T][:, ::-1], initial=init, op0=ALU.mult, op1=ALU.add),
                         reads=[ta, ti, c2, zcol], writes=[ti])
                    if not is_sample:
                        col = si * 32 + 16 + ch
                        c.op("dve", lambda c0=c0, col=col: dve.tensor_copy(out=stpo.f32()[:, col:col + 1], in_=ti.f32()[:, c0:c0 + 1]), reads=[ti], writes=[stpo.sub(col, 1)])
                c.op("dve", lambda: dve.tensor_tensor(out=ti.f32(), in0=ti.f32(), in1=h1v[:, jc, :], op=ALU.add), reads=[ti, h1.sub(jc * Ttot, Ttot)], writes=[ti])
                c.op("dve", lambda: dve.tensor_tensor(out=gv[:, jc, :], in0=ti.f32(), in1=szv[:, jc, :], op=ALU.mult), reads=[ti, sz.sub(jc * (Ttot // 2), Ttot // 2)], writes=[gT.sub(jc * (Ttot // 2), Ttot // 2)])

        def exchange_send(hd):
            c.dma("sp", "car", lambda: sp.dma_start(out=car_in[hd], in_=c1.f32()), reads=[c1], writes=[("car_in", hd)])
            c.cc(f"carcc{hd}", lambda: pool.collective_compute("AllGather", ALU.bypass, replica_groups=RG, ins=[car_in[hd]], outs=[car_out[hd]]),
                 reads=[("car_in", hd)], writes=[("car_out", hd)])
            c.dma("sp", "car", lambda: sp.dma_start(out=cg.f32().rearrange("p (r n) -> p r n", r=2), in_=car_out[hd].rearrange("(r p) n -> p r n", p=128)),
                  reads=[("car_out", hd)], writes=[cg])

        tiles = [(tile0 + i, i * 128) for i in range(ntiles)]
        loads(0)
        part_uz(0)
        part_conv(0)
        for hd in range(8):
            stage(hd, 0, 0)
            stage(hd, 0, 1)
            if is_sample:
                exchange_send(hd)
            wout_load(1, hd, gate_bc, None, wob)
            stage(hd, 1, 0)
            if hd + 1 < 8:
                loads(hd + 1)
            stage(hd, 1, 1)
            if hd + 1 < 8:
                part_uz(hd + 1)
            wout_update(1, hd, gT, gate_bc, None, wob, tiles, load=False)
            if hd + 1 < 8:
                part_conv(hd + 1)
        c.release(mk)

    def final_out(ntiles, tile0, dst):
        mk = c.mark()
        fg_bc = c.alloc(1024)
        tmp = [c.alloc(1024), c.alloc(1024)]
        c.dma("sp", "bc", lambda: sp.dma_start(out=fg_bc.f32(), in_=fgr_d[0, :].partition_broadcast(128)), writes=[fg_bc])
        for t in range(ntiles):
            s = t % 2
            xr = x1t(tile0 + t)
            ssr = ss.sub(t, 1); rsr = rstd.sub(t, 1)
            c.op("act", lambda xr=xr, s=s, ssr=ssr: act.activation(out=tmp[s].f32(), in_=xr.f32(), func=AF.Square, accum_out=ssr.f32()), reads=[xr], writes=[tmp[s], ssr])
            c.op("act", lambda ssr=ssr, rsr=rsr: act.activation(out=rsr.f32(), in_=ssr.f32(), func=AF.Sqrt, scale=1.0 / D, bias=EPS), reads=[ssr], writes=[rsr])
            c.op("dve", lambda rsr=rsr: dve.reciprocal(out=rsr.f32(), in_=rsr.f32()), reads=[rsr], writes=[rsr])
            c.op("dve", lambda xr=xr, s=s, rsr=rsr: dve.scalar_tensor_tensor(out=tmp[s].f32(), in0=xr.f32(), scalar=rsr.f32(), in1=fg_bc.f32(), op0=ALU.mult, op1=ALU.mult),
                 reads=[xr, rsr, fg_bc], writes=[tmp[s]])
            c.dma("sp", f"out{s}", lambda s=s, t=t: sp.dma_start(out=dst[t * 128:(t + 1) * 128, :], in_=tmp[s].f32()), reads=[tmp[s]], writes=[("dout", id(dst), t)])
        c.release(mk)

    layer1(TS, [(0, TS)], 0, True, 0)
    dump("x1_l1", lambda: x1.f32(), [128, 16 * 1024], [x1])
    chk("l1s")
    final_out(16, 0, ys)
    chk("sample")

    mkP = c.mark()
    tabs = load_tables()
    chtab = tabs["ch"]
    front_end(4, 0, 0, 1, xsrc=xp, possrc=None, xdst=x1t)
    mk = c.mark()
    TT = 2 * TP
    wus = [c.alloc(1024), c.alloc(1024)]
    u_tm = c.alloc(2 * 128); Yt = c.alloc(512); Qt = [c.alloc(256) for _ in range(3)]
    gate_bc = c.alloc(1024); wstg = c.alloc(2048); wob = c.alloc(1024)
    wzs = [c.alloc(1024), c.alloc(1024)]
    sz = c.alloc(TT); fT = c.alloc(TT); wf = [c.alloc(256), c.alloc(256)]; gT = c.alloc(TT)
    bcast_ada(0, 1, "gate", gate_bc)
    slotsP = [(tabs["k"][4], tabs["k"][5], False)]
    fv = fT.bf().rearrange("p (l t) -> p l t", l=2)
    for g in range(8):
        load_w_in(0, g * 256, wus[g % 2], f"wu{g % 2}")
        load_w_in(0, DI + g * 256, wzs[g % 2], f"wz{g % 2}")
        c.dma("pool", f"wf{g % 2}", lambda g=g: pool.dma_start(out=wf[g % 2].bf().rearrange("p (l n) -> p l n", l=2), in_=w_fmix[g].rearrange("(l p) n -> p l n", p=128)), writes=[wf[g % 2]])
        for q in range(2):
            def dest(si, lc, q=q):
                return fv[:, lc, q * TP:(q + 1) * TP], fT.sub(lc * (TT // 2) + q * TP // 2, TP // 2)
            l0_pre_seq(g, wus[g % 2], q * TP, TP, 2, tabs["s1p"], slotsP, dest, u_tm, Yt, Qt)
        l0_post_group(g, TT, [(0, TP), (TP, TP)], fT, wzs[g % 2], sz, wf[g % 2], gT, gate_bc, wstg, wob, 0)
    c.release(mk)
    c.release(mkP)
    layer1(TT, [(0, TP), (TP, TP)], 1, False, 0)
    c.dma("sp", "stpo", lambda: sp.dma_start(out=stp, in_=stpo.f32()), reads=[stpo], writes=[("stp",)])
    final_out(4, 0, yp)


def build_nc():
    nc0 = bass.Bass("TRN2", target_bir_lowering=False)
    with ExitStack() as st:
        c0 = Ctx(nc0, st, needed=None)
        _emit(nc0, c0)
        needed = c0.need_out
    nc = bass.Bass("TRN2", target_bir_lowering=False)
    with ExitStack() as st:
        c = Ctx(nc, st, needed=needed)
        _emit(nc, c)
        print(f"[kernel] insts={c.ninst} waits={c.nwaits} incs={ {e: len(needed[e]) for e in needed} } arena_top={c.top}")
    return nc


def _col(v, nchunks):
    return np.ascontiguousarray(np.asarray(v, np.float32).reshape(nchunks, 128).T)


_NC_CACHE = {}
_RETURN_MAPS = False


def kernel(x_prompt, x_sample, state_lru_1, c, c_ctx,
           norm_g_0, w_ada_0, b_ada_0, w_in_0, w_fmix_0, b_fmix_0, w_out_0,
           norm_g_1, w_ada_1, b_ada_1, w_in_1, conv_w_1, conv_b_1,
           w_rgate_1, b_rgate_1, w_igate_1, b_igate_1, lam_1, w_out_1,
           final_g):
    f = lambda a: np.ascontiguousarray(np.asarray(a, np.float32))
    x_prompt, x_sample, state_lru_1, c, c_ctx = map(f, (x_prompt, x_sample, state_lru_1, c, c_ctx))
    pos = pos_embed(4096, D)
    ident = np.eye(128, dtype=np.float32)
    cht = ch_tables().astype(np.float32)
    tabs_s = [dft_tables_sample(s) for s in (0, 1)]
    tabs_p = [dft_tables_prompt(bool(s)) for s in (0, 1)]
    common = {
        "w_ada_0": f(w_ada_0), "w_ada_1": f(w_ada_1), "badar_0": f(b_ada_0).reshape(1, 3072), "badar_1": f(b_ada_1).reshape(1, 3072),
        "ngr_0": f(norm_g_0).reshape(1, 1024), "ngr_1": f(norm_g_1).reshape(1, 1024), "fgr": f(final_g).reshape(1, 1024),
        "w_in_0": f(w_in_0), "w_in_1": f(w_in_1), "w_fmix": f(w_fmix_0), "bfm": _col(b_fmix_0, 16),
        "w_out_0": f(w_out_0), "w_out_1": f(w_out_1), "convb": _col(conv_b_1, 16),
        "ident": ident, "chtab": cht,
    }
    cw = f(conv_w_1)
    zero = np.zeros_like(cw[0])
    in_maps = []
    for cid in range(NCORES):
        p, s = cid // 2, cid % 2
        m = dict(common)
        xs = x_sample[p]
        m["xs_in"] = np.ascontiguousarray(xs[s::2]); m["pos_in"] = np.ascontiguousarray(pos[s::2])
        if s == 0:
            m["xs_res"] = np.ascontiguousarray(xs[:TS]); m["pos_res"] = np.ascontiguousarray(pos[:TS])
        else:
            m["xs_res"] = np.ascontiguousarray(xs[::-1][:TS]); m["pos_res"] = np.ascontiguousarray(pos[::-1][:TS])
        xpp = x_prompt[2 * cid:2 * cid + 2]
        if s == 1:
            xpp = xpp[:, ::-1]
        m["xp"] = np.ascontiguousarray(xpp.reshape(2 * TP, D))
        condT = np.zeros((128, 16), np.float32)
        condT[:, 0::2] = _col(c[p], 8); condT[:, 1::2] = _col(c_ctx, 8)
        m["condT"] = condT
        sel = np.zeros((128, 2), np.float32); sel[:, 1 - s] = 1.0
        m["sel"] = sel
        m["st0"] = _col(state_lru_1[p, s], 16)
        taps = [cw[0], cw[1], cw[2], cw[3], zero] if s == 0 else [zero, cw[3], cw[2], cw[1], cw[0]]
        m["convw"] = np.ascontiguousarray(np.stack([_col(t, 16) for t in taps], axis=-1).reshape(128, 80))
        d1, d2 = (0, 1) if s == 0 else (1, 0)
        m["wr"] = np.ascontiguousarray(f(w_rgate_1)[[d1, d2]]); m["wi"] = np.ascontiguousarray(f(w_igate_1)[[d1, d2]])
        m["br"] = np.concatenate([_col(b_rgate_1[d1], 16), _col(b_rgate_1[d2], 16)], 1)
        m["bi"] = np.concatenate([_col(b_igate_1[d1], 16), _col(b_igate_1[d2], 16)], 1)
        m["lam"] = np.concatenate([_col(lam_1[d1], 16), _col(lam_1[d2], 16)], 1)
        s1, KrA, KiA, KrB, KiB = tabs_s[s]
        s1p, KrP, KiP = tabs_p[s]
        m["s1s"] = s1.astype(np.float32); m["s1p"] = s1p.astype(np.float32)
        m["kmat"] = np.stack([KrA, KiA, KrB, KiB, KrP, KiP]).astype(np.float32)
        in_maps.append({k: np.ascontiguousarray(v, dtype=np.float32) for k, v in m.items()})

    if _RETURN_MAPS:
        return in_maps
    if "nc" not in _NC_CACHE:
        _NC_CACHE["nc"] = build_nc()
    res = run_bass_kernel_spmd(_NC_CACHE["nc"], in_maps, core_ids=list(range(NCORES)))
    outs = res.results

    y_prompt = np.zeros((16, TP, D), np.float32)
    y_sample = np.zeros((4, 4096, D), np.float32)
    new_state = np.zeros((16, 2, DI), np.float32)
    for cid in range(NCORES):
        p, s = cid // 2, cid % 2
        ysv = np.asarray(outs[cid]["ys"], np.float32)
        ypv = np.asarray(outs[cid]["yp"], np.float32).reshape(2, TP, D)
        stv = np.asarray(outs[cid]["stp"], np.float32).reshape(128, 2, 2, 16)
        stv = stv.transpose(1, 2, 3, 0).reshape(2, 2, DI)
        if s == 0:
            y_sample[p, :TS] = ysv
            y_prompt[2 * cid:2 * cid + 2] = ypv
            new_state[2 * cid:2 * cid + 2] = stv
        else:
            y_sample[p, TS:] = ysv[::-1]
            y_prompt[2 * cid:2 * cid + 2] = ypv[:, ::-1]
            new_state[2 * cid:2 * cid + 2] = stv[:, ::-1]
    return (y_prompt, y_sample, new_state)
```

```python
import numpy as np
from contextlib import ExitStack
import concourse.bass as bass
import concourse.mybir as mybir
from concourse.bass_utils import run_bass_kernel_spmd

F32 = mybir.dt.float32
BF16 = mybir.dt.bfloat16
AF = mybir.ActivationFunctionType
ALU = mybir.AluOpType

D = 1024
DI = 2048
EPS = 1e-6
GRID_W = 64
POS_BASE = 10000.0
NCORES = 8
TS = 2048
TP = 256
NW = 53100
BLK = 64


def pos_embed(n_tokens, dim):
    rows = n_tokens // GRID_W
    r = np.broadcast_to(np.arange(rows, dtype=np.float32)[:, None], (rows, GRID_W)).reshape(-1)
    col = np.broadcast_to(np.arange(GRID_W, dtype=np.float32)[None, :], (rows, GRID_W)).reshape(-1)
    quarter = dim // 4
    omega = (1.0 / (np.float32(POS_BASE) ** (np.arange(quarter, dtype=np.float32) / np.float32(quarter)))).astype(np.float32)
    er = (r[:, None] * omega).astype(np.float32)
    ec = (col[:, None] * omega).astype(np.float32)
    return np.concatenate([np.sin(er), np.cos(er), np.sin(ec), np.cos(ec)], axis=-1).astype(np.float32)


def s1_table(S, N2, alpha, beta, gamma, delta):
    m1 = np.arange(128)[:, None]
    k1 = np.arange(128)[None, :]
    out = np.zeros((N2, 128, 256), np.float64)
    for m2 in range(N2):
        m = N2 * m1 + m2
        ph = -2 * np.pi * (((alpha * m + beta) * (gamma * k1 + delta)) % S) / S
        out[m2, :, :128] = np.cos(ph)
        out[m2, :, 128:] = np.sin(ph)
    return out / np.sqrt(128.0)


def s2_kron(S2c, N2, flip):
    NI = 128 // N2
    Kr = np.zeros((128, 128))
    Ki = np.zeros((128, 128))
    for m2 in range(N2):
        for k2 in range(N2):
            for i in range(NI):
                ip = (NI - 1 - i) if flip else i
                Kr[m2 * NI + i, k2 * NI + ip] = S2c[m2, k2].real
                Ki[m2 * NI + i, k2 * NI + ip] = -S2c[m2, k2].imag
    return Kr, Ki


def dft_tables_prompt(rev):
    S = 256
    N2 = 2
    if rev:
        alpha, beta, gamma, delta = -1, 255, -1, 255
    else:
        alpha, beta, gamma, delta = 1, 0, 1, 0
    s1 = s1_table(S, N2, alpha, beta, gamma, delta)
    m2 = np.arange(N2)[:, None]
    k2 = np.arange(N2)[None, :]
    S2c = np.exp(-2j * np.pi * ((alpha * gamma * 128 * m2 * k2 + beta * gamma * 128 * k2) % S) / S) / np.sqrt(S / 128.0)
    Kr, Ki = s2_kron(S2c, N2, False)
    return s1, Kr, Ki


def dft_tables_sample(sig):
    S = 4096
    N2 = 16
    s1 = s1_table(S, N2, 2, sig, 1, 0)
    m2 = np.arange(N2)[:, None]
    k2 = np.arange(N2)[None, :]
    nrm = np.sqrt(S / 128.0)
    S2A = np.exp(-2j * np.pi * (((2 * m2 + sig) * k2) % 32) / 32) / nrm
    S2B = np.exp(-2j * np.pi * (((2 * m2 + sig) * (31 - k2)) % 32) / 32) / nrm
    KrA, KiA = s2_kron(S2A, N2, False)
    KrB, KiB = s2_kron(S2B, N2, True)
    return s1, KrA, KiA, KrB, KiB


def ch_tables():
    c = np.arange(256)[:, None]
    l = np.arange(256)[None, :]
    ph = 2 * np.pi * ((c * l) % 256) / 256
    C = np.cos(ph) / 16.0
    S_ = np.sin(ph) / 16.0
    return np.stack([np.concatenate([C, -S_], 1), np.concatenate([S_, C], 1)], 0)


class Region:
    def __init__(self, ctx, off, n):
        self.ctx, self.off, self.n = ctx, off, n

    def f32(self):
        return self.ctx.arena[:, self.off:self.off + self.n]

    def bf(self):
        return self.ctx.arena[:, self.off:self.off + self.n].bitcast(BF16)

    def sub(self, lo, n):
        assert lo >= 0 and lo + n <= self.n, (lo, n, self.n)
        return Region(self.ctx, self.off + lo, n)

    def keys(self):
        return [("A", b) for b in range(self.off // BLK, (self.off + self.n - 1) // BLK + 1)]


def K(*items):
    out = []
    for it in items:
        if isinstance(it, Region):
            out.extend(it.keys())
        elif isinstance(it, (list, tuple)) and len(it) > 0 and isinstance(it[0], (Region, list)):
            out.extend(K(*it))
        elif isinstance(it, list):
            out.extend(it)
        else:
            out.append(it)
    return out


class Ctx:
    COMPUTE = ("pe", "act", "dve", "pool")

    def __init__(self, nc, stack, needed=None):
        self.nc = nc
        self.stack = stack
        self.dry = needed is None
        self.needed = needed
        self.need_out = {e: set() for e in self.COMPUTE}
        self.eng = {"pe": nc.tensor, "act": nc.scalar, "dve": nc.vector, "pool": nc.gpsimd, "sp": nc.sync}
        self.csem = {e: stack.enter_context(nc.semaphore("c_" + e)) for e in self.COMPUTE}
        self.ccnt = {e: 0 for e in self.COMPUTE}
        self.rank = None
        if needed is not None:
            self.rank = {}
            for e in self.COMPUTE:
                self.rank[e] = {idx: i + 1 for i, idx in enumerate(sorted(needed[e]))}
        self.dsem = {}
        self.dcnt = {}
        self.waited = {e: {} for e in self.eng}
        self.lastw = {}
        self.readers = {}
        self.nwaits = 0
        self.ninst = {e: 0 for e in self.eng}
        self.arena = stack.enter_context(nc.sbuf_tensor("arena", [128, NW], F32))
        self.top = 0
        self.bankrr = 0
        self.drrq = {}
        self.ptrr = 0

    def alloc(self, n):
        n_al = (n + BLK - 1) // BLK * BLK
        r = Region(self, self.top, n)
        self.top += n_al
        assert self.top <= NW, f"arena overflow {self.top} > {NW}"
        return r

    def mark(self):
        return self.top

    def release(self, m):
        self.top = m

    def _deps(self, e, reads, writes):
        deps = {}

        def add(t):
            if t is None:
                return
            s, v, src = t
            if src == "pe" and e == "pe":
                return
            if s not in deps or deps[s] < v:
                deps[s] = v
        for k in reads:
            add(self.lastw.get(k))
        for k in writes:
            add(self.lastw.get(k))
            for t in self.readers.get(k, ()):
                add(t)
        w = self.waited[e]
        for s, v in deps.items():
            if w.get(s, 0) >= v:
                continue
            w[s] = v
            self.nwaits += 1
            if s in self.COMPUTE:
                self.need_out[s].add(v)
                if not self.dry:
                    self.eng[e].wait_ge(self.csem[s], self.rank[s][v])
            else:
                if not self.dry:
                    self.eng[e].wait_ge(self.dsem[s], v)

    def _record(self, t, reads, writes):
        for k in writes:
            self.lastw[k] = t
            self.readers[k] = []
        for k in reads:
            self.readers.setdefault(k, []).append(t)

    def op(self, e, fn, reads=(), writes=()):
        reads = K(*reads)
        writes = K(*writes)
        self._deps(e, reads, writes)
        self.ccnt[e] += 1
        self.ninst[e] += 1
        idx = self.ccnt[e]
        if not self.dry:
            inst = fn()
            if idx in self.rank[e]:
                inst.then_inc(self.csem[e], 1)
        self._record((e, idx, e), reads, writes)

    NPOOL = 28

    def _dsem(self, name):
        if name not in self.dsem:
            self.dsem[name] = self.stack.enter_context(self.nc.semaphore("d_" + name))
            self.dcnt[name] = 0

    def dma(self, q, semname, fn, reads=(), writes=()):
        reads = K(*reads)
        writes = K(*writes)
        self._deps(q, reads, writes)
        rr = self.drrq.setdefault(q, 0)
        self.drrq[q] = rr + 1
        name = "%s%d" % (q, rr % self.NPOOL)
        self._dsem(name)
        prev = self.dcnt[name]
        if prev > 0 and self.waited[q].get(name, 0) < prev:
            self.waited[q][name] = prev
            self.nwaits += 1
            if not self.dry:
                self.eng[q].wait_ge(self.dsem[name], prev)
        self.dcnt[name] += 16
        self.ninst[q] += 1
        if not self.dry:
            fn().then_inc(self.dsem[name], 16)
        self._record((name, self.dcnt[name], "dma"), reads, writes)

    def cc(self, semname, fn, reads=(), writes=()):
        reads = K(*reads)
        writes = K(*writes)
        self._deps("pool", reads, writes)
        self._dsem(semname)
        self.dcnt[semname] += 1
        if not self.dry:
            fn().then_inc(self.dsem[semname], 1)
        self._record((semname, self.dcnt[semname], "cc"), reads, writes)

    def finish(self, e="sp"):
        allk = list(set(self.lastw) | set(self.readers))
        self._deps(e, allk, allk)


STOP = None
DEBUG = ()


class _Stop(Exception):
    pass


def _emit(nc, c):
    try:
        _emit_body(nc, c)
    except _Stop:
        pass
    c.finish("sp")


def _emit_body(nc, c):
    def chk(name):
        if STOP == name:
            raise _Stop()

    def dump(name, ap_fn, shape, reads, dt=F32):
        if name not in DEBUG:
            return
        d = nc.dram_tensor("dbg_" + name, list(shape), dt, kind="ExternalOutput").ap()
        c.dma("sp", "dbg_" + name, lambda: nc.sync.dma_start(out=d, in_=ap_fn()), reads=reads, writes=[("dbg", name)])
    eng = c.eng
    pe, act, dve, pool, sp = nc.tensor, nc.scalar, nc.vector, nc.gpsimd, nc.sync

    def din(name, shape, dt=F32):
        return nc.dram_tensor(name, list(shape), dt, kind="ExternalInput").ap()

    def dout(name, shape, dt=F32):
        return nc.dram_tensor(name, list(shape), dt, kind="ExternalOutput").ap()

    def dint(name, shape, dt):
        return nc.dram_tensor(name, list(shape), dt, kind="Internal").ap()

    xs_in = din("xs_in", [TS, D]); pos_in = din("pos_in", [TS, D])
    xs_res = din("xs_res", [TS, D]); pos_res = din("pos_res", [TS, D])
    xp = din("xp", [2 * TP, D])
    condT = din("condT", [128, 16]); sel_d = din("sel", [128, 2]); st0_d = din("st0", [128, 16])
    w_ada = [din("w_ada_0", [D, 3 * D]), din("w_ada_1", [D, 3 * D])]
    badar_d = [din("badar_0", [1, 3072]), din("badar_1", [1, 3072])]
    ngr_d = [din("ngr_0", [1, 1024]), din("ngr_1", [1, 1024])]
    fgr_d = din("fgr", [1, 1024])
    w_in = [din("w_in_0", [D, 2 * DI]), din("w_in_1", [D, 2 * DI])]
    w_fmix = din("w_fmix", [8, 256, 256]); bfm_d = din("bfm", [128, 16])
    w_out = [din("w_out_0", [DI, D]), din("w_out_1", [DI, D])]
    convw_d = din("convw", [128, 80]); convb_d = din("convb", [128, 16])
    wr_d = din("wr", [2, 8, 256, 256]); wi_d = din("wi", [2, 8, 256, 256])
    br_d = din("br", [128, 32]); bi_d = din("bi", [128, 32]); lam_d = din("lam", [128, 32])
    ident_d = din("ident", [128, 128])
    s1s_d = din("s1s", [16, 128, 256]); s1p_d = din("s1p", [2, 128, 256])
    kmat_d = din("kmat", [6, 128, 128]); chtab_d = din("chtab", [2, 256, 512])
    ys = dout("ys", [TS, D]); yp = dout("yp", [2 * TP, D]); stp = dout("stp", [128, 64])
    rs_in = [dint(f"rs_in{g}", [512, TS], BF16) for g in range(8)]
    rs_out = [dint(f"rs_out{g}", [256, TS], BF16) for g in range(8)]
    halo_in = dint("halo_in", [128, 32], F32); halo_out = dint("halo_out", [256, 32], F32)
    car_in = [dint(f"car_in{h}", [128, 2], F32) for h in range(8)]
    car_out = [dint(f"car_out{h}", [256, 2], F32) for h in range(8)]
    RG = [[0, 1], [2, 3], [4, 5], [6, 7]]

    psb = [c.stack.enter_context(nc.psum_tensor(f"ps{i}", [128, 512], F32)) for i in range(6)]
    ptb = [c.stack.enter_context(nc.psum_tensor(f"pt{i}", [128, 1024], BF16)) for i in range(2)]

    def bank():
        b = c.bankrr
        c.bankrr = (b + 1) % 6
        return b, psb[b], ("ps", b)

    def tbank():
        b = c.ptrr
        c.ptrr = (b + 1) % 2
        return ptb[b], ("pt", b)

    evac_rr = [0]

    def evac_copy(out_ap, in_ap, reads, writes, flip=True):
        if flip:
            evac_rr[0] ^= 1
        if evac_rr[0]:
            c.op("act", lambda: act.copy(out=out_ap, in_=in_ap), reads, writes)
        else:
            c.op("dve", lambda: dve.tensor_copy(out=out_ap, in_=in_ap), reads, writes)

    x1 = c.alloc(16 * 1024)
    hT = c.alloc(8 * 1024)
    smalls = c.alloc(1280)
    so = [0]

    def S(n):
        r = smalls.sub(so[0], n)
        so[0] += n
        return r
    identb = S(64)
    ss = S(32); rstd = S(32)
    scond = S(16); selr = S(2); st0 = S(16); zcol = S(1)
    bfm = S(16); convw = S(80); convb = S(16)
    brc = S(32); bic = S(32); cl = S(32); cl2 = S(32)
    halo = S(32); hg = S(64); c1 = S(2); c2 = S(2); cg = S(4); ctmp = S(32)
    stpo = S(64)
    diagb = [S(64) for _ in range(5)]
    hTv = hT.bf().rearrange("p (k t) -> p k t", k=8)
    x1v = x1.f32().rearrange("p (t d) -> p t d", t=16)

    def x1t(t):
        return x1.sub(t * 1024, 1024)

    def hTk(kc, c0, n):
        return hT.sub(kc * 1024 + c0 // 2, (n + 1) // 2)

    def hT_cols(c0, n):
        return [hTk(kc, c0, n) for kc in range(8)]

    def ld(r, src, sem="small"):
        c.dma("sp", sem, lambda: sp.dma_start(out=r.f32(), in_=src), writes=[r])
    ld(scond, condT); ld(selr, sel_d); ld(st0, st0_d)
    ld(bfm, bfm_d); ld(convw, convw_d); ld(convb, convb_d)
    ld(brc, br_d); ld(bic, bi_d); ld(cl, lam_d)
    c.dma("pool", "identb", lambda: pool.dma_start(out=identb.bf(), in_=ident_d), writes=[identb])
    c.op("dve", lambda: dve.memset(zcol.f32(), 0.0), writes=[zcol])
    c.op("act", lambda: act.activation(out=scond.f32(), in_=scond.f32(), func=AF.Silu), reads=[scond], writes=[scond])
    c.op("act", lambda: act.activation(out=cl.f32(), in_=cl.f32(), func=AF.Exp, scale=-1.0), reads=[cl], writes=[cl])
    c.op("act", lambda: act.activation(out=cl.f32(), in_=cl.f32(), func=AF.Ln, bias=1.0), reads=[cl], writes=[cl])
    c.op("dve", lambda: dve.tensor_scalar(out=cl2.f32(), in0=cl.f32(), scalar1=-4.0, scalar2=None, op0=ALU.mult), reads=[cl], writes=[cl2])
    c.op("dve", lambda: dve.tensor_scalar(out=cl.f32(), in0=cl.f32(), scalar1=-8.0, scalar2=None, op0=ALU.mult), reads=[cl], writes=[cl])
    c.op("dve", lambda: dve.tensor_scalar(out=brc.f32(), in0=brc.f32(), scalar1=0.5, scalar2=None, op0=ALU.mult), reads=[brc], writes=[brc])
    c.op("dve", lambda: dve.tensor_scalar(out=bic.f32(), in0=bic.f32(), scalar1=0.5, scalar2=None, op0=ALU.mult), reads=[bic], writes=[bic])

    ada_dram = [dint(f"ada_rows{l}", [2, 3072], F32) for l in range(2)]

    def ada_layer(l):
        mk = c.mark()
        NS = 3
        wst = [c.alloc(3072) for _ in range(NS)]
        arow = c.alloc(3072); brow = c.alloc(3072); grow = c.alloc(1024)
        c.dma("sp", "brow", lambda: sp.dma_start(out=brow.f32()[0:2, :], in_=badar_d[l][0, :].partition_broadcast(2)), writes=[brow])
        c.dma("sp", "grow", lambda: sp.dma_start(out=grow.f32()[0:2, :], in_=ngr_d[l][0, :].partition_broadcast(2)), writes=[grow])
        for kc in range(8):
            r = wst[kc % NS]
            c.dma("sp", "wada", lambda kc=kc, r=r: sp.dma_start(out=r.f32(), in_=w_ada[l][kc * 128:(kc + 1) * 128, :]), writes=[r])
            for n in range(6):
                c.op("pe", lambda kc=kc, n=n, r=r: pe.matmul(psb[n][0:2, :], lhsT=scond.f32()[:, kc * 2:kc * 2 + 2], rhs=r.f32()[:, n * 512:(n + 1) * 512], start=(kc == 0), stop=(kc == 7)),
                     reads=[r, scond], writes=[("ps", n)])
        for n in range(6):
            c.op("dve", lambda n=n: dve.tensor_tensor(out=arow.f32()[0:2, n * 512:(n + 1) * 512], in0=psb[n][0:2, :], in1=brow.f32()[0:2, n * 512:(n + 1) * 512], op=ALU.add),
                 reads=[("ps", n), brow], writes=[arow.sub(n * 512, 512)])
        c.op("dve", lambda: dve.scalar_tensor_tensor(out=arow.f32()[0:2, 1024:2048], in0=arow.f32()[0:2, 1024:2048], scalar=1.0, in1=grow.f32()[0:2, :], op0=ALU.add, op1=ALU.mult),
             reads=[arow, grow], writes=[arow.sub(1024, 1024)])
        c.dma("sp", "adaout", lambda: sp.dma_start(out=ada_dram[l], in_=arow.f32()[0:2, :]), reads=[arow], writes=[("ada", l)])
        c.release(mk)

    WHICH = {"shift": 0, "m": 1, "gate": 2}

    def bcast_ada(l, j, which, dst):
        w = WHICH[which]
        c.dma("sp", "bc", lambda: sp.dma_start(out=dst.f32(), in_=ada_dram[l][j, w * 1024:(w + 1) * 1024].partition_broadcast(128)), reads=[("ada", l)], writes=[dst])

    ada_layer(0)

    def front_end(ntiles, col0, l, j, xsrc=None, possrc=None, xdst=None, order=None, after_first=None):
        mk = c.mark()
        m_bc = c.alloc(1024); sh_bc = c.alloc(1024)
        NSL = 3
        NLD = 4 if xdst is None else 5
        tmp = [c.alloc(1024) for _ in range(NSL)]
        hb = [c.alloc(512) for _ in range(NSL)]
        xt = [c.alloc(1024) for _ in range(NLD)] if xdst is None else None
        posb = [c.alloc(1024) for _ in range(NLD)] if possrc is not None else None
        bcast_ada(l, j, "m", m_bc)
        bcast_ada(l, j, "shift", sh_bc)
        for it, t in enumerate(order if order is not None else range(ntiles)):
            s = it % NSL
            sl = it % NLD
            xr = xdst(t) if xdst is not None else xt[sl]
            if xsrc is not None:
                c.dma("sp", f"x{s}", lambda xr=xr, t=t: sp.dma_start(out=xr.f32(), in_=xsrc[t * 128:(t + 1) * 128, :]), writes=[xr])
                if possrc is not None:
                    c.dma("sp", f"p{sl}", lambda sl=sl, t=t: sp.dma_start(out=posb[sl].f32(), in_=possrc[t * 128:(t + 1) * 128, :]), writes=[posb[sl]])
                    c.op("dve", lambda xr=xr, sl=sl: dve.tensor_tensor(out=xr.f32(), in0=xr.f32(), in1=posb[sl].f32(), op=ALU.add), reads=[xr, posb[sl]], writes=[xr])
            ssr = ss.sub(t, 1); rsr = rstd.sub(t, 1)
            c.op("act", lambda xr=xr, s=s, ssr=ssr: act.activation(out=tmp[s].f32(), in_=xr.f32(), func=AF.Square, accum_out=ssr.f32()),
                 reads=[xr], writes=[tmp[s], ssr])
            c.op("act", lambda ssr=ssr, rsr=rsr: act.activation(out=rsr.f32(), in_=ssr.f32(), func=AF.Sqrt, scale=1.0 / D, bias=EPS),
                 reads=[ssr], writes=[rsr])
            c.op("dve", lambda rsr=rsr: dve.reciprocal(out=rsr.f32(), in_=rsr.f32()), reads=[rsr], writes=[rsr])
            c.op("dve", lambda xr=xr, s=s, rsr=rsr: dve.scalar_tensor_tensor(out=tmp[s].f32(), in0=xr.f32(), scalar=rsr.f32(), in1=m_bc.f32(), op0=ALU.mult, op1=ALU.mult),
                 reads=[xr, rsr, m_bc], writes=[tmp[s]])
            c.op("dve", lambda s=s: dve.tensor_tensor(out=hb[s].bf(), in0=tmp[s].f32(), in1=sh_bc.f32(), op=ALU.add),
                 reads=[tmp[s], sh_bc], writes=[hb[s]])
            pt, ptk = tbank()
            for kc in range(8):
                c.op("pe", lambda kc=kc, s=s, pt=pt: pe.transpose(pt[:, kc * 128:(kc + 1) * 128], hb[s].bf()[:, kc * 128:(kc + 1) * 128], identb.bf()),
                     reads=[hb[s], identb], writes=[ptk])
            cc0 = col0 + t * 128
            c.op("act", lambda pt=pt, cc0=cc0: act.copy(out=hTv[:, :, cc0:cc0 + 128], in_=pt[:, :].rearrange("p (k t) -> p k t", k=8)),
                 reads=[ptk], writes=hT_cols(cc0, 128))
            if it == 0 and after_first is not None:
                after_first()
        c.release(mk)

    def load_w_in(l, colstart, dst, sem):
        src = w_in[l].rearrange("(k p) n -> p k n", p=128)[:, :, colstart:colstart + 256]
        c.dma("pool", sem, lambda: pool.dma_start(out=dst.bf().rearrange("p (k n) -> p k n", k=8), in_=src), writes=[dst])

    def pieces_of(seqs):
        out = []
        for (c0, T) in seqs:
            for o in range(0, T, 512):
                out.append((c0 + o, min(512, T - o)))
        return out

    def l0_pre_seq(g, wu, col0, T, N2, s1r, slots, dest, u_tm, Yt, Qt):
        NI = 128 // N2
        wuv = wu.bf().rearrange("p (k n) -> p k n", k=8)
        utv = u_tm.bf().rearrange("p (m n) -> p m n", n=256)
        s1v = s1r.bf().rearrange("p (m n) -> p m n", n=256)
        Ytv = Yt.bf().rearrange("p (c q j m i) -> p c q j m i", c=2, q=2, j=N2, m=N2)
        for m2 in range(N2):
            b, ps, pk = bank()
            for kc in range(8):
                c.op("pe", lambda kc=kc, m2=m2: pe.matmul(ps[:, 0:256], lhsT=hTv[:, kc, col0 + m2:col0 + T:N2], rhs=wuv[:, kc, :], start=(kc == 0), stop=(kc == 7)),
                     reads=[hTk(kc, col0, T), wu], writes=[pk])
            ur = u_tm.sub(m2 * 128, 128)
            evac_copy(utv[:, m2, :], ps[:, 0:256], [pk], [ur])
        chk("pre_u")
        for m2 in range(N2):
            ur = u_tm.sub(m2 * 128, 128)
            for cc in range(2):
                b, ps, pk = bank()
                c.op("pe", lambda m2=m2, cc=cc: pe.matmul(ps[:, 0:256], lhsT=utv[:, m2, cc * 128:(cc + 1) * 128], rhs=s1v[:, m2, :], start=True, stop=True),
                     reads=[ur, s1r.sub(m2 * 128, 128)], writes=[pk])
                evac_copy(Ytv[:, cc, :, :, m2, :], ps[:, 0:256].rearrange("p (q j i) -> p q j i", q=2, j=N2), [pk], [Yt])
        chk("pre_s1")
        chv = chtab.bf().rearrange("p (q c n) -> p q c n", q=2, c=2)
        def s2_stage(j, q):
            b2, ps2, pk2 = bank()
            for si, (Kr, Ki, flip) in enumerate(slots):
                for lc in range(2):
                    o = ps2[:, (si * 2 + lc) * 128:(si * 2 + lc + 1) * 128]
                    c.op("pe", lambda o=o, lc=lc, Kr=Kr: pe.matmul(o, lhsT=q.bf()[:, lc * 128:(lc + 1) * 128], rhs=Kr.bf(), start=True, stop=False),
                         reads=[q, Kr], writes=[pk2])
                    c.op("pe", lambda o=o, lc=lc, Ki=Ki: pe.matmul(o, lhsT=q.bf()[:, 256 + lc * 128:256 + (lc + 1) * 128], rhs=Ki.bf(), start=False, stop=True),
                         reads=[q, Ki], writes=[pk2])
            for si, (Kr, Ki, flip) in enumerate(slots):
                jb = (N2 - 1 - j) if flip else j
                for lc in range(2):
                    o = ps2[:, (si * 2 + lc) * 128:(si * 2 + lc + 1) * 128].rearrange("p (k i) -> p k i", i=NI)
                    dap, dreg = dest(si, lc)
                    d = dap.rearrange("p (k c) -> p k c", c=128)[:, :, jb * NI:(jb + 1) * NI]
                    evac_copy(d, o, [pk2], [dreg], flip=(si == 0 and lc == 0))

        prev = None
        for j in range(N2):
            b, ps, pk = bank()
            n = 0
            for cc in range(2):
                for comp in range(2):
                    c.op("pe", lambda cc=cc, comp=comp, n=n: pe.matmul(ps[:, :], lhsT=Ytv[:, cc, comp, j, :, :].rearrange("p m i -> p (m i)"), rhs=chv[:, comp, cc, :], start=(n == 0), stop=(n == 3)),
                         reads=[Yt, chtab], writes=[pk])
                    n += 1
            q = Qt[j % len(Qt)]
            evac_copy(q.bf(), ps[:, :], [pk], [q])
            if prev is not None:
                s2_stage(*prev)
            prev = (j, q)
        s2_stage(*prev)

    def wout_load(l, hd, gate_bc, wstg, wob):
        src = w_out[l][hd * 256:(hd + 1) * 256, :].rearrange("(l p) d -> p l d", p=128)
        if wstg is not None:
            c.dma("sp", "wo", lambda: sp.dma_start(out=wstg.f32().rearrange("p (l d) -> p l d", l=2), in_=src), writes=[wstg])
            for lc in range(2):
                c.op("pool", lambda lc=lc: pool.tensor_tensor(out=wob.bf()[:, lc * 1024:(lc + 1) * 1024], in0=wstg.f32()[:, lc * 1024:(lc + 1) * 1024], in1=gate_bc.f32(), op=ALU.mult),
                     reads=[wstg.sub(lc * 1024, 1024), gate_bc], writes=[wob.sub(lc * 512, 512)])
        else:
            c.dma("pool", "wo", lambda: pool.dma_start(out=wob.bf().rearrange("p (l d) -> p l d", l=2), in_=src), writes=[wob])
            for lc in range(2):
                c.op("pool", lambda lc=lc: pool.tensor_tensor(out=wob.bf()[:, lc * 1024:(lc + 1) * 1024], in0=wob.bf()[:, lc * 1024:(lc + 1) * 1024], in1=gate_bc.f32(), op=ALU.mult),
                     reads=[wob.sub(lc * 512, 512), gate_bc], writes=[wob.sub(lc * 512, 512)])

    def wout_update(l, hd, gT, gate_bc, wstg, wob, seqs_tiles, load=True):
        if load:
            wout_load(l, hd, gate_bc, wstg, wob)
        gv = gT.bf().rearrange("p (l t) -> p l t", l=2)
        Tt = gT.n
        for (tile, col) in seqs_tiles:
            for half in range(2):
                b, ps, pk = bank()
                for lc in range(2):
                    c.op("pe", lambda lc=lc, col=col, half=half: pe.matmul(ps[:, :], lhsT=gv[:, lc, col:col + 128], rhs=wob.bf()[:, lc * 1024 + half * 512:lc * 1024 + (half + 1) * 512], start=(lc == 0), stop=(lc == 1)),
                         reads=[gT.sub(lc * (Tt // 2) + col // 2, 64), wob.sub(lc * 512 + half * 256, 256)], writes=[pk])
                xr = x1t(tile).sub(half * 512, 512)
                c.op("dve", lambda xr=xr: dve.tensor_tensor(out=xr.f32(), in0=ps[:, :], in1=xr.f32(), op=ALU.add), reads=[pk, xr], writes=[xr])

    def l0_post_group(g, Ttot, seqs, fT, wz, sz, wf, gT, gate_bc, wstg, wob, tile0):
        wzv = wz.bf().rearrange("p (k n) -> p k n", k=8)
        szv = sz.bf().rearrange("p (l t) -> p l t", l=2)
        fv = fT.bf().rearrange("p (l t) -> p l t", l=2)
        gv = gT.bf().rearrange("p (l t) -> p l t", l=2)
        wfv = wf.bf().rearrange("p (l n) -> p l n", l=2)
        pcs = pieces_of(seqs)
        for jc in range(2):
            for (c0, n) in pcs:
                b, ps, pk = bank()
                for kc in range(8):
                    c.op("pe", lambda kc=kc, jc=jc, c0=c0, n=n: pe.matmul(ps[:, 0:n], lhsT=wzv[:, kc, jc * 128:(jc + 1) * 128], rhs=hTv[:, kc, c0:c0 + n], start=(kc == 0), stop=(kc == 7)),
                         reads=[wz, hTk(kc, c0, n)], writes=[pk])
                r = sz.sub(jc * (Ttot // 2) + c0 // 2, n // 2)
                c.op("act", lambda jc=jc, c0=c0, n=n: act.activation(out=szv[:, jc, c0:c0 + n], in_=ps[:, 0:n], func=AF.Silu), reads=[pk], writes=[r])
        for jc in range(2):
            for (c0, n) in pcs:
                b, ps, pk = bank()
                for lc in range(2):
                    c.op("pe", lambda lc=lc, jc=jc, c0=c0, n=n: pe.matmul(ps[:, 0:n], lhsT=wfv[:, lc, jc * 128:(jc + 1) * 128], rhs=fv[:, lc, c0:c0 + n], start=(lc == 0), stop=(lc == 1)),
                         reads=[wf, fT.sub(lc * (Ttot // 2) + c0 // 2, n // 2)], writes=[pk])
                rs_ = sz.sub(jc * (Ttot // 2) + c0 // 2, n // 2)
                rg = gT.sub(jc * (Ttot // 2) + c0 // 2, n // 2)
                bcol = bfm.f32()[:, g * 2 + jc:g * 2 + jc + 1]
                c.op("dve", lambda jc=jc, c0=c0, n=n, bcol=bcol: dve.scalar_tensor_tensor(out=gv[:, jc, c0:c0 + n], in0=ps[:, 0:n], scalar=bcol, in1=szv[:, jc, c0:c0 + n], op0=ALU.add, op1=ALU.mult),
                     reads=[pk, bfm, rs_], writes=[rg])
        tiles = [(tile0 + i, i * 128) for i in range(Ttot // 128)]
        wout_update(0, g, gT, gate_bc, wstg, wob, tiles)

    def load_tables():
        t = {}
        t["s1s"] = c.alloc(16 * 128); t["s1p"] = c.alloc(2 * 128)
        t["k"] = [c.alloc(64) for _ in range(6)]
        t["ch"] = c.alloc(2 * 2 * 256)
        c.dma("pool", "tab", lambda: pool.dma_start(out=t["s1s"].bf().rearrange("p (m n) -> p m n", m=16), in_=s1s_d.rearrange("m p n -> p m n")), writes=[t["s1s"]])
        c.dma("pool", "tab", lambda: pool.dma_start(out=t["s1p"].bf().rearrange("p (m n) -> p m n", m=2), in_=s1p_d.rearrange("m p n -> p m n")), writes=[t["s1p"]])
        for i in range(6):
            c.dma("pool", "tab", lambda i=i: pool.dma_start(out=t["k"][i].bf(), in_=kmat_d[i]), writes=[t["k"][i]])
        c.dma("pool", "tab", lambda: pool.dma_start(out=t["ch"].bf().rearrange("p (q c n) -> p q c n", q=2, c=2), in_=chtab_d.rearrange("q (c p) n -> p q c n", p=128)), writes=[t["ch"]])
        return t

    chk("setup")
    mkL0 = c.mark()
    tabs = load_tables()
    chtab = tabs["ch"]
    front_end(16, 0, 0, 0, xsrc=xs_in, possrc=pos_in, xdst=None)
    ada_layer(1)
    dump("hT_in", lambda: hT.bf(), [128, 8 * 2048], [hT], BF16)
    chk("fe_in")
    mk = c.mark()
    wus = [c.alloc(1024), c.alloc(1024)]
    u_tm = c.alloc(16 * 128); Yt = c.alloc(4096); Qt = [c.alloc(256) for _ in range(3)]
    send = [c.alloc(4096), c.alloc(4096)]
    slotsS = [(tabs["k"][0], tabs["k"][1], False), (tabs["k"][2], tabs["k"][3], True)]
    load_w_in(0, 0, wus[0], "wu0")
    for g in range(8):
        if g + 1 < 8:
            load_w_in(0, (g + 1) * 256, wus[(g + 1) % 2], f"wu{(g + 1) % 2}")
        sd = send[g % 2]
        sdv = sd.bf().rearrange("p (s l t) -> p s l t", s=2, l=2)

        def dest(si, lc, sd=sd, sdv=sdv):
            return sdv[:, si, lc, :], sd.sub((si * 2 + lc) * 1024, 1024)
        l0_pre_seq(g, wus[g % 2], 0, TS, 16, tabs["s1s"], slotsS, dest, u_tm, Yt, Qt)
        chk("pre_s2")
        c.dma("sp", f"send{g % 2}", lambda g=g, sdv=sdv: sp.dma_start(out=rs_in[g].rearrange("(s l p) t -> p s l t", s=2, l=2), in_=sdv),
              reads=[sd], writes=[("rs_in", g)])
        chk("pre_send")
        c.cc(f"rs{g}", lambda g=g: pool.collective_compute("ReduceScatter", ALU.add, replica_groups=RG, ins=[rs_in[g]], outs=[rs_out[g]]),
             reads=[("rs_in", g)], writes=[("rs_out", g)])
        chk("pre_rs")
    c.release(mk)
    chk("l0pre")
    front_end(16, 0, 0, 0, xsrc=xs_res, possrc=pos_res, xdst=x1t)
    chk("fe_res")
    mk = c.mark()
    gate_bc = c.alloc(1024); wstg = c.alloc(2048); wob = c.alloc(1024)
    wzs = [c.alloc(1024), c.alloc(1024)]
    sz = c.alloc(2048); fTs = [c.alloc(2048), c.alloc(2048)]; wf = [c.alloc(256), c.alloc(256)]; gT = c.alloc(2048)
    bcast_ada(0, 0, "gate", gate_bc)

    def load_post(g):
        load_w_in(0, DI + g * 256, wzs[g % 2], f"wz{g % 2}")
        c.dma("pool", f"wf{g % 2}", lambda: pool.dma_start(out=wf[g % 2].bf().rearrange("p (l n) -> p l n", l=2), in_=w_fmix[g].rearrange("(l p) n -> p l n", p=128)), writes=[wf[g % 2]])
        c.dma("sp", f"ft{g % 2}", lambda: sp.dma_start(out=fTs[g % 2].bf().rearrange("p (l t) -> p l t", l=2), in_=rs_out[g].rearrange("(l p) t -> p l t", p=128)),
              reads=[("rs_out", g)], writes=[fTs[g % 2]])
    load_post(0)
    for g in range(8):
        if g + 1 < 8:
            load_post(g + 1)
        l0_post_group(g, TS, [(0, TS)], fTs[g % 2], wzs[g % 2], sz, wf[g % 2], gT, gate_bc, wstg, wob, 0)
    c.release(mk)
    c.release(mkL0)
    dump("x1_l0", lambda: x1.f32(), [128, 16 * 1024], [x1])
    chk("l0post")

    def layer1(Ttot, seqs, j, is_sample, tile0):
        ntiles = Ttot // 128
        front_end(ntiles, 0, 1, j, xdst=lambda t: x1t(tile0 + t))
        mk = c.mark()
        RL = (sum(T + 4 for (_, T) in seqs) + BLK - 1) // BLK * BLK
        Rall = c.alloc(5 * RL)
        row = [Rall.sub(i * RL, RL) for i in range(5)]
        U = [row[0], row[1]]
        TA = [row[0].sub(0, Ttot), row[2].sub(0, Ttot)]
        TI = [row[1].sub(0, Ttot), row[3].sub(0, Ttot)]
        ts = row[4].sub(0, Ttot)
        h1 = c.alloc(2 * Ttot)
        gate_bc = c.alloc(1024); wob = c.alloc(1024)
        wu = c.alloc(1024); wz = c.alloc(1024)
        gw = [[c.alloc(256) for _ in range(4)] for _ in range(2)]
        ucbf = c.alloc(Ttot); sz = c.alloc(Ttot); gT = c.alloc(Ttot)
        bcast_ada(1, j, "gate", gate_bc)
        pcs = pieces_of(seqs)
        pads = []
        po = 0
        for (c0, T) in seqs:
            pads.append(po)
            po += T + 4

        if is_sample:
            wuh = [wu, wz]
            b, ps, pk = bank()
            for hd in range(8):
                load_w_in(1, hd * 256, wuh[hd % 2], f"wu{hd % 2}")
                wv = wuh[hd % 2].bf().rearrange("p (k n) -> p k n", k=8)
                for jc in range(2):
                    ch = hd * 2 + jc
                    for kc in range(8):
                        c.op("pe", lambda kc=kc, jc=jc, ch=ch, wv=wv: pe.matmul(ps[:, ch * 2:ch * 2 + 2], lhsT=wv[:, kc, jc * 128:(jc + 1) * 128], rhs=hTv[:, kc, Ttot - 2:Ttot], start=(kc == 0), stop=(kc == 7)),
                             reads=[wuh[hd % 2], hTk(kc, Ttot - 2, 2)], writes=[pk])
            c.op("dve", lambda: dve.tensor_copy(out=halo.f32(), in_=ps[:, 0:32]), reads=[pk], writes=[halo])
            c.dma("sp", "halo", lambda: sp.dma_start(out=halo_in, in_=halo.f32()), reads=[halo], writes=[("halo_in",)])
            c.cc("halocc", lambda: pool.collective_compute("AllGather", ALU.bypass, replica_groups=RG, ins=[halo_in], outs=[halo_out]),
                 reads=[("halo_in",)], writes=[("halo_out",)])
            c.dma("sp", "halo", lambda: sp.dma_start(out=hg.f32().rearrange("p (r n) -> p r n", r=2), in_=halo_out.rearrange("(r p) n -> p r n", p=128)),
                  reads=[("halo_out",)], writes=[hg])
            c.op("dve", lambda: dve.tensor_scalar(out=ctmp.f32(), in0=hg.f32()[:, 0:32], scalar1=selr.f32()[:, 0:1], scalar2=None, op0=ALU.mult), reads=[hg, selr], writes=[ctmp])
            c.op("dve", lambda: dve.scalar_tensor_tensor(out=halo.f32(), in0=hg.f32()[:, 32:64], scalar=selr.f32()[:, 1:2], in1=ctmp.f32(), op0=ALU.mult, op1=ALU.add),
                 reads=[hg, selr, ctmp], writes=[halo])

        wuv = wu.bf().rearrange("p (k n) -> p k n", k=8)
        wzv = wz.bf().rearrange("p (k n) -> p k n", k=8)
        ucv = ucbf.bf().rearrange("p (l t) -> p l t", l=2)
        szv = sz.bf().rearrange("p (l t) -> p l t", l=2)
        gv = gT.bf().rearrange("p (l t) -> p l t", l=2)
        h1v = h1.f32().rearrange("p (l t) -> p l t", l=2)

        def loads(hd):
            load_w_in(1, hd * 256, wu, "wu1h")
            load_w_in(1, DI + hd * 256, wz, "wz1h")
            gws = gw[hd % 2]
            for d in range(2):
                for gi, wsrc in enumerate((wr_d, wi_d)):
                    r = gws[d * 2 + gi]
                    c.dma("pool", f"gw{hd % 2}", lambda r=r, wsrc=wsrc, d=d: pool.dma_start(out=r.bf().rearrange("p (i n) -> p i n", i=2), in_=wsrc[d, hd].rearrange("(i p) n -> p i n", p=128)), writes=[r])

        def part_uz(hd):
            for jc in range(2):
                for si, (c0, T) in enumerate(seqs):
                    for o in range(0, T, 512):
                        n = min(512, T - o)
                        b, ps, pk = bank()
                        for kc in range(8):
                            c.op("pe", lambda kc=kc, jc=jc, c0=c0, o=o, n=n: pe.matmul(ps[:, 0:n], lhsT=wuv[:, kc, jc * 128:(jc + 1) * 128], rhs=hTv[:, kc, c0 + o:c0 + o + n], start=(kc == 0), stop=(kc == 7)),
                                 reads=[wu, hTk(kc, c0 + o, n)], writes=[pk])
                        dcol = pads[si] + 2 + o
                        evac_copy(U[jc].bf()[:, dcol:dcol + n], ps[:, 0:n], [pk], [U[jc].sub(dcol // 2, (n + 1) // 2 + 1)])
            for jc in range(2):
                for (c0, n) in pcs:
                    b, ps, pk = bank()
                    for kc in range(8):
                        c.op("pe", lambda kc=kc, jc=jc, c0=c0, n=n: pe.matmul(ps[:, 0:n], lhsT=wzv[:, kc, jc * 128:(jc + 1) * 128], rhs=hTv[:, kc, c0:c0 + n], start=(kc == 0), stop=(kc == 7)),
                             reads=[wz, hTk(kc, c0, n)], writes=[pk])
                    c.op("act", lambda jc=jc, c0=c0, n=n: act.activation(out=szv[:, jc, c0:c0 + n], in_=ps[:, 0:n], func=AF.Silu), reads=[pk], writes=[sz.sub(jc * (Ttot // 2) + c0 // 2, n // 2)])

        def part_conv(hd):
            for jc in range(2):
                ch = hd * 2 + jc
                Ub = U[jc].bf()
                for si, (c0, T) in enumerate(seqs):
                    p0 = pads[si]
                    c.op("dve", lambda Ub=Ub, p0=p0: dve.memset(Ub[:, p0:p0 + 2], 0.0), writes=[U[jc].sub(p0 // 2, 2)])
                    if not is_sample:
                        c.op("dve", lambda Ub=Ub, p0=p0, T=T: dve.memset(Ub[:, p0 + T + 2:p0 + T + 4], 0.0), writes=[U[jc].sub((p0 + T + 2) // 2, 2)])
                if is_sample:
                    T = seqs[0][1]
                    for e in range(2):
                        c.op("dve", lambda Ub=Ub, ch=ch, e=e, T=T: dve.tensor_copy(out=Ub[:, T + 2 + e:T + 3 + e], in_=halo.f32()[:, ch * 2 + 1 - e:ch * 2 + 2 - e]),
                             reads=[halo], writes=[U[jc].sub((T + 2 + e) // 2, 1)])
                plist = []
                for si, (c0, T) in enumerate(seqs):
                    for o in range(0, T, 512):
                        plist.append((pads[si] + o, c0 + o, min(512, T - o)))
                banks = [bank() for _ in plist]
                for dd in range(5):
                    dg = diagb[dd]
                    c.op("dve", lambda dg=dg, ch=ch, dd=dd: dve.tensor_scalar(out=dg.bf(), in0=identb.bf(), scalar1=convw.f32()[:, ch * 5 + dd:ch * 5 + dd + 1], scalar2=None, op0=ALU.mult),
                         reads=[identb, convw], writes=[dg])
                    for (pcol, c0o, n), (b, ps, pk) in zip(plist, banks):
                        c.op("pe", lambda dg=dg, Ub=Ub, pcol=pcol, n=n, dd=dd, ps=ps: pe.matmul(ps[:, 0:n], lhsT=dg.bf(), rhs=Ub[:, pcol + dd:pcol + dd + n], start=(dd == 0), stop=(dd == 4)),
                             reads=[dg, U[jc].sub((pcol + dd) // 2, (n + 1) // 2 + 1)], writes=[pk])
                for (pcol, c0o, n), (b, ps, pk) in zip(plist, banks):
                    c.op("act", lambda ps=ps, c0o=c0o, n=n, ch=ch, jc=jc: act.activation(out=ucv[:, jc, c0o:c0o + n], in_=ps[:, 0:n], func=AF.Identity, bias=convb.f32()[:, ch:ch + 1]),
                         reads=[pk, convb], writes=[ucbf.sub(jc * (Ttot // 2) + c0o // 2, n // 2)])

        def stage(hd, d, jc, part="all"):
            gws = gw[hd % 2]
            ta, ti = TA[jc], TI[jc]
            ch = hd * 2 + jc
            colc = slice(d * 16 + ch, d * 16 + ch + 1)
            if part in ("all", "front"):
                stage_front(hd, d, jc, gws, ta, ti, ch, colc)
            if part in ("all", "back"):
                stage_back(hd, d, jc, ta, ti, ch)

        def stage_front(hd, d, jc, gws, ta, ti, ch, colc):
            for gi, (dst, bcol) in enumerate(((ta, brc), (ti, bic))):
                wv = gws[d * 2 + gi].bf().rearrange("p (i n) -> p i n", i=2)
                for (c0, n) in pcs:
                    b, ps, pk = bank()
                    for ic in range(2):
                        c.op("pe", lambda ic=ic, c0=c0, n=n, wv=wv: pe.matmul(ps[:, 0:n], lhsT=wv[:, ic, jc * 128:(jc + 1) * 128], rhs=ucv[:, ic, c0:c0 + n], start=(ic == 0), stop=(ic == 1)),
                             reads=[gws[d * 2 + gi], ucbf.sub(ic * (Ttot // 2) + c0 // 2, n // 2)], writes=[pk])
                    c.op("act", lambda c0=c0, n=n, dst=dst, bcol=bcol: act.activation(out=dst.f32()[:, c0:c0 + n], in_=ps[:, 0:n], func=AF.Tanh, scale=0.5, bias=bcol.f32()[:, d * 16 + ch:d * 16 + ch + 1]),
                         reads=[pk, bcol], writes=[dst.sub(c0, n)])
            c.op("act", lambda: act.activation(out=ts.f32(), in_=ta.f32(), func=AF.Exp, scale=cl.f32()[:, colc], bias=cl.f32()[:, colc]), reads=[ta, cl], writes=[ts])
            c.op("act", lambda: act.activation(out=ta.f32(), in_=ta.f32(), func=AF.Exp, scale=cl2.f32()[:, colc], bias=cl2.f32()[:, colc]), reads=[ta, cl2], writes=[ta])
            c.op("act", lambda: act.activation(out=ts.f32(), in_=ts.f32(), func=AF.Sqrt, scale=-1.0, bias=1.0), reads=[ts], writes=[ts])

        def stage_back(hd, d, jc, ta, ti, ch):
            c.op("dve", lambda: dve.scalar_tensor_tensor(out=ti.f32(), in0=ti.f32(), scalar=1.0, in1=ts.f32(), op0=ALU.add, op1=ALU.mult), reads=[ti, ts], writes=[ti])
            c.op("dve", lambda: dve.scalar_tensor_tensor(out=ti.f32(), in0=ti.f32(), scalar=0.5, in1=ucv[:, jc, :], op0=ALU.mult, op1=ALU.mult), reads=[ti, ucbf.sub(jc * (Ttot // 2), Ttot // 2)], writes=[ti])
            if d == 0:
                for si, (c0, T) in enumerate(seqs):
                    init = st0.f32()[:, ch:ch + 1] if is_sample else zcol.f32()
                    c.op("dve", lambda c0=c0, T=T, init=init: dve.tensor_tensor_scan(out=h1v[:, jc, c0:c0 + T], data0=ta.f32()[:, c0:c0 + T], data1=ti.f32()[:, c0:c0 + T], initial=init, op0=ALU.mult, op1=ALU.add),
                         reads=[ta, ti, st0, zcol], writes=[h1.sub(jc * Ttot + c0, T)])
                    if is_sample:
                        c.op("dve", lambda c0=c0, T=T: dve.tensor_copy(out=c1.f32()[:, jc:jc + 1], in_=h1v[:, jc, c0 + T - 1:c0 + T]), reads=[h1.sub(jc * Ttot + c0 + T - 1, 1)], writes=[c1])
                    else:
                        col = si * 32 + ch
                        c.op("dve", lambda c0=c0, T=T, col=col: dve.tensor_copy(out=stpo.f32()[:, col:col + 1], in_=h1v[:, jc, c0 + T - 1:c0 + T]), reads=[h1.sub(jc * Ttot + c0 + T - 1, 1)], writes=[stpo.sub(col, 1)])
            else:
                if is_sample and jc == 0:
                    c.op("dve", lambda: dve.tensor_scalar(out=ctmp.f32()[:, 0:2], in0=cg.f32()[:, 0:2], scalar1=selr.f32()[:, 0:1], scalar2=None, op0=ALU.mult), reads=[cg, selr], writes=[ctmp])
                    c.op("dve", lambda: dve.scalar_tensor_tensor(out=c2.f32(), in0=cg.f32()[:, 2:4], scalar=selr.f32()[:, 1:2], in1=ctmp.f32()[:, 0:2], op0=ALU.mult, op1=ALU.add),
                         reads=[cg, selr, ctmp], writes=[c2])
                for si, (c0, T) in enumerate(seqs):
                    init = c2.f32()[:, jc:jc + 1] if is_sample else zcol.f32()
                    c.op("dve", lambda c0=c0, T=T, init=init: dve.tensor_tensor_scan(out=ti.f32()[:, c0:c0 + T][:, ::-1], data0=ta.f32()[:, c0:c0 + T][:, ::-1], data1=ti.f32()[:, c0:c0 + T][:, ::-1], initial=init, op0=ALU.mult, op1=ALU.add),
                         reads=[ta, ti, c2, zcol], writes=[ti])
                    if not is_sample:
                        col = si * 32 + 16 + ch
                        c.op("dve", lambda c0=c0, col=col: dve.tensor_copy(out=stpo.f32()[:, col:col + 1], in_=ti.f32()[:, c0:c0 + 1]), reads=[ti], writes=[stpo.sub(col, 1)])
                c.op("dve", lambda: dve.tensor_tensor(out=ti.f32(), in0=ti.f32(), in1=h1v[:, jc, :], op=ALU.add), reads=[ti, h1.sub(jc * Ttot, Ttot)], writes=[ti])
                c.op("dve", lambda: dve.tensor_tensor(out=gv[:, jc, :], in0=ti.f32(), in1=szv[:, jc, :], op=ALU.mult), reads=[ti, sz.sub(jc * (Ttot // 2), Ttot // 2)], writes=[gT.sub(jc * (Ttot // 2), Ttot // 2)])

        def exchange_send(hd):
            c.dma("sp", "car", lambda: sp.dma_start(out=car_in[hd], in_=c1.f32()), reads=[c1], writes=[("car_in", hd)])
            c.cc(f"carcc{hd}", lambda: pool.collective_compute("AllGather", ALU.bypass, replica_groups=RG, ins=[car_in[hd]], outs=[car_out[hd]]),
                 reads=[("car_in", hd)], writes=[("car_out", hd)])
            c.dma("sp", "car", lambda: sp.dma_start(out=cg.f32().rearrange("p (r n) -> p r n", r=2), in_=car_out[hd].rearrange("(r p) n -> p r n", p=128)),
                  reads=[("car_out", hd)], writes=[cg])

        tiles = [(tile0 + i, i * 128) for i in range(ntiles)]
        loads(0)
        part_uz(0)
        part_conv(0)
        for hd in range(8):
            stage(hd, 0, 0, "front")
            if hd > 0:
                wout_update(1, hd - 1, gT, gate_bc, None, wob, tiles, load=False)
            stage(hd, 0, 0, "back")
            stage(hd, 0, 1)
            if is_sample:
                exchange_send(hd)
            wout_load(1, hd, gate_bc, None, wob)
            stage(hd, 1, 0)
            if hd + 1 < 8:
                loads(hd + 1)
            stage(hd, 1, 1)
            if hd + 1 < 8:
                part_uz(hd + 1)
                part_conv(hd + 1)
        wout_update(1, 7, gT, gate_bc, None, wob, tiles, load=False)
        c.release(mk)

    def final_out(ntiles, tile0, dst):
        mk = c.mark()
        fg_bc = c.alloc(1024)
        tmp = [c.alloc(1024), c.alloc(1024)]
        c.dma("sp", "bc", lambda: sp.dma_start(out=fg_bc.f32(), in_=fgr_d[0, :].partition_broadcast(128)), writes=[fg_bc])
        for t in range(ntiles):
            s = t % 2
            xr = x1t(tile0 + t)
            ssr = ss.sub(t, 1); rsr = rstd.sub(t, 1)
            c.op("act", lambda xr=xr, s=s, ssr=ssr: act.activation(out=tmp[s].f32(), in_=xr.f32(), func=AF.Square, accum_out=ssr.f32()), reads=[xr], writes=[tmp[s], ssr])
            c.op("act", lambda ssr=ssr, rsr=rsr: act.activation(out=rsr.f32(), in_=ssr.f32(), func=AF.Sqrt, scale=1.0 / D, bias=EPS), reads=[ssr], writes=[rsr])
            c.op("dve", lambda rsr=rsr: dve.reciprocal(out=rsr.f32(), in_=rsr.f32()), reads=[rsr], writes=[rsr])
            c.op("dve", lambda xr=xr, s=s, rsr=rsr: dve.scalar_tensor_tensor(out=tmp[s].f32(), in0=xr.f32(), scalar=rsr.f32(), in1=fg_bc.f32(), op0=ALU.mult, op1=ALU.mult),
                 reads=[xr, rsr, fg_bc], writes=[tmp[s]])
            c.dma("sp", f"out{s}", lambda s=s, t=t: sp.dma_start(out=dst[t * 128:(t + 1) * 128, :], in_=tmp[s].f32()), reads=[tmp[s]], writes=[("dout", id(dst), t)])
        c.release(mk)

    layer1(TS, [(0, TS)], 0, True, 0)
    dump("x1_l1", lambda: x1.f32(), [128, 16 * 1024], [x1])
    chk("l1s")
    final_out(16, 0, ys)
    chk("sample")

    mkP = c.mark()
    tabs = load_tables()
    chtab = tabs["ch"]
    front_end(4, 0, 0, 1, xsrc=xp, possrc=None, xdst=x1t)
    mk = c.mark()
    TT = 2 * TP
    wus = [c.alloc(1024), c.alloc(1024)]
    u_tm = c.alloc(2 * 128); Yt = c.alloc(512); Qt = [c.alloc(256) for _ in range(3)]
    gate_bc = c.alloc(1024); wstg = c.alloc(2048); wob = c.alloc(1024)
    wzs = [c.alloc(1024), c.alloc(1024)]
    sz = c.alloc(TT); fT = c.alloc(TT); wf = [c.alloc(256), c.alloc(256)]; gT = c.alloc(TT)
    bcast_ada(0, 1, "gate", gate_bc)
    slotsP = [(tabs["k"][4], tabs["k"][5], False)]
    fv = fT.bf().rearrange("p (l t) -> p l t", l=2)
    for g in range(8):
        load_w_in(0, g * 256, wus[g % 2], f"wu{g % 2}")
        load_w_in(0, DI + g * 256, wzs[g % 2], f"wz{g % 2}")
        c.dma("pool", f"wf{g % 2}", lambda g=g: pool.dma_start(out=wf[g % 2].bf().rearrange("p (l n) -> p l n", l=2), in_=w_fmix[g].rearrange("(l p) n -> p l n", p=128)), writes=[wf[g % 2]])
        for q in range(2):
            def dest(si, lc, q=q):
                return fv[:, lc, q * TP:(q + 1) * TP], fT.sub(lc * (TT // 2) + q * TP // 2, TP // 2)
            l0_pre_seq(g, wus[g % 2], q * TP, TP, 2, tabs["s1p"], slotsP, dest, u_tm, Yt, Qt)
        l0_post_group(g, TT, [(0, TP), (TP, TP)], fT, wzs[g % 2], sz, wf[g % 2], gT, gate_bc, wstg, wob, 0)
    c.release(mk)
    c.release(mkP)
    layer1(TT, [(0, TP), (TP, TP)], 1, False, 0)
    c.dma("sp", "stpo", lambda: sp.dma_start(out=stp, in_=stpo.f32()), reads=[stpo], writes=[("stp",)])
    final_out(4, 0, yp)


def build_nc():
    nc0 = bass.Bass("TRN2", target_bir_lowering=False)
    with ExitStack() as st:
        c0 = Ctx(nc0, st, needed=None)
        _emit(nc0, c0)
        needed = c0.need_out
    nc = bass.Bass("TRN2", target_bir_lowering=False)
    with ExitStack() as st:
        c = Ctx(nc, st, needed=needed)
        _emit(nc, c)
        print(f"[kernel] insts={c.ninst} waits={c.nwaits} incs={ {e: len(needed[e]) for e in needed} } arena_top={c.top}")
    return nc


def _col(v, nchunks):
    return np.ascontiguousarray(np.asarray(v, np.float32).reshape(nchunks, 128).T)


_NC_CACHE = {}
_RETURN_MAPS = False


def kernel(x_prompt, x_sample, state_lru_1, c, c_ctx,
           norm_g_0, w_ada_0, b_ada_0, w_in_0, w_fmix_0, b_fmix_0, w_out_0,
           norm_g_1, w_ada_1, b_ada_1, w_in_1, conv_w_1, conv_b_1,
           w_rgate_1, b_rgate_1, w_igate_1, b_igate_1, lam_1, w_out_1,
           final_g):
    f = lambda a: np.ascontiguousarray(np.asarray(a, np.float32))
    x_prompt, x_sample, state_lru_1, c, c_ctx = map(f, (x_prompt, x_sample, state_lru_1, c, c_ctx))
    pos = pos_embed(4096, D)
    ident = np.eye(128, dtype=np.float32)
    cht = ch_tables().astype(np.float32)
    tabs_s = [dft_tables_sample(s) for s in (0, 1)]
    tabs_p = [dft_tables_prompt(bool(s)) for s in (0, 1)]
    common = {
        "w_ada_0": f(w_ada_0), "w_ada_1": f(w_ada_1), "badar_0": f(b_ada_0).reshape(1, 3072), "badar_1": f(b_ada_1).reshape(1, 3072),
        "ngr_0": f(norm_g_0).reshape(1, 1024), "ngr_1": f(norm_g_1).reshape(1, 1024), "fgr": f(final_g).reshape(1, 1024),
        "w_in_0": f(w_in_0), "w_in_1": f(w_in_1), "w_fmix": f(w_fmix_0), "bfm": _col(b_fmix_0, 16),
        "w_out_0": f(w_out_0), "w_out_1": f(w_out_1), "convb": _col(conv_b_1, 16),
        "ident": ident, "chtab": cht,
    }
    cw = f(conv_w_1)
    zero = np.zeros_like(cw[0])
    in_maps = []
    for cid in range(NCORES):
        p, s = cid // 2, cid % 2
        m = dict(common)
        xs = x_sample[p]
        m["xs_in"] = np.ascontiguousarray(xs[s::2]); m["pos_in"] = np.ascontiguousarray(pos[s::2])
        if s == 0:
            m["xs_res"] = np.ascontiguousarray(xs[:TS]); m["pos_res"] = np.ascontiguousarray(pos[:TS])
        else:
            m["xs_res"] = np.ascontiguousarray(xs[::-1][:TS]); m["pos_res"] = np.ascontiguousarray(pos[::-1][:TS])
        xpp = x_prompt[2 * cid:2 * cid + 2]
        if s == 1:
            xpp = xpp[:, ::-1]
        m["xp"] = np.ascontiguousarray(xpp.reshape(2 * TP, D))
        condT = np.zeros((128, 16), np.float32)
        condT[:, 0::2] = _col(c[p], 8); condT[:, 1::2] = _col(c_ctx, 8)
        m["condT"] = condT
        sel = np.zeros((128, 2), np.float32); sel[:, 1 - s] = 1.0
        m["sel"] = sel
        m["st0"] = _col(state_lru_1[p, s], 16)
        taps = [cw[0], cw[1], cw[2], cw[3], zero] if s == 0 else [zero, cw[3], cw[2], cw[1], cw[0]]
        m["convw"] = np.ascontiguousarray(np.stack([_col(t, 16) for t in taps], axis=-1).reshape(128, 80))
        d1, d2 = (0, 1) if s == 0 else (1, 0)
        m["wr"] = np.ascontiguousarray(f(w_rgate_1)[[d1, d2]]); m["wi"] = np.ascontiguousarray(f(w_igate_1)[[d1, d2]])
        m["br"] = np.concatenate([_col(b_rgate_1[d1], 16), _col(b_rgate_1[d2], 16)], 1)
        m["bi"] = np.concatenate([_col(b_igate_1[d1], 16), _col(b_igate_1[d2], 16)], 1)
        m["lam"] = np.concatenate([_col(lam_1[d1], 16), _col(lam_1[d2], 16)], 1)
        s1, KrA, KiA, KrB, KiB = tabs_s[s]
        s1p, KrP, KiP = tabs_p[s]
        m["s1s"] = s1.astype(np.float32); m["s1p"] = s1p.astype(np.float32)
        m["kmat"] = np.stack([KrA, KiA, KrB, KiB, KrP, KiP]).astype(np.float32)
        in_maps.append({k: np.ascontiguousarray(v, dtype=np.float32) for k, v in m.items()})

    if _RETURN_MAPS:
        return in_maps
    if "nc" not in _NC_CACHE:
        _NC_CACHE["nc"] = build_nc()
    res = run_bass_kernel_spmd(_NC_CACHE["nc"], in_maps, core_ids=list(range(NCORES)))
    outs = res.results

    y_prompt = np.zeros((16, TP, D), np.float32)
    y_sample = np.zeros((4, 4096, D), np.float32)
    new_state = np.zeros((16, 2, DI), np.float32)
    for cid in range(NCORES):
        p, s = cid // 2, cid % 2
        ysv = np.asarray(outs[cid]["ys"], np.float32)
        ypv = np.asarray(outs[cid]["yp"], np.float32).reshape(2, TP, D)
        stv = np.asarray(outs[cid]["stp"], np.float32).reshape(128, 2, 2, 16)
        stv = stv.transpose(1, 2, 3, 0).reshape(2, 2, DI)
        if s == 0:
            y_sample[p, :TS] = ysv
            y_prompt[2 * cid:2 * cid + 2] = ypv
            new_state[2 * cid:2 * cid + 2] = stv
        else:
            y_sample[p, TS:] = ysv[::-1]
            y_prompt[2 * cid:2 * cid + 2] = ypv[:, ::-1]
            new_state[2 * cid:2 * cid + 2] = stv[:, ::-1]
    return (y_prompt, y_sample, new_state)
```

```python
import numpy as np
from contextlib import ExitStack
import concourse.bass as bass
import concourse.mybir as mybir
from concourse.bass_utils import run_bass_kernel_spmd

F32 = mybir.dt.float32
BF16 = mybir.dt.bfloat16
AF = mybir.ActivationFunctionType
ALU = mybir.AluOpType

D = 1024
DI = 2048
EPS = 1e-6
GRID_W = 64
POS_BASE = 10000.0
NCORES = 8
TS = 2048
TP = 256
NW = 53100
BLK = 64


def pos_embed(n_tokens, dim):
    rows = n_tokens // GRID_W
    r = np.broadcast_to(np.arange(rows, dtype=np.float32)[:, None], (rows, GRID_W)).reshape(-1)
    col = np.broadcast_to(np.arange(GRID_W, dtype=np.float32)[None, :], (rows, GRID_W)).reshape(-1)
    quarter = dim // 4
    omega = (1.0 / (np.float32(POS_BASE) ** (np.arange(quarter, dtype=np.float32) / np.float32(quarter)))).astype(np.float32)
    er = (r[:, None] * omega).astype(np.float32)
    ec = (col[:, None] * omega).astype(np.float32)
    return np.concatenate([np.sin(er), np.cos(er), np.sin(ec), np.cos(ec)], axis=-1).astype(np.float32)


def s1_table(S, N2, alpha, beta, gamma, delta):
    m1 = np.arange(128)[:, None]
    k1 = np.arange(128)[None, :]
    out = np.zeros((N2, 128, 256), np.float64)
    for m2 in range(N2):
        m = N2 * m1 + m2
        ph = -2 * np.pi * (((alpha * m + beta) * (gamma * k1 + delta)) % S) / S
        out[m2, :, :128] = np.cos(ph)
        out[m2, :, 128:] = np.sin(ph)
    return out / np.sqrt(128.0)


def s2_kron(S2c, N2, flip):
    NI = 128 // N2
    Kr = np.zeros((128, 128))
    Ki = np.zeros((128, 128))
    for m2 in range(N2):
        for k2 in range(N2):
            for i in range(NI):
                ip = (NI - 1 - i) if flip else i
                Kr[m2 * NI + i, k2 * NI + ip] = S2c[m2, k2].real
                Ki[m2 * NI + i, k2 * NI + ip] = -S2c[m2, k2].imag
    return Kr, Ki


def dft_tables_prompt(rev):
    S = 256
    N2 = 2
    if rev:
        alpha, beta, gamma, delta = -1, 255, -1, 255
    else:
        alpha, beta, gamma, delta = 1, 0, 1, 0
    s1 = s1_table(S, N2, alpha, beta, gamma, delta)
    m2 = np.arange(N2)[:, None]
    k2 = np.arange(N2)[None, :]
    S2c = np.exp(-2j * np.pi * ((alpha * gamma * 128 * m2 * k2 + beta * gamma * 128 * k2) % S) / S) / np.sqrt(S / 128.0)
    Kr, Ki = s2_kron(S2c, N2, False)
    return s1, Kr, Ki


def dft_tables_sample(sig):
    S = 4096
    N2 = 16
    s1 = s1_table(S, N2, 2, sig, 1, 0)
    m2 = np.arange(N2)[:, None]
    k2 = np.arange(N2)[None, :]
    nrm = np.sqrt(S / 128.0)
    S2A = np.exp(-2j * np.pi * (((2 * m2 + sig) * k2) % 32) / 32) / nrm
    S2B = np.exp(-2j * np.pi * (((2 * m2 + sig) * (31 - k2)) % 32) / 32) / nrm
    KrA, KiA = s2_kron(S2A, N2, False)
    KrB, KiB = s2_kron(S2B, N2, True)
    return s1, KrA, KiA, KrB, KiB


def ch_tables():
    c = np.arange(256)[:, None]
    l = np.arange(256)[None, :]
    ph = 2 * np.pi * ((c * l) % 256) / 256
    C = np.cos(ph) / 16.0
    S_ = np.sin(ph) / 16.0
    return np.stack([np.concatenate([C, -S_], 1), np.concatenate([S_, C], 1)], 0)


class Region:
    def __init__(self, ctx, off, n):
        self.ctx, self.off, self.n = ctx, off, n

    def f32(self):
        return self.ctx.arena[:, self.off:self.off + self.n]

    def bf(self):
        return self.ctx.arena[:, self.off:self.off + self.n].bitcast(BF16)

    def sub(self, lo, n):
        assert lo >= 0 and lo + n <= self.n, (lo, n, self.n)
        return Region(self.ctx, self.off + lo, n)

    def keys(self):
        return [("A", b) for b in range(self.off // BLK, (self.off + self.n - 1) // BLK + 1)]


def K(*items):
    out = []
    for it in items:
        if isinstance(it, Region):
            out.extend(it.keys())
        elif isinstance(it, (list, tuple)) and len(it) > 0 and isinstance(it[0], (Region, list)):
            out.extend(K(*it))
        elif isinstance(it, list):
            out.extend(it)
        else:
            out.append(it)
    return out


class Ctx:
    COMPUTE = ("pe", "act", "dve", "pool")

    def __init__(self, nc, stack, needed=None):
        self.nc = nc
        self.stack = stack
        self.dry = needed is None
        self.needed = needed
        self.need_out = {e: set() for e in self.COMPUTE}
        self.eng = {"pe": nc.tensor, "act": nc.scalar, "dve": nc.vector, "pool": nc.gpsimd, "sp": nc.sync}
        self.csem = {e: stack.enter_context(nc.semaphore("c_" + e)) for e in self.COMPUTE}
        self.ccnt = {e: 0 for e in self.COMPUTE}
        self.rank = None
        if needed is not None:
            self.rank = {}
            for e in self.COMPUTE:
                self.rank[e] = {idx: i + 1 for i, idx in enumerate(sorted(needed[e]))}
        self.dsem = {}
        self.dcnt = {}
        self.waited = {e: {} for e in self.eng}
        self.lastw = {}
        self.readers = {}
        self.nwaits = 0
        self.ninst = {e: 0 for e in self.eng}
        self.arena = stack.enter_context(nc.sbuf_tensor("arena", [128, NW], F32))
        self.top = 0
        self.bankrr = 0
        self.drrq = {}
        self.ptrr = 0

    def alloc(self, n):
        n_al = (n + BLK - 1) // BLK * BLK
        r = Region(self, self.top, n)
        self.top += n_al
        assert self.top <= NW, f"arena overflow {self.top} > {NW}"
        return r

    def mark(self):
        return self.top

    def release(self, m):
        self.top = m

    def _deps(self, e, reads, writes):
        deps = {}

        def add(t):
            if t is None:
                return
            s, v, src = t
            if src == "pe" and e == "pe":
                return
            if s not in deps or deps[s] < v:
                deps[s] = v
        for k in reads:
            add(self.lastw.get(k))
        for k in writes:
            add(self.lastw.get(k))
            for t in self.readers.get(k, ()):
                add(t)
        w = self.waited[e]
        for s, v in deps.items():
            if w.get(s, 0) >= v:
                continue
            w[s] = v
            self.nwaits += 1
            if s in self.COMPUTE:
                self.need_out[s].add(v)
                if not self.dry:
                    self.eng[e].wait_ge(self.csem[s], self.rank[s][v])
            else:
                if not self.dry:
                    self.eng[e].wait_ge(self.dsem[s], v)

    def _record(self, t, reads, writes):
        for k in writes:
            self.lastw[k] = t
            self.readers[k] = []
        for k in reads:
            self.readers.setdefault(k, []).append(t)

    def op(self, e, fn, reads=(), writes=()):
        reads = K(*reads)
        writes = K(*writes)
        self._deps(e, reads, writes)
        self.ccnt[e] += 1
        self.ninst[e] += 1
        idx = self.ccnt[e]
        if not self.dry:
            inst = fn()
            if idx in self.rank[e]:
                inst.then_inc(self.csem[e], 1)
        self._record((e, idx, e), reads, writes)

    NPOOL = 28

    def _dsem(self, name):
        if name not in self.dsem:
            self.dsem[name] = self.stack.enter_context(self.nc.semaphore("d_" + name))
            self.dcnt[name] = 0

    def dma(self, q, semname, fn, reads=(), writes=()):
        reads = K(*reads)
        writes = K(*writes)
        self._deps(q, reads, writes)
        rr = self.drrq.setdefault(q, 0)
        self.drrq[q] = rr + 1
        name = "%s%d" % (q, rr % self.NPOOL)
        self._dsem(name)
        prev = self.dcnt[name]
        if prev > 0 and self.waited[q].get(name, 0) < prev:
            self.waited[q][name] = prev
            self.nwaits += 1
            if not self.dry:
                self.eng[q].wait_ge(self.dsem[name], prev)
        self.dcnt[name] += 16
        self.ninst[q] += 1
        if not self.dry:
            fn().then_inc(self.dsem[name], 16)
        self._record((name, self.dcnt[name], "dma"), reads, writes)

    def cc(self, semname, fn, reads=(), writes=()):
        reads = K(*reads)
        writes = K(*writes)
        self._deps("pool", reads, writes)
        self._dsem(semname)
        self.dcnt[semname] += 1
        if not self.dry:
            fn().then_inc(self.dsem[semname], 1)
        self._record((semname, self.dcnt[semname], "cc"), reads, writes)

    def finish(self, e="sp"):
        allk = list(set(self.lastw) | set(self.readers))
        self._deps(e, allk, allk)


STOP = None
DEBUG = ()


class _Stop(Exception):
    pass


def _emit(nc, c):
    try:
        _emit_body(nc, c)
    except _Stop:
        pass
    c.finish("sp")


def _emit_body(nc, c):
    def chk(name):
        if STOP == name:
            raise _Stop()

    def dump(name, ap_fn, shape, reads, dt=F32):
        if name not in DEBUG:
            return
        d = nc.dram_tensor("dbg_" + name, list(shape), dt, kind="ExternalOutput").ap()
        c.dma("sp", "dbg_" + name, lambda: nc.sync.dma_start(out=d, in_=ap_fn()), reads=reads, writes=[("dbg", name)])
    eng = c.eng
    pe, act, dve, pool, sp = nc.tensor, nc.scalar, nc.vector, nc.gpsimd, nc.sync

    def din(name, shape, dt=F32):
        return nc.dram_tensor(name, list(shape), dt, kind="ExternalInput").ap()

    def dout(name, shape, dt=F32):
        return nc.dram_tensor(name, list(shape), dt, kind="ExternalOutput").ap()

    def dint(name, shape, dt):
        return nc.dram_tensor(name, list(shape), dt, kind="Internal").ap()

    xs_in = din("xs_in", [TS, D]); pos_in = din("pos_in", [TS, D])
    xs_res = din("xs_res", [TS, D]); pos_res = din("pos_res", [TS, D])
    xp = din("xp", [2 * TP, D])
    condT = din("condT", [128, 16]); sel_d = din("sel", [128, 2]); st0_d = din("st0", [128, 16])
    w_ada = [din("w_ada_0", [D, 3 * D]), din("w_ada_1", [D, 3 * D])]
    badar_d = [din("badar_0", [1, 3072]), din("badar_1", [1, 3072])]
    ngr_d = [din("ngr_0", [1, 1024]), din("ngr_1", [1, 1024])]
    fgr_d = din("fgr", [1, 1024])
    w_in = [din("w_in_0", [D, 2 * DI]), din("w_in_1", [D, 2 * DI])]
    w_fmix = din("w_fmix", [8, 256, 256]); bfm_d = din("bfm", [128, 16])
    w_out = [din("w_out_0", [DI, D]), din("w_out_1", [DI, D])]
    convw_d = din("convw", [128, 80]); convb_d = din("convb", [128, 16])
    wr_d = din("wr", [2, 8, 256, 256]); wi_d = din("wi", [2, 8, 256, 256])
    br_d = din("br", [128, 32]); bi_d = din("bi", [128, 32]); lam_d = din("lam", [128, 32])
    ident_d = din("ident", [128, 128])
    s1s_d = din("s1s", [16, 128, 256]); s1p_d = din("s1p", [2, 128, 256])
    kmat_d = din("kmat", [6, 128, 128]); chtab_d = din("chtab", [2, 256, 512])
    ys = dout("ys", [TS, D]); yp = dout("yp", [2 * TP, D]); stp = dout("stp", [128, 64])
    rs_in = [dint(f"rs_in{g}", [512, TS], BF16) for g in range(8)]
    rs_out = [dint(f"rs_out{g}", [256, TS], BF16) for g in range(8)]
    halo_in = dint("halo_in", [128, 32], F32); halo_out = dint("halo_out", [256, 32], F32)
    car_in = [dint(f"car_in{h}", [128, 2], F32) for h in range(8)]
    car_out = [dint(f"car_out{h}", [256, 2], F32) for h in range(8)]
    RG = [[0, 1], [2, 3], [4, 5], [6, 7]]

    psb = [c.stack.enter_context(nc.psum_tensor(f"ps{i}", [128, 512], F32)) for i in range(6)]
    ptb = [c.stack.enter_context(nc.psum_tensor(f"pt{i}", [128, 1024], BF16)) for i in range(2)]

    def bank():
        b = c.bankrr
        c.bankrr = (b + 1) % 6
        return b, psb[b], ("ps", b)

    def tbank():
        b = c.ptrr
        c.ptrr = (b + 1) % 2
        return ptb[b], ("pt", b)

    evac_rr = [0]

    def evac_copy(out_ap, in_ap, reads, writes, flip=True):
        if flip:
            evac_rr[0] ^= 1
        if evac_rr[0]:
            c.op("act", lambda: act.copy(out=out_ap, in_=in_ap), reads, writes)
        else:
            c.op("dve", lambda: dve.tensor_copy(out=out_ap, in_=in_ap), reads, writes)

    x1 = c.alloc(16 * 1024)
    hT = c.alloc(8 * 1024)
    smalls = c.alloc(1280)
    so = [0]

    def S(n):
        r = smalls.sub(so[0], n)
        so[0] += n
        return r
    identb = S(64)
    ss = S(32); rstd = S(32)
    scond = S(16); selr = S(2); st0 = S(16); zcol = S(1)
    bfm = S(16); convw = S(80); convb = S(16)
    brc = S(32); bic = S(32); cl = S(32); cl2 = S(32)
    halo = S(32); hg = S(64); c1 = S(2); c2 = S(2); cg = S(4); ctmp = S(32)
    stpo = S(64)
    diagb = [S(64) for _ in range(5)]
    hTv = hT.bf().rearrange("p (k t) -> p k t", k=8)
    x1v = x1.f32().rearrange("p (t d) -> p t d", t=16)

    def x1t(t):
        return x1.sub(t * 1024, 1024)

    def hTk(kc, c0, n):
        return hT.sub(kc * 1024 + c0 // 2, (n + 1) // 2)

    def hT_cols(c0, n):
        return [hTk(kc, c0, n) for kc in range(8)]

    def ld(r, src, sem="small"):
        c.dma("sp", sem, lambda: sp.dma_start(out=r.f32(), in_=src), writes=[r])
    ld(scond, condT); ld(selr, sel_d); ld(st0, st0_d)
    ld(bfm, bfm_d); ld(convw, convw_d); ld(convb, convb_d)
    ld(brc, br_d); ld(bic, bi_d); ld(cl, lam_d)
    c.dma("pool", "identb", lambda: pool.dma_start(out=identb.bf(), in_=ident_d), writes=[identb])
    c.op("dve", lambda: dve.memset(zcol.f32(), 0.0), writes=[zcol])
    c.op("act", lambda: act.activation(out=scond.f32(), in_=scond.f32(), func=AF.Silu), reads=[scond], writes=[scond])
    c.op("act", lambda: act.activation(out=cl.f32(), in_=cl.f32(), func=AF.Exp, scale=-1.0), reads=[cl], writes=[cl])
    c.op("act", lambda: act.activation(out=cl.f32(), in_=cl.f32(), func=AF.Ln, bias=1.0), reads=[cl], writes=[cl])
    c.op("dve", lambda: dve.tensor_scalar(out=cl2.f32(), in0=cl.f32(), scalar1=-4.0, scalar2=None, op0=ALU.mult), reads=[cl], writes=[cl2])
    c.op("dve", lambda: dve.tensor_scalar(out=cl.f32(), in0=cl.f32(), scalar1=-8.0, scalar2=None, op0=ALU.mult), reads=[cl], writes=[cl])
    c.op("dve", lambda: dve.tensor_scalar(out=brc.f32(), in0=brc.f32(), scalar1=0.5, scalar2=None, op0=ALU.mult), reads=[brc], writes=[brc])
    c.op("dve", lambda: dve.tensor_scalar(out=bic.f32(), in0=bic.f32(), scalar1=0.5, scalar2=None, op0=ALU.mult), reads=[bic], writes=[bic])

    ada_dram = [dint(f"ada_rows{l}", [2, 3072], F32) for l in range(2)]

    def ada_layer(l):
        mk = c.mark()
        NS = 3
        wst = [c.alloc(3072) for _ in range(NS)]
        arow = c.alloc(3072); brow = c.alloc(3072); grow = c.alloc(1024)
        c.dma("sp", "brow", lambda: sp.dma_start(out=brow.f32()[0:2, :], in_=badar_d[l][0, :].partition_broadcast(2)), writes=[brow])
        c.dma("sp", "grow", lambda: sp.dma_start(out=grow.f32()[0:2, :], in_=ngr_d[l][0, :].partition_broadcast(2)), writes=[grow])
        for kc in range(8):
            r = wst[kc % NS]
            c.dma("sp", "wada", lambda kc=kc, r=r: sp.dma_start(out=r.f32(), in_=w_ada[l][kc * 128:(kc + 1) * 128, :]), writes=[r])
            for n in range(6):
                c.op("pe", lambda kc=kc, n=n, r=r: pe.matmul(psb[n][0:2, :], lhsT=scond.f32()[:, kc * 2:kc * 2 + 2], rhs=r.f32()[:, n * 512:(n + 1) * 512], start=(kc == 0), stop=(kc == 7)),
                     reads=[r, scond], writes=[("ps", n)])
        for n in range(6):
            c.op("dve", lambda n=n: dve.tensor_tensor(out=arow.f32()[0:2, n * 512:(n + 1) * 512], in0=psb[n][0:2, :], in1=brow.f32()[0:2, n * 512:(n + 1) * 512], op=ALU.add),
                 reads=[("ps", n), brow], writes=[arow.sub(n * 512, 512)])
        c.op("dve", lambda: dve.scalar_tensor_tensor(out=arow.f32()[0:2, 1024:2048], in0=arow.f32()[0:2, 1024:2048], scalar=1.0, in1=grow.f32()[0:2, :], op0=ALU.add, op1=ALU.mult),
             reads=[arow, grow], writes=[arow.sub(1024, 1024)])
        c.dma("sp", "adaout", lambda: sp.dma_start(out=ada_dram[l], in_=arow.f32()[0:2, :]), reads=[arow], writes=[("ada", l)])
        c.release(mk)

    WHICH = {"shift": 0, "m": 1, "gate": 2}

    def bcast_ada(l, j, which, dst):
        w = WHICH[which]
        c.dma("sp", "bc", lambda: sp.dma_start(out=dst.f32(), in_=ada_dram[l][j, w * 1024:(w + 1) * 1024].partition_broadcast(128)), reads=[("ada", l)], writes=[dst])

    ada_layer(0)

    def front_end(ntiles, col0, l, j, xsrc=None, possrc=None, xdst=None, order=None, after_first=None):
        mk = c.mark()
        m_bc = c.alloc(1024); sh_bc = c.alloc(1024)
        NSL = 3
        NLD = 4 if xdst is None else 5
        tmp = [c.alloc(1024) for _ in range(NSL)]
        hb = [c.alloc(512) for _ in range(NSL)]
        xt = [c.alloc(1024) for _ in range(NLD)] if xdst is None else None
        posb = [c.alloc(1024) for _ in range(NLD)] if possrc is not None else None
        bcast_ada(l, j, "m", m_bc)
        bcast_ada(l, j, "shift", sh_bc)
        for it, t in enumerate(order if order is not None else range(ntiles)):
            s = it % NSL
            sl = it % NLD
            xr = xdst(t) if xdst is not None else xt[sl]
            if xsrc is not None:
                c.dma("sp", f"x{s}", lambda xr=xr, t=t: sp.dma_start(out=xr.f32(), in_=xsrc[t * 128:(t + 1) * 128, :]), writes=[xr])
                if possrc is not None:
                    c.dma("sp", f"p{sl}", lambda sl=sl, t=t: sp.dma_start(out=posb[sl].f32(), in_=possrc[t * 128:(t + 1) * 128, :]), writes=[posb[sl]])
                    c.op("dve", lambda xr=xr, sl=sl: dve.tensor_tensor(out=xr.f32(), in0=xr.f32(), in1=posb[sl].f32(), op=ALU.add), reads=[xr, posb[sl]], writes=[xr])
            ssr = ss.sub(t, 1); rsr = rstd.sub(t, 1)
            c.op("act", lambda xr=xr, s=s, ssr=ssr: act.activation(out=tmp[s].f32(), in_=xr.f32(), func=AF.Square, accum_out=ssr.f32()),
                 reads=[xr], writes=[tmp[s], ssr])
            c.op("act", lambda ssr=ssr, rsr=rsr: act.activation(out=rsr.f32(), in_=ssr.f32(), func=AF.Sqrt, scale=1.0 / D, bias=EPS),
                 reads=[ssr], writes=[rsr])
            c.op("dve", lambda rsr=rsr: dve.reciprocal(out=rsr.f32(), in_=rsr.f32()), reads=[rsr], writes=[rsr])
            c.op("dve", lambda xr=xr, s=s, rsr=rsr: dve.scalar_tensor_tensor(out=tmp[s].f32(), in0=xr.f32(), scalar=rsr.f32(), in1=m_bc.f32(), op0=ALU.mult, op1=ALU.mult),
                 reads=[xr, rsr, m_bc], writes=[tmp[s]])
            c.op("dve", lambda s=s: dve.tensor_tensor(out=hb[s].bf(), in0=tmp[s].f32(), in1=sh_bc.f32(), op=ALU.add),
                 reads=[tmp[s], sh_bc], writes=[hb[s]])
            pt, ptk = tbank()
            for kc in range(8):
                c.op("pe", lambda kc=kc, s=s, pt=pt: pe.transpose(pt[:, kc * 128:(kc + 1) * 128], hb[s].bf()[:, kc * 128:(kc + 1) * 128], identb.bf()),
                     reads=[hb[s], identb], writes=[ptk])
            cc0 = col0 + t * 128
            c.op("act", lambda pt=pt, cc0=cc0: act.copy(out=hTv[:, :, cc0:cc0 + 128], in_=pt[:, :].rearrange("p (k t) -> p k t", k=8)),
                 reads=[ptk], writes=hT_cols(cc0, 128))
            if it == 0 and after_first is not None:
                after_first()
        c.release(mk)

    def load_w_in(l, colstart, dst, sem):
        src = w_in[l].rearrange("(k p) n -> p k n", p=128)[:, :, colstart:colstart + 256]
        c.dma("pool", sem, lambda: pool.dma_start(out=dst.bf().rearrange("p (k n) -> p k n", k=8), in_=src), writes=[dst])

    def pieces_of(seqs):
        out = []
        for (c0, T) in seqs:
            for o in range(0, T, 512):
                out.append((c0 + o, min(512, T - o)))
        return out

    def l0_pre_seq(g, wu, col0, T, N2, s1r, slots, dest, u_tm, Yt, Qt):
        NI = 128 // N2
        wuv = wu.bf().rearrange("p (k n) -> p k n", k=8)
        utv = u_tm.bf().rearrange("p (m n) -> p m n", n=256)
        s1v = s1r.bf().rearrange("p (m n) -> p m n", n=256)
        Ytv = Yt.bf().rearrange("p (c q j m i) -> p c q j m i", c=2, q=2, j=N2, m=N2)
        for m2 in range(N2):
            b, ps, pk = bank()
            for kc in range(8):
                c.op("pe", lambda kc=kc, m2=m2: pe.matmul(ps[:, 0:256], lhsT=hTv[:, kc, col0 + m2:col0 + T:N2], rhs=wuv[:, kc, :], start=(kc == 0), stop=(kc == 7)),
                     reads=[hTk(kc, col0, T), wu], writes=[pk])
            ur = u_tm.sub(m2 * 128, 128)
            evac_copy(utv[:, m2, :], ps[:, 0:256], [pk], [ur])
        chk("pre_u")
        for m2 in range(N2):
            ur = u_tm.sub(m2 * 128, 128)
            for cc in range(2):
                b, ps, pk = bank()
                c.op("pe", lambda m2=m2, cc=cc: pe.matmul(ps[:, 0:256], lhsT=utv[:, m2, cc * 128:(cc + 1) * 128], rhs=s1v[:, m2, :], start=True, stop=True),
                     reads=[ur, s1r.sub(m2 * 128, 128)], writes=[pk])
                evac_copy(Ytv[:, cc, :, :, m2, :], ps[:, 0:256].rearrange("p (q j i) -> p q j i", q=2, j=N2), [pk], [Yt])
        chk("pre_s1")
        chv = chtab.bf().rearrange("p (q c n) -> p q c n", q=2, c=2)
        def s2_stage(j, q):
            b2, ps2, pk2 = bank()
            for si, (Kr, Ki, flip) in enumerate(slots):
                for lc in range(2):
                    o = ps2[:, (si * 2 + lc) * 128:(si * 2 + lc + 1) * 128]
                    c.op("pe", lambda o=o, lc=lc, Kr=Kr: pe.matmul(o, lhsT=q.bf()[:, lc * 128:(lc + 1) * 128], rhs=Kr.bf(), start=True, stop=False),
                         reads=[q, Kr], writes=[pk2])
                    c.op("pe", lambda o=o, lc=lc, Ki=Ki: pe.matmul(o, lhsT=q.bf()[:, 256 + lc * 128:256 + (lc + 1) * 128], rhs=Ki.bf(), start=False, stop=True),
                         reads=[q, Ki], writes=[pk2])
            for si, (Kr, Ki, flip) in enumerate(slots):
                jb = (N2 - 1 - j) if flip else j
                for lc in range(2):
                    o = ps2[:, (si * 2 + lc) * 128:(si * 2 + lc + 1) * 128].rearrange("p (k i) -> p k i", i=NI)
                    dap, dreg = dest(si, lc)
                    d = dap.rearrange("p (k c) -> p k c", c=128)[:, :, jb * NI:(jb + 1) * NI]
                    evac_copy(d, o, [pk2], [dreg], flip=(si == 0 and lc == 0))

        prev = None
        for j in range(N2):
            b, ps, pk = bank()
            n = 0
            for cc in range(2):
                for comp in range(2):
                    c.op("pe", lambda cc=cc, comp=comp, n=n: pe.matmul(ps[:, :], lhsT=Ytv[:, cc, comp, j, :, :].rearrange("p m i -> p (m i)"), rhs=chv[:, comp, cc, :], start=(n == 0), stop=(n == 3)),
                         reads=[Yt, chtab], writes=[pk])
                    n += 1
            q = Qt[j % len(Qt)]
            evac_copy(q.bf(), ps[:, :], [pk], [q])
            if prev is not None:
                s2_stage(*prev)
            prev = (j, q)
        s2_stage(*prev)

    def wout_load(l, hd, gate_bc, wstg, wob):
        src = w_out[l][hd * 256:(hd + 1) * 256, :].rearrange("(l p) d -> p l d", p=128)
        if wstg is not None:
            c.dma("sp", "wo", lambda: sp.dma_start(out=wstg.f32().rearrange("p (l d) -> p l d", l=2), in_=src), writes=[wstg])
            for lc in range(2):
                c.op("pool", lambda lc=lc: pool.tensor_tensor(out=wob.bf()[:, lc * 1024:(lc + 1) * 1024], in0=wstg.f32()[:, lc * 1024:(lc + 1) * 1024], in1=gate_bc.f32(), op=ALU.mult),
                     reads=[wstg.sub(lc * 1024, 1024), gate_bc], writes=[wob.sub(lc * 512, 512)])
        else:
            c.dma("pool", "wo", lambda: pool.dma_start(out=wob.bf().rearrange("p (l d) -> p l d", l=2), in_=src), writes=[wob])
            for lc in range(2):
                c.op("pool", lambda lc=lc: pool.tensor_tensor(out=wob.bf()[:, lc * 1024:(lc + 1) * 1024], in0=wob.bf()[:, lc * 1024:(lc + 1) * 1024], in1=gate_bc.f32(), op=ALU.mult),
                     reads=[wob.sub(lc * 512, 512), gate_bc], writes=[wob.sub(lc * 512, 512)])

    def wout_update(l, hd, gT, gate_bc, wstg, wob, seqs_tiles, load=True):
        if load:
            wout_load(l, hd, gate_bc, wstg, wob)
        gv = gT.bf().rearrange("p (l t) -> p l t", l=2)
        Tt = gT.n
        for (tile, col) in seqs_tiles:
            for half in range(2):
                b, ps, pk = bank()
                for lc in range(2):
                    c.op("pe", lambda lc=lc, col=col, half=half: pe.matmul(ps[:, :], lhsT=gv[:, lc, col:col + 128], rhs=wob.bf()[:, lc * 1024 + half * 512:lc * 1024 + (half + 1) * 512], start=(lc == 0), stop=(lc == 1)),
                         reads=[gT.sub(lc * (Tt // 2) + col // 2, 64), wob.sub(lc * 512 + half * 256, 256)], writes=[pk])
                xr = x1t(tile).sub(half * 512, 512)
                c.op("dve", lambda xr=xr: dve.tensor_tensor(out=xr.f32(), in0=ps[:, :], in1=xr.f32(), op=ALU.add), reads=[pk, xr], writes=[xr])

    def l0_post_group(g, Ttot, seqs, fT, wz, sz, wf, gT, gate_bc, wstg, wob, tile0):
        wzv = wz.bf().rearrange("p (k n) -> p k n", k=8)
        szv = sz.bf().rearrange("p (l t) -> p l t", l=2)
        fv = fT.bf().rearrange("p (l t) -> p l t", l=2)
        gv = gT.bf().rearrange("p (l t) -> p l t", l=2)
        wfv = wf.bf().rearrange("p (l n) -> p l n", l=2)
        pcs = pieces_of(seqs)
        for jc in range(2):
            for (c0, n) in pcs:
                b, ps, pk = bank()
                for kc in range(8):
                    c.op("pe", lambda kc=kc, jc=jc, c0=c0, n=n: pe.matmul(ps[:, 0:n], lhsT=wzv[:, kc, jc * 128:(jc + 1) * 128], rhs=hTv[:, kc, c0:c0 + n], start=(kc == 0), stop=(kc == 7)),
                         reads=[wz, hTk(kc, c0, n)], writes=[pk])
                r = sz.sub(jc * (Ttot // 2) + c0 // 2, n // 2)
                c.op("act", lambda jc=jc, c0=c0, n=n: act.activation(out=szv[:, jc, c0:c0 + n], in_=ps[:, 0:n], func=AF.Silu), reads=[pk], writes=[r])
        for jc in range(2):
            for (c0, n) in pcs:
                b, ps, pk = bank()
                for lc in range(2):
                    c.op("pe", lambda lc=lc, jc=jc, c0=c0, n=n: pe.matmul(ps[:, 0:n], lhsT=wfv[:, lc, jc * 128:(jc + 1) * 128], rhs=fv[:, lc, c0:c0 + n], start=(lc == 0), stop=(lc == 1)),
                         reads=[wf, fT.sub(lc * (Ttot // 2) + c0 // 2, n // 2)], writes=[pk])
                rs_ = sz.sub(jc * (Ttot // 2) + c0 // 2, n // 2)
                rg = gT.sub(jc * (Ttot // 2) + c0 // 2, n // 2)
                bcol = bfm.f32()[:, g * 2 + jc:g * 2 + jc + 1]
                c.op("dve", lambda jc=jc, c0=c0, n=n, bcol=bcol: dve.scalar_tensor_tensor(out=gv[:, jc, c0:c0 + n], in0=ps[:, 0:n], scalar=bcol, in1=szv[:, jc, c0:c0 + n], op0=ALU.add, op1=ALU.mult),
                     reads=[pk, bfm, rs_], writes=[rg])
        tiles = [(tile0 + i, i * 128) for i in range(Ttot // 128)]
        wout_update(0, g, gT, gate_bc, wstg, wob, tiles)

    def load_tables():
        t = {}
        t["s1s"] = c.alloc(16 * 128); t["s1p"] = c.alloc(2 * 128)
        t["k"] = [c.alloc(64) for _ in range(6)]
        t["ch"] = c.alloc(2 * 2 * 256)
        c.dma("pool", "tab", lambda: pool.dma_start(out=t["s1s"].bf().rearrange("p (m n) -> p m n", m=16), in_=s1s_d.rearrange("m p n -> p m n")), writes=[t["s1s"]])
        c.dma("pool", "tab", lambda: pool.dma_start(out=t["s1p"].bf().rearrange("p (m n) -> p m n", m=2), in_=s1p_d.rearrange("m p n -> p m n")), writes=[t["s1p"]])
        for i in range(6):
            c.dma("pool", "tab", lambda i=i: pool.dma_start(out=t["k"][i].bf(), in_=kmat_d[i]), writes=[t["k"][i]])
        c.dma("pool", "tab", lambda: pool.dma_start(out=t["ch"].bf().rearrange("p (q c n) -> p q c n", q=2, c=2), in_=chtab_d.rearrange("q (c p) n -> p q c n", p=128)), writes=[t["ch"]])
        return t

    chk("setup")
    mkL0 = c.mark()
    tabs = load_tables()
    chtab = tabs["ch"]
    front_end(16, 0, 0, 0, xsrc=xs_in, possrc=pos_in, xdst=None)
    ada_layer(1)
    dump("hT_in", lambda: hT.bf(), [128, 8 * 2048], [hT], BF16)
    chk("fe_in")
    mk = c.mark()
    wus = [c.alloc(1024), c.alloc(1024)]
    u_tm = c.alloc(16 * 128); Yt = c.alloc(4096); Qt = [c.alloc(256) for _ in range(3)]
    send = [c.alloc(4096), c.alloc(4096)]
    slotsS = [(tabs["k"][0], tabs["k"][1], False), (tabs["k"][2], tabs["k"][3], True)]
    load_w_in(0, 0, wus[0], "wu0")
    for g in range(8):
        if g + 1 < 8:
            load_w_in(0, (g + 1) * 256, wus[(g + 1) % 2], f"wu{(g + 1) % 2}")
        sd = send[g % 2]
        sdv = sd.bf().rearrange("p (s l t) -> p s l t", s=2, l=2)

        def dest(si, lc, sd=sd, sdv=sdv):
            return sdv[:, si, lc, :], sd.sub((si * 2 + lc) * 1024, 1024)
        l0_pre_seq(g, wus[g % 2], 0, TS, 16, tabs["s1s"], slotsS, dest, u_tm, Yt, Qt)
        chk("pre_s2")
        c.dma("sp", f"send{g % 2}", lambda g=g, sdv=sdv: sp.dma_start(out=rs_in[g].rearrange("(s l p) t -> p s l t", s=2, l=2), in_=sdv),
              reads=[sd], writes=[("rs_in", g)])
        chk("pre_send")
        c.cc(f"rs{g}", lambda g=g: pool.collective_compute("ReduceScatter", ALU.add, replica_groups=RG, ins=[rs_in[g]], outs=[rs_out[g]]),
             reads=[("rs_in", g)], writes=[("rs_out", g)])
        chk("pre_rs")
    c.release(mk)
    chk("l0pre")
    front_end(16, 0, 0, 0, xsrc=xs_res, possrc=pos_res, xdst=x1t)
    chk("fe_res")
    mk = c.mark()
    gate_bc = c.alloc(1024); wstg = c.alloc(2048); wob = c.alloc(1024)
    wzs = [c.alloc(1024), c.alloc(1024)]
    sz = c.alloc(2048); fTs = [c.alloc(2048), c.alloc(2048)]; wf = [c.alloc(256), c.alloc(256)]; gT = c.alloc(2048)
    bcast_ada(0, 0, "gate", gate_bc)

    def load_post(g):
        load_w_in(0, DI + g * 256, wzs[g % 2], f"wz{g % 2}")
        c.dma("pool", f"wf{g % 2}", lambda: pool.dma_start(out=wf[g % 2].bf().rearrange("p (l n) -> p l n", l=2), in_=w_fmix[g].rearrange("(l p) n -> p l n", p=128)), writes=[wf[g % 2]])
        c.dma("sp", f"ft{g % 2}", lambda: sp.dma_start(out=fTs[g % 2].bf().rearrange("p (l t) -> p l t", l=2), in_=rs_out[g].rearrange("(l p) t -> p l t", p=128)),
              reads=[("rs_out", g)], writes=[fTs[g % 2]])
    load_post(0)
    for g in range(8):
        if g + 1 < 8:
            load_post(g + 1)
        l0_post_group(g, TS, [(0, TS)], fTs[g % 2], wzs[g % 2], sz, wf[g % 2], gT, gate_bc, wstg, wob, 0)
    c.release(mk)
    c.release(mkL0)
    dump("x1_l0", lambda: x1.f32(), [128, 16 * 1024], [x1])
    chk("l0post")

    def layer1(Ttot, seqs, j, is_sample, tile0):
        ntiles = Ttot // 128
        front_end(ntiles, 0, 1, j, xdst=lambda t: x1t(tile0 + t))
        mk = c.mark()
        RL = (sum(T + 4 for (_, T) in seqs) + BLK - 1) // BLK * BLK
        Rall = c.alloc(5 * RL)
        row = [Rall.sub(i * RL, RL) for i in range(5)]
        RLB = (RL // 2 + BLK - 1) // BLK * BLK
        U = [c.alloc(RLB), c.alloc(RLB)]
        TA = [row[0].sub(0, Ttot), row[2].sub(0, Ttot)]
        TI = [row[1].sub(0, Ttot), row[3].sub(0, Ttot)]
        ts = row[4].sub(0, Ttot)
        h1 = c.alloc(Ttot)
        gate_bc = c.alloc(1024); wob = c.alloc(1024)
        wu = c.alloc(1024); wz = c.alloc(1024)
        gw = [[c.alloc(256) for _ in range(4)] for _ in range(2)]
        ucbf = c.alloc(Ttot); sz = c.alloc(Ttot); gT = c.alloc(Ttot)
        bcast_ada(1, j, "gate", gate_bc)
        pcs = pieces_of(seqs)
        pads = []
        po = 0
        for (c0, T) in seqs:
            pads.append(po)
            po += T + 4

        if is_sample:
            wuh = [wu, wz]
            b, ps, pk = bank()
            for hd in range(8):
                load_w_in(1, hd * 256, wuh[hd % 2], f"wu{hd % 2}")
                wv = wuh[hd % 2].bf().rearrange("p (k n) -> p k n", k=8)
                for jc in range(2):
                    ch = hd * 2 + jc
                    for kc in range(8):
                        c.op("pe", lambda kc=kc, jc=jc, ch=ch, wv=wv: pe.matmul(ps[:, ch * 2:ch * 2 + 2], lhsT=wv[:, kc, jc * 128:(jc + 1) * 128], rhs=hTv[:, kc, Ttot - 2:Ttot], start=(kc == 0), stop=(kc == 7)),
                             reads=[wuh[hd % 2], hTk(kc, Ttot - 2, 2)], writes=[pk])
            c.op("dve", lambda: dve.tensor_copy(out=halo.f32(), in_=ps[:, 0:32]), reads=[pk], writes=[halo])
            c.dma("sp", "halo", lambda: sp.dma_start(out=halo_in, in_=halo.f32()), reads=[halo], writes=[("halo_in",)])
            c.cc("halocc", lambda: pool.collective_compute("AllGather", ALU.bypass, replica_groups=RG, ins=[halo_in], outs=[halo_out]),
                 reads=[("halo_in",)], writes=[("halo_out",)])
            c.dma("sp", "halo", lambda: sp.dma_start(out=hg.f32().rearrange("p (r n) -> p r n", r=2), in_=halo_out.rearrange("(r p) n -> p r n", p=128)),
                  reads=[("halo_out",)], writes=[hg])
            c.op("dve", lambda: dve.tensor_scalar(out=ctmp.f32(), in0=hg.f32()[:, 0:32], scalar1=selr.f32()[:, 0:1], scalar2=None, op0=ALU.mult), reads=[hg, selr], writes=[ctmp])
            c.op("dve", lambda: dve.scalar_tensor_tensor(out=halo.f32(), in0=hg.f32()[:, 32:64], scalar=selr.f32()[:, 1:2], in1=ctmp.f32(), op0=ALU.mult, op1=ALU.add),
                 reads=[hg, selr, ctmp], writes=[halo])

        wuv = wu.bf().rearrange("p (k n) -> p k n", k=8)
        wzv = wz.bf().rearrange("p (k n) -> p k n", k=8)
        ucv = ucbf.bf().rearrange("p (l t) -> p l t", l=2)
        szv = sz.bf().rearrange("p (l t) -> p l t", l=2)
        gv = gT.bf().rearrange("p (l t) -> p l t", l=2)
        h1v = h1.bf().rearrange("p (l t) -> p l t", l=2)

        def h1r(jc, c0, n):
            return h1.sub(jc * (Ttot // 2) + c0 // 2, max(1, (n + 1) // 2))

        def load_wu(hd):
            load_w_in(1, hd * 256, wu, "wu1h")

        def loads(hd):
            load_w_in(1, DI + hd * 256, wz, "wz1h")
            gws = gw[hd % 2]
            for d in range(2):
                for gi, wsrc in enumerate((wr_d, wi_d)):
                    r = gws[d * 2 + gi]
                    c.dma("pool", f"gw{hd % 2}", lambda r=r, wsrc=wsrc, d=d: pool.dma_start(out=r.bf().rearrange("p (i n) -> p i n", i=2), in_=wsrc[d, hd].rearrange("(i p) n -> p i n", p=128)), writes=[r])

        def part_u(hd):
            for jc in range(2):
                for si, (c0, T) in enumerate(seqs):
                    for o in range(0, T, 512):
                        n = min(512, T - o)
                        b, ps, pk = bank()
                        for kc in range(8):
                            c.op("pe", lambda kc=kc, jc=jc, c0=c0, o=o, n=n: pe.matmul(ps[:, 0:n], lhsT=wuv[:, kc, jc * 128:(jc + 1) * 128], rhs=hTv[:, kc, c0 + o:c0 + o + n], start=(kc == 0), stop=(kc == 7)),
                                 reads=[wu, hTk(kc, c0 + o, n)], writes=[pk])
                        dcol = pads[si] + 2 + o
                        evac_copy(U[jc].bf()[:, dcol:dcol + n], ps[:, 0:n], [pk], [U[jc].sub(dcol // 2, (n + 1) // 2 + 1)])

        def part_z(hd):
            for jc in range(2):
                for (c0, n) in pcs:
                    b, ps, pk = bank()
                    for kc in range(8):
                        c.op("pe", lambda kc=kc, jc=jc, c0=c0, n=n: pe.matmul(ps[:, 0:n], lhsT=wzv[:, kc, jc * 128:(jc + 1) * 128], rhs=hTv[:, kc, c0:c0 + n], start=(kc == 0), stop=(kc == 7)),
                             reads=[wz, hTk(kc, c0, n)], writes=[pk])
                    c.op("act", lambda jc=jc, c0=c0, n=n: act.activation(out=szv[:, jc, c0:c0 + n], in_=ps[:, 0:n], func=AF.Silu), reads=[pk], writes=[sz.sub(jc * (Ttot // 2) + c0 // 2, n // 2)])

        def part_conv(hd):
            for jc in range(2):
                ch = hd * 2 + jc
                Ub = U[jc].bf()
                for si, (c0, T) in enumerate(seqs):
                    p0 = pads[si]
                    c.op("dve", lambda Ub=Ub, p0=p0: dve.memset(Ub[:, p0:p0 + 2], 0.0), writes=[U[jc].sub(p0 // 2, 2)])
                    if not is_sample:
                        c.op("dve", lambda Ub=Ub, p0=p0, T=T: dve.memset(Ub[:, p0 + T + 2:p0 + T + 4], 0.0), writes=[U[jc].sub((p0 + T + 2) // 2, 2)])
                if is_sample:
                    T = seqs[0][1]
                    for e in range(2):
                        c.op("dve", lambda Ub=Ub, ch=ch, e=e, T=T: dve.tensor_copy(out=Ub[:, T + 2 + e:T + 3 + e], in_=halo.f32()[:, ch * 2 + 1 - e:ch * 2 + 2 - e]),
                             reads=[halo], writes=[U[jc].sub((T + 2 + e) // 2, 1)])
                plist = []
                for si, (c0, T) in enumerate(seqs):
                    for o in range(0, T, 512):
                        plist.append((pads[si] + o, c0 + o, min(512, T - o)))
                banks = [bank() for _ in plist]
                for dd in range(5):
                    dg = diagb[dd]
                    c.op("dve", lambda dg=dg, ch=ch, dd=dd: dve.tensor_scalar(out=dg.bf(), in0=identb.bf(), scalar1=convw.f32()[:, ch * 5 + dd:ch * 5 + dd + 1], scalar2=None, op0=ALU.mult),
                         reads=[identb, convw], writes=[dg])
                    for (pcol, c0o, n), (b, ps, pk) in zip(plist, banks):
                        c.op("pe", lambda dg=dg, Ub=Ub, pcol=pcol, n=n, dd=dd, ps=ps: pe.matmul(ps[:, 0:n], lhsT=dg.bf(), rhs=Ub[:, pcol + dd:pcol + dd + n], start=(dd == 0), stop=(dd == 4)),
                             reads=[dg, U[jc].sub((pcol + dd) // 2, (n + 1) // 2 + 1)], writes=[pk])
                for (pcol, c0o, n), (b, ps, pk) in zip(plist, banks):
                    c.op("act", lambda ps=ps, c0o=c0o, n=n, ch=ch, jc=jc: act.activation(out=ucv[:, jc, c0o:c0o + n], in_=ps[:, 0:n], func=AF.Identity, bias=convb.f32()[:, ch:ch + 1]),
                         reads=[pk, convb], writes=[ucbf.sub(jc * (Ttot // 2) + c0o // 2, n // 2)])

        def stage(hd, d, jc, part="all"):
            gws = gw[hd % 2]
            ta, ti = TA[jc], TI[jc]
            ch = hd * 2 + jc
            colc = slice(d * 16 + ch, d * 16 + ch + 1)
            if part in ("all", "front"):
                stage_front(hd, d, jc, gws, ta, ti, ch, colc)
            if part in ("all", "back"):
                stage_back(hd, d, jc, ta, ti, ch)

        def stage_front(hd, d, jc, gws, ta, ti, ch, colc):
            for gi, (dst, bcol) in enumerate(((ta, brc), (ti, bic))):
                wv = gws[d * 2 + gi].bf().rearrange("p (i n) -> p i n", i=2)
                for (c0, n) in pcs:
                    b, ps, pk = bank()
                    for ic in range(2):
                        c.op("pe", lambda ic=ic, c0=c0, n=n, wv=wv: pe.matmul(ps[:, 0:n], lhsT=wv[:, ic, jc * 128:(jc + 1) * 128], rhs=ucv[:, ic, c0:c0 + n], start=(ic == 0), stop=(ic == 1)),
                             reads=[gws[d * 2 + gi], ucbf.sub(ic * (Ttot // 2) + c0 // 2, n // 2)], writes=[pk])
                    c.op("act", lambda c0=c0, n=n, dst=dst, bcol=bcol: act.activation(out=dst.f32()[:, c0:c0 + n], in_=ps[:, 0:n], func=AF.Tanh, scale=0.5, bias=bcol.f32()[:, d * 16 + ch:d * 16 + ch + 1]),
                         reads=[pk, bcol], writes=[dst.sub(c0, n)])
            c.op("act", lambda: act.activation(out=ts.f32(), in_=ta.f32(), func=AF.Exp, scale=cl.f32()[:, colc], bias=cl.f32()[:, colc]), reads=[ta, cl], writes=[ts])
            c.op("act", lambda: act.activation(out=ta.f32(), in_=ta.f32(), func=AF.Exp, scale=cl2.f32()[:, colc], bias=cl2.f32()[:, colc]), reads=[ta, cl2], writes=[ta])
            c.op("act", lambda: act.activation(out=ts.f32(), in_=ts.f32(), func=AF.Sqrt, scale=-1.0, bias=1.0), reads=[ts], writes=[ts])

        def stage_back(hd, d, jc, ta, ti, ch):
            c.op("dve", lambda: dve.scalar_tensor_tensor(out=ti.f32(), in0=ti.f32(), scalar=1.0, in1=ts.f32(), op0=ALU.add, op1=ALU.mult), reads=[ti, ts], writes=[ti])
            c.op("dve", lambda: dve.scalar_tensor_tensor(out=ti.f32(), in0=ti.f32(), scalar=0.5, in1=ucv[:, jc, :], op0=ALU.mult, op1=ALU.mult), reads=[ti, ucbf.sub(jc * (Ttot // 2), Ttot // 2)], writes=[ti])
            if d == 0:
                for si, (c0, T) in enumerate(seqs):
                    init = st0.f32()[:, ch:ch + 1] if is_sample else zcol.f32()
                    c.op("dve", lambda c0=c0, T=T, init=init: dve.tensor_tensor_scan(out=h1v[:, jc, c0:c0 + T], data0=ta.f32()[:, c0:c0 + T], data1=ti.f32()[:, c0:c0 + T], initial=init, op0=ALU.mult, op1=ALU.add),
                         reads=[ta, ti, st0, zcol], writes=[h1r(jc, c0, T)])
                    if is_sample:
                        c.op("dve", lambda c0=c0, T=T: dve.tensor_copy(out=c1.f32()[:, jc:jc + 1], in_=h1v[:, jc, c0 + T - 1:c0 + T]), reads=[h1r(jc, c0 + T - 2, 2)], writes=[c1])
                    else:
                        col = si * 32 + ch
                        c.op("dve", lambda c0=c0, T=T, col=col: dve.tensor_copy(out=stpo.f32()[:, col:col + 1], in_=h1v[:, jc, c0 + T - 1:c0 + T]), reads=[h1r(jc, c0 + T - 2, 2)], writes=[stpo.sub(col, 1)])
            else:
                if is_sample and jc == 0:
                    c.op("dve", lambda: dve.tensor_scalar(out=ctmp.f32()[:, 0:2], in0=cg.f32()[:, 0:2], scalar1=selr.f32()[:, 0:1], scalar2=None, op0=ALU.mult), reads=[cg, selr], writes=[ctmp])
                    c.op("dve", lambda: dve.scalar_tensor_tensor(out=c2.f32(), in0=cg.f32()[:, 2:4], scalar=selr.f32()[:, 1:2], in1=ctmp.f32()[:, 0:2], op0=ALU.mult, op1=ALU.add),
                         reads=[cg, selr, ctmp], writes=[c2])
                for si, (c0, T) in enumerate(seqs):
                    init = c2.f32()[:, jc:jc + 1] if is_sample else zcol.f32()
                    c.op("dve", lambda c0=c0, T=T, init=init: dve.tensor_tensor_scan(out=ti.f32()[:, c0:c0 + T][:, ::-1], data0=ta.f32()[:, c0:c0 + T][:, ::-1], data1=ti.f32()[:, c0:c0 + T][:, ::-1], initial=init, op0=ALU.mult, op1=ALU.add),
                         reads=[ta, ti, c2, zcol], writes=[ti])
                    if not is_sample:
                        col = si * 32 + 16 + ch
                        c.op("dve", lambda c0=c0, col=col: dve.tensor_copy(out=stpo.f32()[:, col:col + 1], in_=ti.f32()[:, c0:c0 + 1]), reads=[ti], writes=[stpo.sub(col, 1)])
                c.op("dve", lambda: dve.tensor_tensor(out=ti.f32(), in0=ti.f32(), in1=h1v[:, jc, :], op=ALU.add), reads=[ti, h1r(jc, 0, Ttot)], writes=[ti])
                c.op("dve", lambda: dve.tensor_tensor(out=gv[:, jc, :], in0=ti.f32(), in1=szv[:, jc, :], op=ALU.mult), reads=[ti, sz.sub(jc * (Ttot // 2), Ttot // 2)], writes=[gT.sub(jc * (Ttot // 2), Ttot // 2)])

        def exchange_send(hd):
            c.dma("sp", "car", lambda: sp.dma_start(out=car_in[hd], in_=c1.f32()), reads=[c1], writes=[("car_in", hd)])
            c.cc(f"carcc{hd}", lambda: pool.collective_compute("AllGather", ALU.bypass, replica_groups=RG, ins=[car_in[hd]], outs=[car_out[hd]]),
                 reads=[("car_in", hd)], writes=[("car_out", hd)])
            c.dma("sp", "car", lambda: sp.dma_start(out=cg.f32().rearrange("p (r n) -> p r n", r=2), in_=car_out[hd].rearrange("(r p) n -> p r n", p=128)),
                  reads=[("car_out", hd)], writes=[cg])

        tiles = [(tile0 + i, i * 128) for i in range(ntiles)]
        load_wu(0)
        loads(0)
        part_u(0)
        part_z(0)
        part_conv(0)
        for hd in range(8):
            if hd + 1 < 8:
                load_wu(hd + 1)
            stage(hd, 0, 0, "front")
            if hd > 0:
                wout_update(1, hd - 1, gT, gate_bc, None, wob, tiles, load=False)
            stage(hd, 0, 0, "back")
            stage(hd, 0, 1)
            if hd + 1 < 8:
                part_u(hd + 1)
            if is_sample:
                exchange_send(hd)
            wout_load(1, hd, gate_bc, None, wob)
            stage(hd, 1, 0)
            if hd + 1 < 8:
                loads(hd + 1)
            stage(hd, 1, 1)
            if hd + 1 < 8:
                part_z(hd + 1)
                part_conv(hd + 1)
        wout_update(1, 7, gT, gate_bc, None, wob, tiles, load=False)
        c.release(mk)

    def final_out(ntiles, tile0, dst):
        mk = c.mark()
        fg_bc = c.alloc(1024)
        tmp = [c.alloc(1024), c.alloc(1024)]
        c.dma("sp", "bc", lambda: sp.dma_start(out=fg_bc.f32(), in_=fgr_d[0, :].partition_broadcast(128)), writes=[fg_bc])
        for t in range(ntiles):
            s = t % 2
            xr = x1t(tile0 + t)
            ssr = ss.sub(t, 1); rsr = rstd.sub(t, 1)
            c.op("act", lambda xr=xr, s=s, ssr=ssr: act.activation(out=tmp[s].f32(), in_=xr.f32(), func=AF.Square, accum_out=ssr.f32()), reads=[xr], writes=[tmp[s], ssr])
            c.op("act", lambda ssr=ssr, rsr=rsr: act.activation(out=rsr.f32(), in_=ssr.f32(), func=AF.Sqrt, scale=1.0 / D, bias=EPS), reads=[ssr], writes=[rsr])
            c.op("dve", lambda rsr=rsr: dve.reciprocal(out=rsr.f32(), in_=rsr.f32()), reads=[rsr], writes=[rsr])
            c.op("dve", lambda xr=xr, s=s, rsr=rsr: dve.scalar_tensor_tensor(out=tmp[s].f32(), in0=xr.f32(), scalar=rsr.f32(), in1=fg_bc.f32(), op0=ALU.mult, op1=ALU.mult),
                 reads=[xr, rsr, fg_bc], writes=[tmp[s]])
            c.dma("sp", f"out{s}", lambda s=s, t=t: sp.dma_start(out=dst[t * 128:(t + 1) * 128, :], in_=tmp[s].f32()), reads=[tmp[s]], writes=[("dout", id(dst), t)])
        c.release(mk)

    layer1(TS, [(0, TS)], 0, True, 0)
    dump("x1_l1", lambda: x1.f32(), [128, 16 * 1024], [x1])
    chk("l1s")
    final_out(16, 0, ys)
    chk("sample")

    mkP = c.mark()
    tabs = load_tables()
    chtab = tabs["ch"]
    front_end(4, 0, 0, 1, xsrc=xp, possrc=None, xdst=x1t)
    mk = c.mark()
    TT = 2 * TP
    wus = [c.alloc(1024), c.alloc(1024)]
    u_tm = c.alloc(2 * 128); Yt = c.alloc(512); Qt = [c.alloc(256) for _ in range(3)]
    gate_bc = c.alloc(1024); wstg = c.alloc(2048); wob = c.alloc(1024)
    wzs = [c.alloc(1024), c.alloc(1024)]
    sz = c.alloc(TT); fT = c.alloc(TT); wf = [c.alloc(256), c.alloc(256)]; gT = c.alloc(TT)
    bcast_ada(0, 1, "gate", gate_bc)
    slotsP = [(tabs["k"][4], tabs["k"][5], False)]
    fv = fT.bf().rearrange("p (l t) -> p l t", l=2)
    for g in range(8):
        load_w_in(0, g * 256, wus[g % 2], f"wu{g % 2}")
        load_w_in(0, DI + g * 256, wzs[g % 2], f"wz{g % 2}")
        c.dma("pool", f"wf{g % 2}", lambda g=g: pool.dma_start(out=wf[g % 2].bf().rearrange("p (l n) -> p l n", l=2), in_=w_fmix[g].rearrange("(l p) n -> p l n", p=128)), writes=[wf[g % 2]])
        for q in range(2):
            def dest(si, lc, q=q):
                return fv[:, lc, q * TP:(q + 1) * TP], fT.sub(lc * (TT // 2) + q * TP // 2, TP // 2)
            l0_pre_seq(g, wus[g % 2], q * TP, TP, 2, tabs["s1p"], slotsP, dest, u_tm, Yt, Qt)
        l0_post_group(g, TT, [(0, TP), (TP, TP)], fT, wzs[g % 2], sz, wf[g % 2], gT, gate_bc, wstg, wob, 0)
    c.release(mk)
    c.release(mkP)
    layer1(TT, [(0, TP), (TP, TP)], 1, False, 0)
    c.dma("sp", "stpo", lambda: sp.dma_start(out=stp, in_=stpo.f32()), reads=[stpo], writes=[("stp",)])
    final_out(4, 0, yp)


def build_nc():
    nc0 = bass.Bass("TRN2", target_bir_lowering=False)
    with ExitStack() as st:
        c0 = Ctx(nc0, st, needed=None)
        _emit(nc0, c0)
        needed = c0.need_out
    nc = bass.Bass("TRN2", target_bir_lowering=False)
    with ExitStack() as st:
        c = Ctx(nc, st, needed=needed)
        _emit(nc, c)
        print(f"[kernel] insts={c.ninst} waits={c.nwaits} incs={ {e: len(needed[e]) for e in needed} } arena_top={c.top}")
    return nc


def _col(v, nchunks):
    return np.ascontiguousarray(np.asarray(v, np.float32).reshape(nchunks, 128).T)


_NC_CACHE = {}
_RETURN_MAPS = False


def kernel(x_prompt, x_sample, state_lru_1, c, c_ctx,
           norm_g_0, w_ada_0, b_ada_0, w_in_0, w_fmix_0, b_fmix_0, w_out_0,
           norm_g_1, w_ada_1, b_ada_1, w_in_1, conv_w_1, conv_b_1,
           w_rgate_1, b_rgate_1, w_igate_1, b_igate_1, lam_1, w_out_1,
           final_g):
    f = lambda a: np.ascontiguousarray(np.asarray(a, np.float32))
    x_prompt, x_sample, state_lru_1, c, c_ctx = map(f, (x_prompt, x_sample, state_lru_1, c, c_ctx))
    pos = pos_embed(4096, D)
    ident = np.eye(128, dtype=np.float32)
    cht = ch_tables().astype(np.float32)
    tabs_s = [dft_tables_sample(s) for s in (0, 1)]
    tabs_p = [dft_tables_prompt(bool(s)) for s in (0, 1)]
    common = {
        "w_ada_0": f(w_ada_0), "w_ada_1": f(w_ada_1), "badar_0": f(b_ada_0).reshape(1, 3072), "badar_1": f(b_ada_1).reshape(1, 3072),
        "ngr_0": f(norm_g_0).reshape(1, 1024), "ngr_1": f(norm_g_1).reshape(1, 1024), "fgr": f(final_g).reshape(1, 1024),
        "w_in_0": f(w_in_0), "w_in_1": f(w_in_1), "w_fmix": f(w_fmix_0), "bfm": _col(b_fmix_0, 16),
        "w_out_0": f(w_out_0), "w_out_1": f(w_out_1), "convb": _col(conv_b_1, 16),
        "ident": ident, "chtab": cht,
    }
    cw = f(conv_w_1)
    zero = np.zeros_like(cw[0])
    in_maps = []
    for cid in range(NCORES):
        p, s = cid // 2, cid % 2
        m = dict(common)
        xs = x_sample[p]
        m["xs_in"] = np.ascontiguousarray(xs[s::2]); m["pos_in"] = np.ascontiguousarray(pos[s::2])
        if s == 0:
            m["xs_res"] = np.ascontiguousarray(xs[:TS]); m["pos_res"] = np.ascontiguousarray(pos[:TS])
        else:
            m["xs_res"] = np.ascontiguousarray(xs[::-1][:TS]); m["pos_res"] = np.ascontiguousarray(pos[::-1][:TS])
        xpp = x_prompt[2 * cid:2 * cid + 2]
        if s == 1:
            xpp = xpp[:, ::-1]
        m["xp"] = np.ascontiguousarray(xpp.reshape(2 * TP, D))
        condT = np.zeros((128, 16), np.float32)
        condT[:, 0::2] = _col(c[p], 8); condT[:, 1::2] = _col(c_ctx, 8)
        m["condT"] = condT
        sel = np.zeros((128, 2), np.float32); sel[:, 1 - s] = 1.0
        m["sel"] = sel
        m["st0"] = _col(state_lru_1[p, s], 16)
        taps = [cw[0], cw[1], cw[2], cw[3], zero] if s == 0 else [zero, cw[3], cw[2], cw[1], cw[0]]
        m["convw"] = np.ascontiguousarray(np.stack([_col(t, 16) for t in taps], axis=-1).reshape(128, 80))
        d1, d2 = (0, 1) if s == 0 else (1, 0)
        m["wr"] = np.ascontiguousarray(f(w_rgate_1)[[d1, d2]]); m["wi"] = np.ascontiguousarray(f(w_igate_1)[[d1, d2]])
        m["br"] = np.concatenate([_col(b_rgate_1[d1], 16), _col(b_rgate_1[d2], 16)], 1)
        m["bi"] = np.concatenate([_col(b_igate_1[d1], 16), _col(b_igate_1[d2], 16)], 1)
        m["lam"] = np.concatenate([_col(lam_1[d1], 16), _col(lam_1[d2], 16)], 1)
        s1, KrA, KiA, KrB, KiB = tabs_s[s]
        s1p, KrP, KiP = tabs_p[s]
        m["s1s"] = s1.astype(np.float32); m["s1p"] = s1p.astype(np.float32)
        m["kmat"] = np.stack([KrA, KiA, KrB, KiB, KrP, KiP]).astype(np.float32)
        in_maps.append({k: np.ascontiguousarray(v, dtype=np.float32) for k, v in m.items()})

    if _RETURN_MAPS:
        return in_maps
    if "nc" not in _NC_CACHE:
        _NC_CACHE["nc"] = build_nc()
    res = run_bass_kernel_spmd(_NC_CACHE["nc"], in_maps, core_ids=list(range(NCORES)))
    outs = res.results

    y_prompt = np.zeros((16, TP, D), np.float32)
    y_sample = np.zeros((4, 4096, D), np.float32)
    new_state = np.zeros((16, 2, DI), np.float32)
    for cid in range(NCORES):
        p, s = cid // 2, cid % 2
        ysv = np.asarray(outs[cid]["ys"], np.float32)
        ypv = np.asarray(outs[cid]["yp"], np.float32).reshape(2, TP, D)
        stv = np.asarray(outs[cid]["stp"], np.float32).reshape(128, 2, 2, 16)
        stv = stv.transpose(1, 2, 3, 0).reshape(2, 2, DI)
        if s == 0:
            y_sample[p, :TS] = ysv
            y_prompt[2 * cid:2 * cid + 2] = ypv
            new_state[2 * cid:2 * cid + 2] = stv
        else:
            y_sample[p, TS:] = ysv[::-1]
            y_prompt[2 * cid:2 * cid + 2] = ypv[:, ::-1]
            new_state[2 * cid:2 * cid + 2] = stv[:, ::-1]
    return (y_prompt, y_sample, new_state)
```

```python
import numpy as np
from contextlib import ExitStack
import concourse.bass as bass
import concourse.mybir as mybir
from concourse.bass_utils import run_bass_kernel_spmd

F32 = mybir.dt.float32
BF16 = mybir.dt.bfloat16
AF = mybir.ActivationFunctionType
ALU = mybir.AluOpType

D = 1024
DI = 2048
EPS = 1e-6
GRID_W = 64
POS_BASE = 10000.0
NCORES = 8
TS = 2048
TP = 256
NW = 53100
BLK = 64


def pos_embed(n_tokens, dim):
    rows = n_tokens // GRID_W
    r = np.broadcast_to(np.arange(rows, dtype=np.float32)[:, None], (rows, GRID_W)).reshape(-1)
    col = np.broadcast_to(np.arange(GRID_W, dtype=np.float32)[None, :], (rows, GRID_W)).reshape(-1)
    quarter = dim // 4
    omega = (1.0 / (np.float32(POS_BASE) ** (np.arange(quarter, dtype=np.float32) / np.float32(quarter)))).astype(np.float32)
    er = (r[:, None] * omega).astype(np.float32)
    ec = (col[:, None] * omega).astype(np.float32)
    return np.concatenate([np.sin(er), np.cos(er), np.sin(ec), np.cos(ec)], axis=-1).astype(np.float32)


def s1_table(S, N2, alpha, beta, gamma, delta):
    m1 = np.arange(128)[:, None]
    k1 = np.arange(128)[None, :]
    out = np.zeros((N2, 128, 256), np.float64)
    for m2 in range(N2):
        m = N2 * m1 + m2
        ph = -2 * np.pi * (((alpha * m + beta) * (gamma * k1 + delta)) % S) / S
        out[m2, :, :128] = np.cos(ph)
        out[m2, :, 128:] = np.sin(ph)
    return out / np.sqrt(128.0)


def s2_kron(S2c, N2, flip):
    NI = 128 // N2
    Kr = np.zeros((128, 128))
    Ki = np.zeros((128, 128))
    for m2 in range(N2):
        for k2 in range(N2):
            for i in range(NI):
                ip = (NI - 1 - i) if flip else i
                Kr[m2 * NI + i, k2 * NI + ip] = S2c[m2, k2].real
                Ki[m2 * NI + i, k2 * NI + ip] = -S2c[m2, k2].imag
    return Kr, Ki


def dft_tables_prompt(rev):
    S = 256
    N2 = 2
    if rev:
        alpha, beta, gamma, delta = -1, 255, -1, 255
    else:
        alpha, beta, gamma, delta = 1, 0, 1, 0
    s1 = s1_table(S, N2, alpha, beta, gamma, delta)
    m2 = np.arange(N2)[:, None]
    k2 = np.arange(N2)[None, :]
    S2c = np.exp(-2j * np.pi * ((alpha * gamma * 128 * m2 * k2 + beta * gamma * 128 * k2) % S) / S) / np.sqrt(S / 128.0)
    Kr, Ki = s2_kron(S2c, N2, False)
    return s1, Kr, Ki


def dft_tables_sample(sig):
    S = 4096
    N2 = 16
    s1 = s1_table(S, N2, 2, sig, 1, 0)
    m2 = np.arange(N2)[:, None]
    k2 = np.arange(N2)[None, :]
    nrm = np.sqrt(S / 128.0)
    S2A = np.exp(-2j * np.pi * (((2 * m2 + sig) * k2) % 32) / 32) / nrm
    S2B = np.exp(-2j * np.pi * (((2 * m2 + sig) * (31 - k2)) % 32) / 32) / nrm
    KrA, KiA = s2_kron(S2A, N2, False)
    KrB, KiB = s2_kron(S2B, N2, True)
    return s1, KrA, KiA, KrB, KiB


def ch_tables():
    c = np.arange(256)[:, None]
    l = np.arange(256)[None, :]
    ph = 2 * np.pi * ((c * l) % 256) / 256
    C = np.cos(ph) / 16.0
    S_ = np.sin(ph) / 16.0
    return np.stack([np.concatenate([C, -S_], 1), np.concatenate([S_, C], 1)], 0)


class Region:
    def __init__(self, ctx, off, n):
        self.ctx, self.off, self.n = ctx, off, n

    def f32(self):
        return self.ctx.arena[:, self.off:self.off + self.n]

    def bf(self):
        return self.ctx.arena[:, self.off:self.off + self.n].bitcast(BF16)

    def sub(self, lo, n):
        assert lo >= 0 and lo + n <= self.n, (lo, n, self.n)
        return Region(self.ctx, self.off + lo, n)

    def keys(self):
        return [("A", b) for b in range(self.off // BLK, (self.off + self.n - 1) // BLK + 1)]


def K(*items):
    out = []
    for it in items:
        if isinstance(it, Region):
            out.extend(it.keys())
        elif isinstance(it, (list, tuple)) and len(it) > 0 and isinstance(it[0], (Region, list)):
            out.extend(K(*it))
        elif isinstance(it, list):
            out.extend(it)
        else:
            out.append(it)
    return out


class Ctx:
    COMPUTE = ("pe", "act", "dve", "pool")

    def __init__(self, nc, stack, needed=None):
        self.nc = nc
        self.stack = stack
        self.dry = needed is None
        self.needed = needed
        self.need_out = {e: set() for e in self.COMPUTE}
        self.eng = {"pe": nc.tensor, "act": nc.scalar, "dve": nc.vector, "pool": nc.gpsimd, "sp": nc.sync}
        self.csem = {e: stack.enter_context(nc.semaphore("c_" + e)) for e in self.COMPUTE}
        self.ccnt = {e: 0 for e in self.COMPUTE}
        self.rank = None
        if needed is not None:
            self.rank = {}
            for e in self.COMPUTE:
                self.rank[e] = {idx: i + 1 for i, idx in enumerate(sorted(needed[e]))}
        self.dsem = {}
        self.dcnt = {}
        self.waited = {e: {} for e in self.eng}
        self.lastw = {}
        self.readers = {}
        self.nwaits = 0
        self.ninst = {e: 0 for e in self.eng}
        self.arena = stack.enter_context(nc.sbuf_tensor("arena", [128, NW], F32))
        self.top = 0
        self.bankrr = 0
        self.drrq = {}
        self.ptrr = 0

    def alloc(self, n):
        n_al = (n + BLK - 1) // BLK * BLK
        r = Region(self, self.top, n)
        self.top += n_al
        assert self.top <= NW, f"arena overflow {self.top} > {NW}"
        return r

    def mark(self):
        return self.top

    def release(self, m):
        self.top = m

    def _deps(self, e, reads, writes):
        deps = {}

        def add(t):
            if t is None:
                return
            s, v, src = t
            if src == "pe" and e == "pe":
                return
            if s not in deps or deps[s] < v:
                deps[s] = v
        for k in reads:
            add(self.lastw.get(k))
        for k in writes:
            add(self.lastw.get(k))
            for t in self.readers.get(k, ()):
                add(t)
        w = self.waited[e]
        for s, v in deps.items():
            if w.get(s, 0) >= v:
                continue
            w[s] = v
            self.nwaits += 1
            if s in self.COMPUTE:
                self.need_out[s].add(v)
                if not self.dry:
                    self.eng[e].wait_ge(self.csem[s], self.rank[s][v])
            else:
                if not self.dry:
                    self.eng[e].wait_ge(self.dsem[s], v)

    def _record(self, t, reads, writes):
        for k in writes:
            self.lastw[k] = t
            self.readers[k] = []
        for k in reads:
            self.readers.setdefault(k, []).append(t)

    def op(self, e, fn, reads=(), writes=()):
        reads = K(*reads)
        writes = K(*writes)
        self._deps(e, reads, writes)
        self.ccnt[e] += 1
        self.ninst[e] += 1
        idx = self.ccnt[e]
        if not self.dry:
            inst = fn()
            if idx in self.rank[e]:
                inst.then_inc(self.csem[e], 1)
        self._record((e, idx, e), reads, writes)

    NPOOL = 28

    def _dsem(self, name):
        if name not in self.dsem:
            self.dsem[name] = self.stack.enter_context(self.nc.semaphore("d_" + name))
            self.dcnt[name] = 0

    def dma(self, q, semname, fn, reads=(), writes=()):
        reads = K(*reads)
        writes = K(*writes)
        self._deps(q, reads, writes)
        rr = self.drrq.setdefault(q, 0)
        self.drrq[q] = rr + 1
        name = "%s%d" % (q, rr % self.NPOOL)
        self._dsem(name)
        prev = self.dcnt[name]
        if prev > 0 and self.waited[q].get(name, 0) < prev:
            self.waited[q][name] = prev
            self.nwaits += 1
            if not self.dry:
                self.eng[q].wait_ge(self.dsem[name], prev)
        self.dcnt[name] += 16
        self.ninst[q] += 1
        if not self.dry:
            fn().then_inc(self.dsem[name], 16)
        self._record((name, self.dcnt[name], "dma"), reads, writes)

    def cc(self, semname, fn, reads=(), writes=()):
        reads = K(*reads)
        writes = K(*writes)
        self._deps("pool", reads, writes)
        self._dsem(semname)
        self.dcnt[semname] += 1
        if not self.dry:
            fn().then_inc(self.dsem[semname], 1)
        self._record((semname, self.dcnt[semname], "cc"), reads, writes)

    def finish(self, e="sp"):
        allk = list(set(self.lastw) | set(self.readers))
        self._deps(e, allk, allk)


STOP = None
DEBUG = ()


class _Stop(Exception):
    pass


def _emit(nc, c):
    try:
        _emit_body(nc, c)
    except _Stop:
        pass
    c.finish("sp")


def _emit_body(nc, c):
    def chk(name):
        if STOP == name:
            raise _Stop()

    def dump(name, ap_fn, shape, reads, dt=F32):
        if name not in DEBUG:
            return
        d = nc.dram_tensor("dbg_" + name, list(shape), dt, kind="ExternalOutput").ap()
        c.dma("sp", "dbg_" + name, lambda: nc.sync.dma_start(out=d, in_=ap_fn()), reads=reads, writes=[("dbg", name)])
    eng = c.eng
    pe, act, dve, pool, sp = nc.tensor, nc.scalar, nc.vector, nc.gpsimd, nc.sync

    def din(name, shape, dt=F32):
        return nc.dram_tensor(name, list(shape), dt, kind="ExternalInput").ap()

    def dout(name, shape, dt=F32):
        return nc.dram_tensor(name, list(shape), dt, kind="ExternalOutput").ap()

    def dint(name, shape, dt):
        return nc.dram_tensor(name, list(shape), dt, kind="Internal").ap()

    xs_in = din("xs_in", [TS, D]); pos_in = din("pos_in", [TS, D])
    xs_res = din("xs_res", [TS, D]); pos_res = din("pos_res", [TS, D])
    xp = din("xp", [2 * TP, D])
    condT = din("condT", [128, 16]); sel_d = din("sel", [128, 2]); st0_d = din("st0", [128, 16])
    w_ada = [din("w_ada_0", [D, 3 * D]), din("w_ada_1", [D, 3 * D])]
    badar_d = [din("badar_0", [1, 3072]), din("badar_1", [1, 3072])]
    ngr_d = [din("ngr_0", [1, 1024]), din("ngr_1", [1, 1024])]
    fgr_d = din("fgr", [1, 1024])
    w_in = [din("w_in_0", [D, 2 * DI]), din("w_in_1", [D, 2 * DI])]
    w_fmix = din("w_fmix", [8, 256, 256]); bfm_d = din("bfm", [128, 16])
    w_out = [din("w_out_0", [DI, D]), din("w_out_1", [DI, D])]
    convw_d = din("convw", [128, 80]); convb_d = din("convb", [128, 16])
    wr_d = din("wr", [2, 8, 256, 256]); wi_d = din("wi", [2, 8, 256, 256])
    br_d = din("br", [128, 32]); bi_d = din("bi", [128, 32]); lam_d = din("lam", [128, 32])
    ident_d = din("ident", [128, 128])
    s1s_d = din("s1s", [16, 128, 256]); s1p_d = din("s1p", [2, 128, 256])
    kmat_d = din("kmat", [6, 128, 128]); chtab_d = din("chtab", [2, 256, 512])
    ys = dout("ys", [TS, D]); yp = dout("yp", [2 * TP, D]); stp = dout("stp", [128, 64])
    rs_in = [dint(f"rs_in{g}", [512, TS], BF16) for g in range(8)]
    rs_out = [dint(f"rs_out{g}", [256, TS], BF16) for g in range(8)]
    halo_in = dint("halo_in", [128, 32], F32); halo_out = dint("halo_out", [256, 32], F32)
    car_in = [dint(f"car_in{h}", [128, 2], F32) for h in range(8)]
    car_out = [dint(f"car_out{h}", [256, 2], F32) for h in range(8)]
    RG = [[0, 1], [2, 3], [4, 5], [6, 7]]

    psb = [c.stack.enter_context(nc.psum_tensor(f"ps{i}", [128, 512], F32)) for i in range(6)]
    ptb = [c.stack.enter_context(nc.psum_tensor(f"pt{i}", [128, 1024], BF16)) for i in range(2)]

    def bank():
        b = c.bankrr
        c.bankrr = (b + 1) % 6
        return b, psb[b], ("ps", b)

    def tbank():
        b = c.ptrr
        c.ptrr = (b + 1) % 2
        return ptb[b], ("pt", b)

    evac_rr = [0]

    def evac_copy(out_ap, in_ap, reads, writes, flip=True):
        if flip:
            evac_rr[0] ^= 1
        if evac_rr[0]:
            c.op("act", lambda: act.copy(out=out_ap, in_=in_ap), reads, writes)
        else:
            c.op("dve", lambda: dve.tensor_copy(out=out_ap, in_=in_ap), reads, writes)

    x1 = c.alloc(16 * 1024)
    hT = c.alloc(8 * 1024)
    smalls = c.alloc(1280)
    so = [0]

    def S(n):
        r = smalls.sub(so[0], n)
        so[0] += n
        return r
    identb = S(64)
    ss = S(32); rstd = S(32)
    scond = S(16); selr = S(2); st0 = S(16); zcol = S(1)
    bfm = S(16); convw = S(80); convb = S(16)
    brc = S(32); bic = S(32); cl = S(32); cl2 = S(32)
    halo = S(32); hg = S(64); c1 = S(2); c2 = S(2); cg = S(4); ctmp = S(32)
    stpo = S(64)
    diagb = [S(64) for _ in range(5)]
    hTv = hT.bf().rearrange("p (k t) -> p k t", k=8)
    x1v = x1.f32().rearrange("p (t d) -> p t d", t=16)

    def x1t(t):
        return x1.sub(t * 1024, 1024)

    def hTk(kc, c0, n):
        return hT.sub(kc * 1024 + c0 // 2, (n + 1) // 2)

    def hT_cols(c0, n):
        return [hTk(kc, c0, n) for kc in range(8)]

    def ld(r, src, sem="small"):
        c.dma("sp", sem, lambda: sp.dma_start(out=r.f32(), in_=src), writes=[r])
    ld(scond, condT); ld(selr, sel_d); ld(st0, st0_d)
    ld(bfm, bfm_d); ld(convw, convw_d); ld(convb, convb_d)
    ld(brc, br_d); ld(bic, bi_d); ld(cl, lam_d)
    c.dma("pool", "identb", lambda: pool.dma_start(out=identb.bf(), in_=ident_d), writes=[identb])
    c.op("dve", lambda: dve.memset(zcol.f32(), 0.0), writes=[zcol])
    c.op("act", lambda: act.activation(out=scond.f32(), in_=scond.f32(), func=AF.Silu), reads=[scond], writes=[scond])
    c.op("act", lambda: act.activation(out=cl.f32(), in_=cl.f32(), func=AF.Exp, scale=-1.0), reads=[cl], writes=[cl])
    c.op("act", lambda: act.activation(out=cl.f32(), in_=cl.f32(), func=AF.Ln, bias=1.0), reads=[cl], writes=[cl])
    c.op("dve", lambda: dve.tensor_scalar(out=cl2.f32(), in0=cl.f32(), scalar1=-4.0, scalar2=None, op0=ALU.mult), reads=[cl], writes=[cl2])
    c.op("dve", lambda: dve.tensor_scalar(out=cl.f32(), in0=cl.f32(), scalar1=-8.0, scalar2=None, op0=ALU.mult), reads=[cl], writes=[cl])
    c.op("dve", lambda: dve.tensor_scalar(out=brc.f32(), in0=brc.f32(), scalar1=0.5, scalar2=None, op0=ALU.mult), reads=[brc], writes=[brc])
    c.op("dve", lambda: dve.tensor_scalar(out=bic.f32(), in0=bic.f32(), scalar1=0.5, scalar2=None, op0=ALU.mult), reads=[bic], writes=[bic])

    ada_dram = [dint(f"ada_rows{l}", [2, 3072], F32) for l in range(2)]

    def ada_layer(l):
        mk = c.mark()
        NS = 3
        wst = [c.alloc(3072) for _ in range(NS)]
        arow = c.alloc(3072); brow = c.alloc(3072); grow = c.alloc(1024)
        c.dma("sp", "brow", lambda: sp.dma_start(out=brow.f32()[0:2, :], in_=badar_d[l][0, :].partition_broadcast(2)), writes=[brow])
        c.dma("sp", "grow", lambda: sp.dma_start(out=grow.f32()[0:2, :], in_=ngr_d[l][0, :].partition_broadcast(2)), writes=[grow])
        for kc in range(8):
            r = wst[kc % NS]
            c.dma("sp", "wada", lambda kc=kc, r=r: sp.dma_start(out=r.f32(), in_=w_ada[l][kc * 128:(kc + 1) * 128, :]), writes=[r])
            for n in range(6):
                c.op("pe", lambda kc=kc, n=n, r=r: pe.matmul(psb[n][0:2, :], lhsT=scond.f32()[:, kc * 2:kc * 2 + 2], rhs=r.f32()[:, n * 512:(n + 1) * 512], start=(kc == 0), stop=(kc == 7)),
                     reads=[r, scond], writes=[("ps", n)])
        for n in range(6):
            c.op("dve", lambda n=n: dve.tensor_tensor(out=arow.f32()[0:2, n * 512:(n + 1) * 512], in0=psb[n][0:2, :], in1=brow.f32()[0:2, n * 512:(n + 1) * 512], op=ALU.add),
                 reads=[("ps", n), brow], writes=[arow.sub(n * 512, 512)])
        c.op("dve", lambda: dve.scalar_tensor_tensor(out=arow.f32()[0:2, 1024:2048], in0=arow.f32()[0:2, 1024:2048], scalar=1.0, in1=grow.f32()[0:2, :], op0=ALU.add, op1=ALU.mult),
             reads=[arow, grow], writes=[arow.sub(1024, 1024)])
        c.dma("sp", "adaout", lambda: sp.dma_start(out=ada_dram[l], in_=arow.f32()[0:2, :]), reads=[arow], writes=[("ada", l)])
        c.release(mk)

    WHICH = {"shift": 0, "m": 1, "gate": 2}

    def bcast_ada(l, j, which, dst):
        w = WHICH[which]
        c.dma("sp", "bc", lambda: sp.dma_start(out=dst.f32(), in_=ada_dram[l][j, w * 1024:(w + 1) * 1024].partition_broadcast(128)), reads=[("ada", l)], writes=[dst])

    ada_layer(0)

    def front_end(ntiles, col0, l, j, xsrc=None, possrc=None, xdst=None, order=None, after_first=None):
        mk = c.mark()
        m_bc = c.alloc(1024); sh_bc = c.alloc(1024)
        NSL = 3
        NLD = 4 if xdst is None else 5
        tmp = [c.alloc(1024) for _ in range(NSL)]
        hb = [c.alloc(512) for _ in range(NSL)]
        xt = [c.alloc(1024) for _ in range(NLD)] if xdst is None else None
        posb = [c.alloc(1024) for _ in range(NLD)] if possrc is not None else None
        bcast_ada(l, j, "m", m_bc)
        bcast_ada(l, j, "shift", sh_bc)
        for it, t in enumerate(order if order is not None else range(ntiles)):
            s = it % NSL
            sl = it % NLD
            xr = xdst(t) if xdst is not None else xt[sl]
            if xsrc is not None:
                c.dma("sp", f"x{s}", lambda xr=xr, t=t: sp.dma_start(out=xr.f32(), in_=xsrc[t * 128:(t + 1) * 128, :]), writes=[xr])
                if possrc is not None:
                    c.dma("sp", f"p{sl}", lambda sl=sl, t=t: sp.dma_start(out=posb[sl].f32(), in_=possrc[t * 128:(t + 1) * 128, :]), writes=[posb[sl]])
                    c.op("dve", lambda xr=xr, sl=sl: dve.tensor_tensor(out=xr.f32(), in0=xr.f32(), in1=posb[sl].f32(), op=ALU.add), reads=[xr, posb[sl]], writes=[xr])
            ssr = ss.sub(t, 1); rsr = rstd.sub(t, 1)
            c.op("act", lambda xr=xr, s=s, ssr=ssr: act.activation(out=tmp[s].f32(), in_=xr.f32(), func=AF.Square, accum_out=ssr.f32()),
                 reads=[xr], writes=[tmp[s], ssr])
            c.op("act", lambda ssr=ssr, rsr=rsr: act.activation(out=rsr.f32(), in_=ssr.f32(), func=AF.Sqrt, scale=1.0 / D, bias=EPS),
                 reads=[ssr], writes=[rsr])
            c.op("dve", lambda rsr=rsr: dve.reciprocal(out=rsr.f32(), in_=rsr.f32()), reads=[rsr], writes=[rsr])
            c.op("dve", lambda xr=xr, s=s, rsr=rsr: dve.scalar_tensor_tensor(out=tmp[s].f32(), in0=xr.f32(), scalar=rsr.f32(), in1=m_bc.f32(), op0=ALU.mult, op1=ALU.mult),
                 reads=[xr, rsr, m_bc], writes=[tmp[s]])
            c.op("dve", lambda s=s: dve.tensor_tensor(out=hb[s].bf(), in0=tmp[s].f32(), in1=sh_bc.f32(), op=ALU.add),
                 reads=[tmp[s], sh_bc], writes=[hb[s]])
            pt, ptk = tbank()
            for kc in range(8):
                c.op("pe", lambda kc=kc, s=s, pt=pt: pe.transpose(pt[:, kc * 128:(kc + 1) * 128], hb[s].bf()[:, kc * 128:(kc + 1) * 128], identb.bf()),
                     reads=[hb[s], identb], writes=[ptk])
            cc0 = col0 + t * 128
            c.op("act", lambda pt=pt, cc0=cc0: act.copy(out=hTv[:, :, cc0:cc0 + 128], in_=pt[:, :].rearrange("p (k t) -> p k t", k=8)),
                 reads=[ptk], writes=hT_cols(cc0, 128))
            if it == 0 and after_first is not None:
                after_first()
        c.release(mk)

    def load_w_in(l, colstart, dst, sem):
        src = w_in[l].rearrange("(k p) n -> p k n", p=128)[:, :, colstart:colstart + 256]
        c.dma("pool", sem, lambda: pool.dma_start(out=dst.bf().rearrange("p (k n) -> p k n", k=8), in_=src), writes=[dst])

    def pieces_of(seqs):
        out = []
        for (c0, T) in seqs:
            for o in range(0, T, 512):
                out.append((c0 + o, min(512, T - o)))
        return out

    def l0_pre_seq(g, wu, col0, T, N2, s1r, slots, dest, u_tm, Yt, Qt):
        NI = 128 // N2
        wuv = wu.bf().rearrange("p (k n) -> p k n", k=8)
        utv = u_tm.bf().rearrange("p (m n) -> p m n", n=256)
        s1v = s1r.bf().rearrange("p (m n) -> p m n", n=256)
        Ytv = Yt.bf().rearrange("p (c q j m i) -> p c q j m i", c=2, q=2, j=N2, m=N2)
        for m2 in range(N2):
            b, ps, pk = bank()
            for kc in range(8):
                c.op("pe", lambda kc=kc, m2=m2: pe.matmul(ps[:, 0:256], lhsT=hTv[:, kc, col0 + m2:col0 + T:N2], rhs=wuv[:, kc, :], start=(kc == 0), stop=(kc == 7)),
                     reads=[hTk(kc, col0, T), wu], writes=[pk])
            ur = u_tm.sub(m2 * 128, 128)
            evac_copy(utv[:, m2, :], ps[:, 0:256], [pk], [ur])
        chk("pre_u")
        for m2 in range(N2):
            ur = u_tm.sub(m2 * 128, 128)
            for cc in range(2):
                b, ps, pk = bank()
                c.op("pe", lambda m2=m2, cc=cc: pe.matmul(ps[:, 0:256], lhsT=utv[:, m2, cc * 128:(cc + 1) * 128], rhs=s1v[:, m2, :], start=True, stop=True),
                     reads=[ur, s1r.sub(m2 * 128, 128)], writes=[pk])
                evac_copy(Ytv[:, cc, :, :, m2, :], ps[:, 0:256].rearrange("p (q j i) -> p q j i", q=2, j=N2), [pk], [Yt])
        chk("pre_s1")
        chv = chtab.bf().rearrange("p (q c n) -> p q c n", q=2, c=2)
        def s2_stage(j, q):
            b2, ps2, pk2 = bank()
            for si, (Kr, Ki, flip) in enumerate(slots):
                for lc in range(2):
                    o = ps2[:, (si * 2 + lc) * 128:(si * 2 + lc + 1) * 128]
                    c.op("pe", lambda o=o, lc=lc, Kr=Kr: pe.matmul(o, lhsT=q.bf()[:, lc * 128:(lc + 1) * 128], rhs=Kr.bf(), start=True, stop=False),
                         reads=[q, Kr], writes=[pk2])
                    c.op("pe", lambda o=o, lc=lc, Ki=Ki: pe.matmul(o, lhsT=q.bf()[:, 256 + lc * 128:256 + (lc + 1) * 128], rhs=Ki.bf(), start=False, stop=True),
                         reads=[q, Ki], writes=[pk2])
            for si, (Kr, Ki, flip) in enumerate(slots):
                jb = (N2 - 1 - j) if flip else j
                for lc in range(2):
                    o = ps2[:, (si * 2 + lc) * 128:(si * 2 + lc + 1) * 128].rearrange("p (k i) -> p k i", i=NI)
                    dap, dreg = dest(si, lc)
                    d = dap.rearrange("p (k c) -> p k c", c=128)[:, :, jb * NI:(jb + 1) * NI]
                    evac_copy(d, o, [pk2], [dreg], flip=(si == 0 and lc == 0))

        prev = None
        for j in range(N2):
            b, ps, pk = bank()
            n = 0
            for cc in range(2):
                for comp in range(2):
                    c.op("pe", lambda cc=cc, comp=comp, n=n: pe.matmul(ps[:, :], lhsT=Ytv[:, cc, comp, j, :, :].rearrange("p m i -> p (m i)"), rhs=chv[:, comp, cc, :], start=(n == 0), stop=(n == 3)),
                         reads=[Yt, chtab], writes=[pk])
                    n += 1
            q = Qt[j % len(Qt)]
            evac_copy(q.bf(), ps[:, :], [pk], [q])
            if prev is not None:
                s2_stage(*prev)
            prev = (j, q)
        s2_stage(*prev)

    def wout_load(l, hd, gate_bc, wstg, wob):
        src = w_out[l][hd * 256:(hd + 1) * 256, :].rearrange("(l p) d -> p l d", p=128)
        if wstg is not None:
            c.dma("sp", "wo", lambda: sp.dma_start(out=wstg.f32().rearrange("p (l d) -> p l d", l=2), in_=src), writes=[wstg])
            for lc in range(2):
                c.op("pool", lambda lc=lc: pool.tensor_tensor(out=wob.bf()[:, lc * 1024:(lc + 1) * 1024], in0=wstg.f32()[:, lc * 1024:(lc + 1) * 1024], in1=gate_bc.f32(), op=ALU.mult),
                     reads=[wstg.sub(lc * 1024, 1024), gate_bc], writes=[wob.sub(lc * 512, 512)])
        else:
            c.dma("pool", "wo", lambda: pool.dma_start(out=wob.bf().rearrange("p (l d) -> p l d", l=2), in_=src), writes=[wob])
            for lc in range(2):
                c.op("pool", lambda lc=lc: pool.tensor_tensor(out=wob.bf()[:, lc * 1024:(lc + 1) * 1024], in0=wob.bf()[:, lc * 1024:(lc + 1) * 1024], in1=gate_bc.f32(), op=ALU.mult),
                     reads=[wob.sub(lc * 512, 512), gate_bc], writes=[wob.sub(lc * 512, 512)])

    def wout_update(l, hd, gT, gate_bc, wstg, wob, seqs_tiles, load=True):
        if load:
            wout_load(l, hd, gate_bc, wstg, wob)
        gv = gT.bf().rearrange("p (l t) -> p l t", l=2)
        Tt = gT.n
        for (tile, col) in seqs_tiles:
            for half in range(2):
                b, ps, pk = bank()
                for lc in range(2):
                    c.op("pe", lambda lc=lc, col=col, half=half: pe.matmul(ps[:, :], lhsT=gv[:, lc, col:col + 128], rhs=wob.bf()[:, lc * 1024 + half * 512:lc * 1024 + (half + 1) * 512], start=(lc == 0), stop=(lc == 1)),
                         reads=[gT.sub(lc * (Tt // 2) + col // 2, 64), wob.sub(lc * 512 + half * 256, 256)], writes=[pk])
                xr = x1t(tile).sub(half * 512, 512)
                c.op("dve", lambda xr=xr: dve.tensor_tensor(out=xr.f32(), in0=ps[:, :], in1=xr.f32(), op=ALU.add), reads=[pk, xr], writes=[xr])

    def l0_post_group(g, Ttot, seqs, fT, wz, sz, wf, gT, gate_bc, wstg, wob, tile0):
        wzv = wz.bf().rearrange("p (k n) -> p k n", k=8)
        szv = sz.bf().rearrange("p (l t) -> p l t", l=2)
        fv = fT.bf().rearrange("p (l t) -> p l t", l=2)
        gv = gT.bf().rearrange("p (l t) -> p l t", l=2)
        wfv = wf.bf().rearrange("p (l n) -> p l n", l=2)
        pcs = pieces_of(seqs)
        for jc in range(2):
            for (c0, n) in pcs:
                b, ps, pk = bank()
                for kc in range(8):
                    c.op("pe", lambda kc=kc, jc=jc, c0=c0, n=n: pe.matmul(ps[:, 0:n], lhsT=wzv[:, kc, jc * 128:(jc + 1) * 128], rhs=hTv[:, kc, c0:c0 + n], start=(kc == 0), stop=(kc == 7)),
                         reads=[wz, hTk(kc, c0, n)], writes=[pk])
                r = sz.sub(jc * (Ttot // 2) + c0 // 2, n // 2)
                c.op("act", lambda jc=jc, c0=c0, n=n: act.activation(out=szv[:, jc, c0:c0 + n], in_=ps[:, 0:n], func=AF.Silu), reads=[pk], writes=[r])
        for jc in range(2):
            for (c0, n) in pcs:
                b, ps, pk = bank()
                for lc in range(2):
                    c.op("pe", lambda lc=lc, jc=jc, c0=c0, n=n: pe.matmul(ps[:, 0:n], lhsT=wfv[:, lc, jc * 128:(jc + 1) * 128], rhs=fv[:, lc, c0:c0 + n], start=(lc == 0), stop=(lc == 1)),
                         reads=[wf, fT.sub(lc * (Ttot // 2) + c0 // 2, n // 2)], writes=[pk])
                rs_ = sz.sub(jc * (Ttot // 2) + c0 // 2, n // 2)
                rg = gT.sub(jc * (Ttot // 2) + c0 // 2, n // 2)
                bcol = bfm.f32()[:, g * 2 + jc:g * 2 + jc + 1]
                c.op("dve", lambda jc=jc, c0=c0, n=n, bcol=bcol: dve.scalar_tensor_tensor(out=gv[:, jc, c0:c0 + n], in0=ps[:, 0:n], scalar=bcol, in1=szv[:, jc, c0:c0 + n], op0=ALU.add, op1=ALU.mult),
                     reads=[pk, bfm, rs_], writes=[rg])
        tiles = [(tile0 + i, i * 128) for i in range(Ttot // 128)]
        wout_update(0, g, gT, gate_bc, wstg, wob, tiles)

    def load_tables():
        t = {}
        t["s1s"] = c.alloc(16 * 128); t["s1p"] = c.alloc(2 * 128)
        t["k"] = [c.alloc(64) for _ in range(6)]
        t["ch"] = c.alloc(2 * 2 * 256)
        c.dma("pool", "tab", lambda: pool.dma_start(out=t["s1s"].bf().rearrange("p (m n) -> p m n", m=16), in_=s1s_d.rearrange("m p n -> p m n")), writes=[t["s1s"]])
        c.dma("pool", "tab", lambda: pool.dma_start(out=t["s1p"].bf().rearrange("p (m n) -> p m n", m=2), in_=s1p_d.rearrange("m p n -> p m n")), writes=[t["s1p"]])
        for i in range(6):
            c.dma("pool", "tab", lambda i=i: pool.dma_start(out=t["k"][i].bf(), in_=kmat_d[i]), writes=[t["k"][i]])
        c.dma("pool", "tab", lambda: pool.dma_start(out=t["ch"].bf().rearrange("p (q c n) -> p q c n", q=2, c=2), in_=chtab_d.rearrange("q (c p) n -> p q c n", p=128)), writes=[t["ch"]])
        return t

    chk("setup")
    mkL0 = c.mark()
    tabs = load_tables()
    chtab = tabs["ch"]
    front_end(16, 0, 0, 0, xsrc=xs_in, possrc=pos_in, xdst=None)
    ada_layer(1)
    dump("hT_in", lambda: hT.bf(), [128, 8 * 2048], [hT], BF16)
    chk("fe_in")
    mk = c.mark()
    wus = [c.alloc(1024), c.alloc(1024)]
    u_tm = c.alloc(16 * 128); Yt = c.alloc(4096); Qt = [c.alloc(256) for _ in range(3)]
    send = [c.alloc(4096), c.alloc(4096)]
    slotsS = [(tabs["k"][0], tabs["k"][1], False), (tabs["k"][2], tabs["k"][3], True)]
    load_w_in(0, 0, wus[0], "wu0")
    for t in range(16):
        xr = x1t(t)
        c.dma("sp", "xres", lambda xr=xr, t=t: sp.dma_start(out=xr.f32(), in_=xs_res[t * 128:(t + 1) * 128, :]), writes=[xr])
    for g in range(8):
        if g + 1 < 8:
            load_w_in(0, (g + 1) * 256, wus[(g + 1) % 2], f"wu{(g + 1) % 2}")
        sd = send[g % 2]
        sdv = sd.bf().rearrange("p (s l t) -> p s l t", s=2, l=2)

        def dest(si, lc, sd=sd, sdv=sdv):
            return sdv[:, si, lc, :], sd.sub((si * 2 + lc) * 1024, 1024)
        l0_pre_seq(g, wus[g % 2], 0, TS, 16, tabs["s1s"], slotsS, dest, u_tm, Yt, Qt)
        chk("pre_s2")
        c.dma("sp", f"send{g % 2}", lambda g=g, sdv=sdv: sp.dma_start(out=rs_in[g].rearrange("(s l p) t -> p s l t", s=2, l=2), in_=sdv),
              reads=[sd], writes=[("rs_in", g)])
        chk("pre_send")
        c.cc(f"rs{g}", lambda g=g: pool.collective_compute("ReduceScatter", ALU.add, replica_groups=RG, ins=[rs_in[g]], outs=[rs_out[g]]),
             reads=[("rs_in", g)], writes=[("rs_out", g)])
        for t in (2 * g, 2 * g + 1):
            xr = x1t(t)
            c.dma("pool", "xres", lambda xr=xr, t=t: pool.dma_start(out=xr.f32(), in_=pos_res[t * 128:(t + 1) * 128, :], accum_op=ALU.add), reads=[xr], writes=[xr])
        chk("pre_rs")
    c.release(mk)
    chk("l0pre")
    front_end(16, 0, 0, 0, xsrc=None, possrc=None, xdst=x1t)
    chk("fe_res")
    mk = c.mark()
    gate_bc = c.alloc(1024); wstg = c.alloc(2048); wob = c.alloc(1024)
    wzs = [c.alloc(1024), c.alloc(1024)]
    sz = c.alloc(2048); fTs = [c.alloc(2048), c.alloc(2048)]; wf = [c.alloc(256), c.alloc(256)]; gT = c.alloc(2048)
    bcast_ada(0, 0, "gate", gate_bc)

    def load_post(g):
        load_w_in(0, DI + g * 256, wzs[g % 2], f"wz{g % 2}")
        c.dma("pool", f"wf{g % 2}", lambda: pool.dma_start(out=wf[g % 2].bf().rearrange("p (l n) -> p l n", l=2), in_=w_fmix[g].rearrange("(l p) n -> p l n", p=128)), writes=[wf[g % 2]])
        c.dma("sp", f"ft{g % 2}", lambda: sp.dma_start(out=fTs[g % 2].bf().rearrange("p (l t) -> p l t", l=2), in_=rs_out[g].rearrange("(l p) t -> p l t", p=128)),
              reads=[("rs_out", g)], writes=[fTs[g % 2]])
    load_post(0)
    for g in range(8):
        if g + 1 < 8:
            load_post(g + 1)
        l0_post_group(g, TS, [(0, TS)], fTs[g % 2], wzs[g % 2], sz, wf[g % 2], gT, gate_bc, wstg, wob, 0)
    c.release(mk)
    c.release(mkL0)
    dump("x1_l0", lambda: x1.f32(), [128, 16 * 1024], [x1])
    chk("l0post")

    def layer1(Ttot, seqs, j, is_sample, tile0):
        ntiles = Ttot // 128
        front_end(ntiles, 0, 1, j, xdst=lambda t: x1t(tile0 + t))
        mk = c.mark()
        RL = (sum(T + 4 for (_, T) in seqs) + BLK - 1) // BLK * BLK
        Rall = c.alloc(5 * RL)
        row = [Rall.sub(i * RL, RL) for i in range(5)]
        RLB = (RL // 2 + BLK - 1) // BLK * BLK
        U = [c.alloc(RLB), c.alloc(RLB)]
        TA = [row[0].sub(0, Ttot), row[2].sub(0, Ttot)]
        TI = [row[1].sub(0, Ttot), row[3].sub(0, Ttot)]
        ts = row[4].sub(0, Ttot)
        h1 = c.alloc(Ttot)
        gate_bc = c.alloc(1024); wob = c.alloc(1024)
        wu = c.alloc(1024); wz = c.alloc(1024)
        gw = [[c.alloc(256) for _ in range(4)] for _ in range(2)]
        ucbf = c.alloc(Ttot); sz = c.alloc(Ttot); gT = c.alloc(Ttot)
        bcast_ada(1, j, "gate", gate_bc)
        pcs = pieces_of(seqs)
        pads = []
        po = 0
        for (c0, T) in seqs:
            pads.append(po)
            po += T + 4

        if is_sample:
            wuh = [wu, wz]
            b, ps, pk = bank()
            for hd in range(8):
                load_w_in(1, hd * 256, wuh[hd % 2], f"wu{hd % 2}")
                wv = wuh[hd % 2].bf().rearrange("p (k n) -> p k n", k=8)
                for jc in range(2):
                    ch = hd * 2 + jc
                    for kc in range(8):
                        c.op("pe", lambda kc=kc, jc=jc, ch=ch, wv=wv: pe.matmul(ps[:, ch * 2:ch * 2 + 2], lhsT=wv[:, kc, jc * 128:(jc + 1) * 128], rhs=hTv[:, kc, Ttot - 2:Ttot], start=(kc == 0), stop=(kc == 7)),
                             reads=[wuh[hd % 2], hTk(kc, Ttot - 2, 2)], writes=[pk])
            c.op("dve", lambda: dve.tensor_copy(out=halo.f32(), in_=ps[:, 0:32]), reads=[pk], writes=[halo])
            c.dma("sp", "halo", lambda: sp.dma_start(out=halo_in, in_=halo.f32()), reads=[halo], writes=[("halo_in",)])
            c.cc("halocc", lambda: pool.collective_compute("AllGather", ALU.bypass, replica_groups=RG, ins=[halo_in], outs=[halo_out]),
                 reads=[("halo_in",)], writes=[("halo_out",)])
            c.dma("sp", "halo", lambda: sp.dma_start(out=hg.f32().rearrange("p (r n) -> p r n", r=2), in_=halo_out.rearrange("(r p) n -> p r n", p=128)),
                  reads=[("halo_out",)], writes=[hg])
            c.op("dve", lambda: dve.tensor_scalar(out=ctmp.f32(), in0=hg.f32()[:, 0:32], scalar1=selr.f32()[:, 0:1], scalar2=None, op0=ALU.mult), reads=[hg, selr], writes=[ctmp])
            c.op("dve", lambda: dve.scalar_tensor_tensor(out=halo.f32(), in0=hg.f32()[:, 32:64], scalar=selr.f32()[:, 1:2], in1=ctmp.f32(), op0=ALU.mult, op1=ALU.add),
                 reads=[hg, selr, ctmp], writes=[halo])

        wuv = wu.bf().rearrange("p (k n) -> p k n", k=8)
        wzv = wz.bf().rearrange("p (k n) -> p k n", k=8)
        ucv = ucbf.bf().rearrange("p (l t) -> p l t", l=2)
        szv = sz.bf().rearrange("p (l t) -> p l t", l=2)
        gv = gT.bf().rearrange("p (l t) -> p l t", l=2)
        h1v = h1.bf().rearrange("p (l t) -> p l t", l=2)

        def h1r(jc, c0, n):
            return h1.sub(jc * (Ttot // 2) + c0 // 2, max(1, (n + 1) // 2))

        def load_wu(hd):
            load_w_in(1, hd * 256, wu, "wu1h")

        def loads(hd):
            load_w_in(1, DI + hd * 256, wz, "wz1h")
            gws = gw[hd % 2]
            for d in range(2):
                for gi, wsrc in enumerate((wr_d, wi_d)):
                    r = gws[d * 2 + gi]
                    c.dma("pool", f"gw{hd % 2}", lambda r=r, wsrc=wsrc, d=d: pool.dma_start(out=r.bf().rearrange("p (i n) -> p i n", i=2), in_=wsrc[d, hd].rearrange("(i p) n -> p i n", p=128)), writes=[r])

        def part_u(hd):
            for jc in range(2):
                for si, (c0, T) in enumerate(seqs):
                    for o in range(0, T, 512):
                        n = min(512, T - o)
                        b, ps, pk = bank()
                        for kc in range(8):
                            c.op("pe", lambda kc=kc, jc=jc, c0=c0, o=o, n=n: pe.matmul(ps[:, 0:n], lhsT=wuv[:, kc, jc * 128:(jc + 1) * 128], rhs=hTv[:, kc, c0 + o:c0 + o + n], start=(kc == 0), stop=(kc == 7)),
                                 reads=[wu, hTk(kc, c0 + o, n)], writes=[pk])
                        dcol = pads[si] + 2 + o
                        evac_copy(U[jc].bf()[:, dcol:dcol + n], ps[:, 0:n], [pk], [U[jc].sub(dcol // 2, (n + 1) // 2 + 1)])

        def part_z(hd):
            for jc in range(2):
                for (c0, n) in pcs:
                    b, ps, pk = bank()
                    for kc in range(8):
                        c.op("pe", lambda kc=kc, jc=jc, c0=c0, n=n: pe.matmul(ps[:, 0:n], lhsT=wzv[:, kc, jc * 128:(jc + 1) * 128], rhs=hTv[:, kc, c0:c0 + n], start=(kc == 0), stop=(kc == 7)),
                             reads=[wz, hTk(kc, c0, n)], writes=[pk])
                    c.op("act", lambda jc=jc, c0=c0, n=n: act.activation(out=szv[:, jc, c0:c0 + n], in_=ps[:, 0:n], func=AF.Silu), reads=[pk], writes=[sz.sub(jc * (Ttot // 2) + c0 // 2, n // 2)])

        def part_conv(hd):
            for jc in range(2):
                ch = hd * 2 + jc
                Ub = U[jc].bf()
                for si, (c0, T) in enumerate(seqs):
                    p0 = pads[si]
                    c.op("dve", lambda Ub=Ub, p0=p0: dve.memset(Ub[:, p0:p0 + 2], 0.0), writes=[U[jc].sub(p0 // 2, 2)])
                    if not is_sample:
                        c.op("dve", lambda Ub=Ub, p0=p0, T=T: dve.memset(Ub[:, p0 + T + 2:p0 + T + 4], 0.0), writes=[U[jc].sub((p0 + T + 2) // 2, 2)])
                if is_sample:
                    T = seqs[0][1]
                    for e in range(2):
                        c.op("dve", lambda Ub=Ub, ch=ch, e=e, T=T: dve.tensor_copy(out=Ub[:, T + 2 + e:T + 3 + e], in_=halo.f32()[:, ch * 2 + 1 - e:ch * 2 + 2 - e]),
                             reads=[halo], writes=[U[jc].sub((T + 2 + e) // 2, 1)])
                plist = []
                for si, (c0, T) in enumerate(seqs):
                    for o in range(0, T, 512):
                        plist.append((pads[si] + o, c0 + o, min(512, T - o)))
                banks = [bank() for _ in plist]
                for dd in range(5):
                    dg = diagb[dd]
                    c.op("dve", lambda dg=dg, ch=ch, dd=dd: dve.tensor_scalar(out=dg.bf(), in0=identb.bf(), scalar1=convw.f32()[:, ch * 5 + dd:ch * 5 + dd + 1], scalar2=None, op0=ALU.mult),
                         reads=[identb, convw], writes=[dg])
                    for (pcol, c0o, n), (b, ps, pk) in zip(plist, banks):
                        c.op("pe", lambda dg=dg, Ub=Ub, pcol=pcol, n=n, dd=dd, ps=ps: pe.matmul(ps[:, 0:n], lhsT=dg.bf(), rhs=Ub[:, pcol + dd:pcol + dd + n], start=(dd == 0), stop=(dd == 4)),
                             reads=[dg, U[jc].sub((pcol + dd) // 2, (n + 1) // 2 + 1)], writes=[pk])
                for (pcol, c0o, n), (b, ps, pk) in zip(plist, banks):
                    c.op("act", lambda ps=ps, c0o=c0o, n=n, ch=ch, jc=jc: act.activation(out=ucv[:, jc, c0o:c0o + n], in_=ps[:, 0:n], func=AF.Identity, bias=convb.f32()[:, ch:ch + 1]),
                         reads=[pk, convb], writes=[ucbf.sub(jc * (Ttot // 2) + c0o // 2, n // 2)])

        def stage(hd, d, jc, part="all"):
            gws = gw[hd % 2]
            ta, ti = TA[jc], TI[jc]
            ch = hd * 2 + jc
            colc = slice(d * 16 + ch, d * 16 + ch + 1)
            if part in ("all", "front"):
                stage_front(hd, d, jc, gws, ta, ti, ch, colc)
            if part in ("all", "back"):
                stage_back(hd, d, jc, ta, ti, ch)

        def stage_front(hd, d, jc, gws, ta, ti, ch, colc):
            for gi, (dst, bcol) in enumerate(((ta, brc), (ti, bic))):
                wv = gws[d * 2 + gi].bf().rearrange("p (i n) -> p i n", i=2)
                for (c0, n) in pcs:
                    b, ps, pk = bank()
                    for ic in range(2):
                        c.op("pe", lambda ic=ic, c0=c0, n=n, wv=wv: pe.matmul(ps[:, 0:n], lhsT=wv[:, ic, jc * 128:(jc + 1) * 128], rhs=ucv[:, ic, c0:c0 + n], start=(ic == 0), stop=(ic == 1)),
                             reads=[gws[d * 2 + gi], ucbf.sub(ic * (Ttot // 2) + c0 // 2, n // 2)], writes=[pk])
                    c.op("act", lambda c0=c0, n=n, dst=dst, bcol=bcol: act.activation(out=dst.f32()[:, c0:c0 + n], in_=ps[:, 0:n], func=AF.Tanh, scale=0.5, bias=bcol.f32()[:, d * 16 + ch:d * 16 + ch + 1]),
                         reads=[pk, bcol], writes=[dst.sub(c0, n)])
            c.op("act", lambda: act.activation(out=ts.f32(), in_=ta.f32(), func=AF.Exp, scale=cl.f32()[:, colc], bias=cl.f32()[:, colc]), reads=[ta, cl], writes=[ts])
            c.op("act", lambda: act.activation(out=ta.f32(), in_=ta.f32(), func=AF.Exp, scale=cl2.f32()[:, colc], bias=cl2.f32()[:, colc]), reads=[ta, cl2], writes=[ta])
            c.op("act", lambda: act.activation(out=ts.f32(), in_=ts.f32(), func=AF.Sqrt, scale=-1.0, bias=1.0), reads=[ts], writes=[ts])

        def stage_back(hd, d, jc, ta, ti, ch):
            c.op("dve", lambda: dve.scalar_tensor_tensor(out=ti.f32(), in0=ti.f32(), scalar=1.0, in1=ts.f32(), op0=ALU.add, op1=ALU.mult), reads=[ti, ts], writes=[ti])
            c.op("dve", lambda: dve.scalar_tensor_tensor(out=ti.f32(), in0=ti.f32(), scalar=0.5, in1=ucv[:, jc, :], op0=ALU.mult, op1=ALU.mult), reads=[ti, ucbf.sub(jc * (Ttot // 2), Ttot // 2)], writes=[ti])
            if d == 0:
                for si, (c0, T) in enumerate(seqs):
                    init = st0.f32()[:, ch:ch + 1] if is_sample else zcol.f32()
                    c.op("dve", lambda c0=c0, T=T, init=init: dve.tensor_tensor_scan(out=h1v[:, jc, c0:c0 + T], data0=ta.f32()[:, c0:c0 + T], data1=ti.f32()[:, c0:c0 + T], initial=init, op0=ALU.mult, op1=ALU.add),
                         reads=[ta, ti, st0, zcol], writes=[h1r(jc, c0, T)])
                    if is_sample:
                        c.op("dve", lambda c0=c0, T=T: dve.tensor_copy(out=c1.f32()[:, jc:jc + 1], in_=h1v[:, jc, c0 + T - 1:c0 + T]), reads=[h1r(jc, c0 + T - 2, 2)], writes=[c1])
                    else:
                        col = si * 32 + ch
                        c.op("dve", lambda c0=c0, T=T, col=col: dve.tensor_copy(out=stpo.f32()[:, col:col + 1], in_=h1v[:, jc, c0 + T - 1:c0 + T]), reads=[h1r(jc, c0 + T - 2, 2)], writes=[stpo.sub(col, 1)])
            else:
                if is_sample and jc == 0:
                    c.op("dve", lambda: dve.tensor_scalar(out=ctmp.f32()[:, 0:2], in0=cg.f32()[:, 0:2], scalar1=selr.f32()[:, 0:1], scalar2=None, op0=ALU.mult), reads=[cg, selr], writes=[ctmp])
                    c.op("dve", lambda: dve.scalar_tensor_tensor(out=c2.f32(), in0=cg.f32()[:, 2:4], scalar=selr.f32()[:, 1:2], in1=ctmp.f32()[:, 0:2], op0=ALU.mult, op1=ALU.add),
                         reads=[cg, selr, ctmp], writes=[c2])
                for si, (c0, T) in enumerate(seqs):
                    init = c2.f32()[:, jc:jc + 1] if is_sample else zcol.f32()
                    c.op("dve", lambda c0=c0, T=T, init=init: dve.tensor_tensor_scan(out=ti.f32()[:, c0:c0 + T][:, ::-1], data0=ta.f32()[:, c0:c0 + T][:, ::-1], data1=ti.f32()[:, c0:c0 + T][:, ::-1], initial=init, op0=ALU.mult, op1=ALU.add),
                         reads=[ta, ti, c2, zcol], writes=[ti])
                    if not is_sample:
                        col = si * 32 + 16 + ch
                        c.op("dve", lambda c0=c0, col=col: dve.tensor_copy(out=stpo.f32()[:, col:col + 1], in_=ti.f32()[:, c0:c0 + 1]), reads=[ti], writes=[stpo.sub(col, 1)])
                c.op("dve", lambda: dve.tensor_tensor(out=ti.f32(), in0=ti.f32(), in1=h1v[:, jc, :], op=ALU.add), reads=[ti, h1r(jc, 0, Ttot)], writes=[ti])
                c.op("dve", lambda: dve.tensor_tensor(out=gv[:, jc, :], in0=ti.f32(), in1=szv[:, jc, :], op=ALU.mult), reads=[ti, sz.sub(jc * (Ttot // 2), Ttot // 2)], writes=[gT.sub(jc * (Ttot // 2), Ttot // 2)])

        def exchange_send(hd):
            c.dma("sp", "car", lambda: sp.dma_start(out=car_in[hd], in_=c1.f32()), reads=[c1], writes=[("car_in", hd)])
            c.cc(f"carcc{hd}", lambda: pool.collective_compute("AllGather", ALU.bypass, replica_groups=RG, ins=[car_in[hd]], outs=[car_out[hd]]),
                 reads=[("car_in", hd)], writes=[("car_out", hd)])
            c.dma("sp", "car", lambda: sp.dma_start(out=cg.f32().rearrange("p (r n) -> p r n", r=2), in_=car_out[hd].rearrange("(r p) n -> p r n", p=128)),
                  reads=[("car_out", hd)], writes=[cg])

        tiles = [(tile0 + i, i * 128) for i in range(ntiles)]
        load_wu(0)
        loads(0)
        part_u(0)
        part_z(0)
        part_conv(0)
        for hd in range(8):
            if hd + 1 < 8:
                load_wu(hd + 1)
            stage(hd, 0, 0, "front")
            if hd > 0:
                wout_update(1, hd - 1, gT, gate_bc, None, wob, tiles, load=False)
            stage(hd, 0, 0, "back")
            stage(hd, 0, 1)
            if hd + 1 < 8:
                part_u(hd + 1)
            if is_sample:
                exchange_send(hd)
            wout_load(1, hd, gate_bc, None, wob)
            stage(hd, 1, 0)
            if hd + 1 < 8:
                loads(hd + 1)
            stage(hd, 1, 1)
            if hd + 1 < 8:
                part_z(hd + 1)
                part_conv(hd + 1)
        wout_update(1, 7, gT, gate_bc, None, wob, tiles, load=False)
        c.release(mk)

    def final_out(ntiles, tile0, dst):
        mk = c.mark()
        fg_bc = c.alloc(1024)
        tmp = [c.alloc(1024), c.alloc(1024)]
        c.dma("sp", "bc", lambda: sp.dma_start(out=fg_bc.f32(), in_=fgr_d[0, :].partition_broadcast(128)), writes=[fg_bc])
        for t in range(ntiles):
            s = t % 2
            xr = x1t(tile0 + t)
            ssr = ss.sub(t, 1); rsr = rstd.sub(t, 1)
            c.op("act", lambda xr=xr, s=s, ssr=ssr: act.activation(out=tmp[s].f32(), in_=xr.f32(), func=AF.Square, accum_out=ssr.f32()), reads=[xr], writes=[tmp[s], ssr])
            c.op("act", lambda ssr=ssr, rsr=rsr: act.activation(out=rsr.f32(), in_=ssr.f32(), func=AF.Sqrt, scale=1.0 / D, bias=EPS), reads=[ssr], writes=[rsr])
            c.op("dve", lambda rsr=rsr: dve.reciprocal(out=rsr.f32(), in_=rsr.f32()), reads=[rsr], writes=[rsr])
            c.op("dve", lambda xr=xr, s=s, rsr=rsr: dve.scalar_tensor_tensor(out=tmp[s].f32(), in0=xr.f32(), scalar=rsr.f32(), in1=fg_bc.f32(), op0=ALU.mult, op1=ALU.mult),
                 reads=[xr, rsr, fg_bc], writes=[tmp[s]])
            c.dma("sp", f"out{s}", lambda s=s, t=t: sp.dma_start(out=dst[t * 128:(t + 1) * 128, :], in_=tmp[s].f32()), reads=[tmp[s]], writes=[("dout", id(dst), t)])
        c.release(mk)

    layer1(TS, [(0, TS)], 0, True, 0)
    dump("x1_l1", lambda: x1.f32(), [128, 16 * 1024], [x1])
    chk("l1s")
    final_out(16, 0, ys)
    chk("sample")

    mkP = c.mark()
    tabs = load_tables()
    chtab = tabs["ch"]
    front_end(4, 0, 0, 1, xsrc=xp, possrc=None, xdst=x1t)
    mk = c.mark()
    TT = 2 * TP
    wus = [c.alloc(1024), c.alloc(1024)]
    u_tm = c.alloc(2 * 128); Yt = c.alloc(512); Qt = [c.alloc(256) for _ in range(3)]
    gate_bc = c.alloc(1024); wstg = c.alloc(2048); wob = c.alloc(1024)
    wzs = [c.alloc(1024), c.alloc(1024)]
    sz = c.alloc(TT); fT = c.alloc(TT); wf = [c.alloc(256), c.alloc(256)]; gT = c.alloc(TT)
    bcast_ada(0, 1, "gate", gate_bc)
    slotsP = [(tabs["k"][4], tabs["k"][5], False)]
    fv = fT.bf().rearrange("p (l t) -> p l t", l=2)
    for g in range(8):
        load_w_in(0, g * 256, wus[g % 2], f"wu{g % 2}")
        load_w_in(0, DI + g * 256, wzs[g % 2], f"wz{g % 2}")
        c.dma("pool", f"wf{g % 2}", lambda g=g: pool.dma_start(out=wf[g % 2].bf().rearrange("p (l n) -> p l n", l=2), in_=w_fmix[g].rearrange("(l p) n -> p l n", p=128)), writes=[wf[g % 2]])
        for q in range(2):
            def dest(si, lc, q=q):
                return fv[:, lc, q * TP:(q + 1) * TP], fT.sub(lc * (TT // 2) + q * TP // 2, TP // 2)
            l0_pre_seq(g, wus[g % 2], q * TP, TP, 2, tabs["s1p"], slotsP, dest, u_tm, Yt, Qt)
        l0_post_group(g, TT, [(0, TP), (TP, TP)], fT, wzs[g % 2], sz, wf[g % 2], gT, gate_bc, wstg, wob, 0)
    c.release(mk)
    c.release(mkP)
    layer1(TT, [(0, TP), (TP, TP)], 1, False, 0)
    c.dma("sp", "stpo", lambda: sp.dma_start(out=stp, in_=stpo.f32()), reads=[stpo], writes=[("stp",)])
    final_out(4, 0, yp)


def build_nc():
    nc0 = bass.Bass("TRN2", target_bir_lowering=False)
    with ExitStack() as st:
        c0 = Ctx(nc0, st, needed=None)
        _emit(nc0, c0)
        needed = c0.need_out
    nc = bass.Bass("TRN2", target_bir_lowering=False)
    with ExitStack() as st:
        c = Ctx(nc, st, needed=needed)
        _emit(nc, c)
        print(f"[kernel] insts={c.ninst} waits={c.nwaits} incs={ {e: len(needed[e]) for e in needed} } arena_top={c.top}")
    return nc


def _col(v, nchunks):
    return np.ascontiguousarray(np.asarray(v, np.float32).reshape(nchunks, 128).T)


_NC_CACHE = {}
_RETURN_MAPS = False


def kernel(x_prompt, x_sample, state_lru_1, c, c_ctx,
           norm_g_0, w_ada_0, b_ada_0, w_in_0, w_fmix_0, b_fmix_0, w_out_0,
           norm_g_1, w_ada_1, b_ada_1, w_in_1, conv_w_1, conv_b_1,
           w_rgate_1, b_rgate_1, w_igate_1, b_igate_1, lam_1, w_out_1,
           final_g):
    f = lambda a: np.ascontiguousarray(np.asarray(a, np.float32))
    x_prompt, x_sample, state_lru_1, c, c_ctx = map(f, (x_prompt, x_sample, state_lru_1, c, c_ctx))
    pos = pos_embed(4096, D)
    ident = np.eye(128, dtype=np.float32)
    cht = ch_tables().astype(np.float32)
    tabs_s = [dft_tables_sample(s) for s in (0, 1)]
    tabs_p = [dft_tables_prompt(bool(s)) for s in (0, 1)]
    common = {
        "w_ada_0": f(w_ada_0), "w_ada_1": f(w_ada_1), "badar_0": f(b_ada_0).reshape(1, 3072), "badar_1": f(b_ada_1).reshape(1, 3072),
        "ngr_0": f(norm_g_0).reshape(1, 1024), "ngr_1": f(norm_g_1).reshape(1, 1024), "fgr": f(final_g).reshape(1, 1024),
        "w_in_0": f(w_in_0), "w_in_1": f(w_in_1), "w_fmix": f(w_fmix_0), "bfm": _col(b_fmix_0, 16),
        "w_out_0": f(w_out_0), "w_out_1": f(w_out_1), "convb": _col(conv_b_1, 16),
        "ident": ident, "chtab": cht,
    }
    cw = f(conv_w_1)
    zero = np.zeros_like(cw[0])
    in_maps = []
    for cid in range(NCORES):
        p, s = cid // 2, cid % 2
        m = dict(common)
        xs = x_sample[p]
        m["xs_in"] = np.ascontiguousarray(xs[s::2]); m["pos_in"] = np.ascontiguousarray(pos[s::2])
        if s == 0:
            m["xs_res"] = np.ascontiguousarray(xs[:TS]); m["pos_res"] = np.ascontiguousarray(pos[:TS])
        else:
            m["xs_res"] = np.ascontiguousarray(xs[::-1][:TS]); m["pos_res"] = np.ascontiguousarray(pos[::-1][:TS])
        xpp = x_prompt[2 * cid:2 * cid + 2]
        if s == 1:
            xpp = xpp[:, ::-1]
        m["xp"] = np.ascontiguousarray(xpp.reshape(2 * TP, D))
        condT = np.zeros((128, 16), np.float32)
        condT[:, 0::2] = _col(c[p], 8); condT[:, 1::2] = _col(c_ctx, 8)
        m["condT"] = condT
        sel = np.zeros((128, 2), np.float32); sel[:, 1 - s] = 1.0
        m["sel"] = sel
        m["st0"] = _col(state_lru_1[p, s], 16)
        taps = [cw[0], cw[1], cw[2], cw[3], zero] if s == 0 else [zero, cw[3], cw[2], cw[1], cw[0]]
        m["convw"] = np.ascontiguousarray(np.stack([_col(t, 16) for t in taps], axis=-1).reshape(128, 80))
        d1, d2 = (0, 1) if s == 0 else (1, 0)
        m["wr"] = np.ascontiguousarray(f(w_rgate_1)[[d1, d2]]); m["wi"] = np.ascontiguousarray(f(w_igate_1)[[d1, d2]])
        m["br"] = np.concatenate([_col(b_rgate_1[d1], 16), _col(b_rgate_1[d2], 16)], 1)
        m["bi"] = np.concatenate([_col(b_igate_1[d1], 16), _col(b_igate_1[d2], 16)], 1)
        m["lam"] = np.concatenate([_col(lam_1[d1], 16), _col(lam_1[d2], 16)], 1)
        s1, KrA, KiA, KrB, KiB = tabs_s[s]
        s1p, KrP, KiP = tabs_p[s]
        m["s1s"] = s1.astype(np.float32); m["s1p"] = s1p.astype(np.float32)
        m["kmat"] = np.stack([KrA, KiA, KrB, KiB, KrP, KiP]).astype(np.float32)
        in_maps.append({k: np.ascontiguousarray(v, dtype=np.float32) for k, v in m.items()})

    if _RETURN_MAPS:
        return in_maps
    if "nc" not in _NC_CACHE:
        _NC_CACHE["nc"] = build_nc()
    res = run_bass_kernel_spmd(_NC_CACHE["nc"], in_maps, core_ids=list(range(NCORES)))
    outs = res.results

    y_prompt = np.zeros((16, TP, D), np.float32)
    y_sample = np.zeros((4, 4096, D), np.float32)
    new_state = np.zeros((16, 2, DI), np.float32)
    for cid in range(NCORES):
        p, s = cid // 2, cid % 2
        ysv = np.asarray(outs[cid]["ys"], np.float32)
        ypv = np.asarray(outs[cid]["yp"], np.float32).reshape(2, TP, D)
        stv = np.asarray(outs[cid]["stp"], np.float32).reshape(128, 2, 2, 16)
        stv = stv.transpose(1, 2, 3, 0).reshape(2, 2, DI)
        if s == 0:
            y_sample[p, :TS] = ysv
            y_prompt[2 * cid:2 * cid + 2] = ypv
            new_state[2 * cid:2 * cid + 2] = stv
        else:
            y_sample[p, TS:] = ysv[::-1]
            y_prompt[2 * cid:2 * cid + 2] = ypv[:, ::-1]
            new_state[2 * cid:2 * cid + 2] = stv[:, ::-1]
    return (y_prompt, y_sample, new_state)
```

```python
import numpy as np
from contextlib import ExitStack
import concourse.bass as bass
import concourse.mybir as mybir
from concourse.bass_utils import run_bass_kernel_spmd

F32 = mybir.dt.float32
BF16 = mybir.dt.bfloat16
AF = mybir.ActivationFunctionType
ALU = mybir.AluOpType

D = 1024
DI = 2048
EPS = 1e-6
GRID_W = 64
POS_BASE = 10000.0
NCORES = 8
TS = 2048
TP = 256
NW = 53100
BLK = 64


def pos_embed(n_tokens, dim):
    rows = n_tokens // GRID_W
    r = np.broadcast_to(np.arange(rows, dtype=np.float32)[:, None], (rows, GRID_W)).reshape(-1)
    col = np.broadcast_to(np.arange(GRID_W, dtype=np.float32)[None, :], (rows, GRID_W)).reshape(-1)
    quarter = dim // 4
    omega = (1.0 / (np.float32(POS_BASE) ** (np.arange(quarter, dtype=np.float32) / np.float32(quarter)))).astype(np.float32)
    er = (r[:, None] * omega).astype(np.float32)
    ec = (col[:, None] * omega).astype(np.float32)
    return np.concatenate([np.sin(er), np.cos(er), np.sin(ec), np.cos(ec)], axis=-1).astype(np.float32)


def s1_table(S, N2, alpha, beta, gamma, delta):
    m1 = np.arange(128)[:, None]
    k1 = np.arange(128)[None, :]
    out = np.zeros((N2, 128, 256), np.float64)
    for m2 in range(N2):
        m = N2 * m1 + m2
        ph = -2 * np.pi * (((alpha * m + beta) * (gamma * k1 + delta)) % S) / S
        out[m2, :, :128] = np.cos(ph)
        out[m2, :, 128:] = np.sin(ph)
    return out / np.sqrt(128.0)


def s2_kron(S2c, N2, flip):
    NI = 128 // N2
    Kr = np.zeros((128, 128))
    Ki = np.zeros((128, 128))
    for m2 in range(N2):
        for k2 in range(N2):
            for i in range(NI):
                ip = (NI - 1 - i) if flip else i
                Kr[m2 * NI + i, k2 * NI + ip] = S2c[m2, k2].real
                Ki[m2 * NI + i, k2 * NI + ip] = -S2c[m2, k2].imag
    return Kr, Ki


def dft_tables_prompt(rev):
    S = 256
    N2 = 2
    if rev:
        alpha, beta, gamma, delta = -1, 255, -1, 255
    else:
        alpha, beta, gamma, delta = 1, 0, 1, 0
    s1 = s1_table(S, N2, alpha, beta, gamma, delta)
    m2 = np.arange(N2)[:, None]
    k2 = np.arange(N2)[None, :]
    S2c = np.exp(-2j * np.pi * ((alpha * gamma * 128 * m2 * k2 + beta * gamma * 128 * k2) % S) / S) / np.sqrt(S / 128.0)
    Kr, Ki = s2_kron(S2c, N2, False)
    return s1, Kr, Ki


def dft_tables_sample(sig):
    S = 4096
    N2 = 16
    s1 = s1_table(S, N2, 2, sig, 1, 0)
    m2 = np.arange(N2)[:, None]
    k2 = np.arange(N2)[None, :]
    nrm = np.sqrt(S / 128.0)
    S2A = np.exp(-2j * np.pi * (((2 * m2 + sig) * k2) % 32) / 32) / nrm
    S2B = np.exp(-2j * np.pi * (((2 * m2 + sig) * (31 - k2)) % 32) / 32) / nrm
    KrA, KiA = s2_kron(S2A, N2, False)
    KrB, KiB = s2_kron(S2B, N2, True)
    return s1, KrA, KiA, KrB, KiB


def ch_tables():
    c = np.arange(256)[:, None]
    l = np.arange(256)[None, :]
    ph = 2 * np.pi * ((c * l) % 256) / 256
    C = np.cos(ph) / 16.0
    S_ = np.sin(ph) / 16.0
    return np.stack([np.concatenate([C, -S_], 1), np.concatenate([S_, C], 1)], 0)


class Region:
    def __init__(self, ctx, off, n):
        self.ctx, self.off, self.n = ctx, off, n

    def f32(self):
        return self.ctx.arena[:, self.off:self.off + self.n]

    def bf(self):
        return self.ctx.arena[:, self.off:self.off + self.n].bitcast(BF16)

    def sub(self, lo, n):
        assert lo >= 0 and lo + n <= self.n, (lo, n, self.n)
        return Region(self.ctx, self.off + lo, n)

    def keys(self):
        return [("A", b) for b in range(self.off // BLK, (self.off + self.n - 1) // BLK + 1)]


def K(*items):
    out = []
    for it in items:
        if isinstance(it, Region):
            out.extend(it.keys())
        elif isinstance(it, (list, tuple)) and len(it) > 0 and isinstance(it[0], (Region, list)):
            out.extend(K(*it))
        elif isinstance(it, list):
            out.extend(it)
        else:
            out.append(it)
    return out


class Ctx:
    COMPUTE = ("pe", "act", "dve", "pool")

    def __init__(self, nc, stack, needed=None):
        self.nc = nc
        self.stack = stack
        self.dry = needed is None
        self.needed = needed
        self.need_out = {e: set() for e in self.COMPUTE}
        self.eng = {"pe": nc.tensor, "act": nc.scalar, "dve": nc.vector, "pool": nc.gpsimd, "sp": nc.sync}
        self.csem = {e: stack.enter_context(nc.semaphore("c_" + e)) for e in self.COMPUTE}
        self.ccnt = {e: 0 for e in self.COMPUTE}
        self.rank = None
        if needed is not None:
            self.rank = {}
            for e in self.COMPUTE:
                self.rank[e] = {idx: i + 1 for i, idx in enumerate(sorted(needed[e]))}
        self.dsem = {}
        self.dcnt = {}
        self.waited = {e: {} for e in self.eng}
        self.lastw = {}
        self.readers = {}
        self.nwaits = 0
        self.ninst = {e: 0 for e in self.eng}
        self.arena = stack.enter_context(nc.sbuf_tensor("arena", [128, NW], F32))
        self.top = 0
        self.bankrr = 0
        self.drrq = {}
        self.ptrr = 0

    def alloc(self, n):
        n_al = (n + BLK - 1) // BLK * BLK
        r = Region(self, self.top, n)
        self.top += n_al
        assert self.top <= NW, f"arena overflow {self.top} > {NW}"
        return r

    def mark(self):
        return self.top

    def release(self, m):
        self.top = m

    def _deps(self, e, reads, writes):
        deps = {}

        def add(t):
            if t is None:
                return
            s, v, src = t
            if src == "pe" and e == "pe":
                return
            if s not in deps or deps[s] < v:
                deps[s] = v
        for k in reads:
            add(self.lastw.get(k))
        for k in writes:
            add(self.lastw.get(k))
            for t in self.readers.get(k, ()):
                add(t)
        w = self.waited[e]
        for s, v in deps.items():
            if w.get(s, 0) >= v:
                continue
            w[s] = v
            self.nwaits += 1
            if s in self.COMPUTE:
                self.need_out[s].add(v)
                if not self.dry:
                    self.eng[e].wait_ge(self.csem[s], self.rank[s][v])
            else:
                if not self.dry:
                    self.eng[e].wait_ge(self.dsem[s], v)

    def _record(self, t, reads, writes):
        for k in writes:
            self.lastw[k] = t
            self.readers[k] = []
        for k in reads:
            self.readers.setdefault(k, []).append(t)

    def op(self, e, fn, reads=(), writes=()):
        reads = K(*reads)
        writes = K(*writes)
        self._deps(e, reads, writes)
        self.ccnt[e] += 1
        self.ninst[e] += 1
        idx = self.ccnt[e]
        if not self.dry:
            inst = fn()
            if idx in self.rank[e]:
                inst.then_inc(self.csem[e], 1)
        self._record((e, idx, e), reads, writes)

    NPOOL = 28

    def _dsem(self, name):
        if name not in self.dsem:
            self.dsem[name] = self.stack.enter_context(self.nc.semaphore("d_" + name))
            self.dcnt[name] = 0

    def dma(self, q, semname, fn, reads=(), writes=()):
        reads = K(*reads)
        writes = K(*writes)
        self._deps(q, reads, writes)
        rr = self.drrq.setdefault(q, 0)
        self.drrq[q] = rr + 1
        name = "%s%d" % (q, rr % self.NPOOL)
        self._dsem(name)
        prev = self.dcnt[name]
        if prev > 0 and self.waited[q].get(name, 0) < prev:
            self.waited[q][name] = prev
            self.nwaits += 1
            if not self.dry:
                self.eng[q].wait_ge(self.dsem[name], prev)
        self.dcnt[name] += 16
        self.ninst[q] += 1
        if not self.dry:
            fn().then_inc(self.dsem[name], 16)
        self._record((name, self.dcnt[name], "dma"), reads, writes)

    def cc(self, semname, fn, reads=(), writes=()):
        reads = K(*reads)
        writes = K(*writes)
        self._deps("pool", reads, writes)
        self._dsem(semname)
        self.dcnt[semname] += 1
        if not self.dry:
            fn().then_inc(self.dsem[semname], 1)
        self._record((semname, self.dcnt[semname], "cc"), reads, writes)

    def finish(self, e="sp"):
        allk = list(set(self.lastw) | set(self.readers))
        self._deps(e, allk, allk)


STOP = None
DEBUG = ()


class _Stop(Exception):
    pass


def _emit(nc, c):
    try:
        _emit_body(nc, c)
    except _Stop:
        pass
    c.finish("sp")


def _emit_body(nc, c):
    def chk(name):
        if STOP == name:
            raise _Stop()

    def dump(name, ap_fn, shape, reads, dt=F32):
        if name not in DEBUG:
            return
        d = nc.dram_tensor("dbg_" + name, list(shape), dt, kind="ExternalOutput").ap()
        c.dma("sp", "dbg_" + name, lambda: nc.sync.dma_start(out=d, in_=ap_fn()), reads=reads, writes=[("dbg", name)])
    eng = c.eng
    pe, act, dve, pool, sp = nc.tensor, nc.scalar, nc.vector, nc.gpsimd, nc.sync

    def din(name, shape, dt=F32):
        return nc.dram_tensor(name, list(shape), dt, kind="ExternalInput").ap()

    def dout(name, shape, dt=F32):
        return nc.dram_tensor(name, list(shape), dt, kind="ExternalOutput").ap()

    def dint(name, shape, dt):
        return nc.dram_tensor(name, list(shape), dt, kind="Internal").ap()

    xs_in = din("xs_in", [TS, D]); pos_in = din("pos_in", [TS, D])
    xs_res = din("xs_res", [TS, D]); pos_res = din("pos_res", [TS, D])
    xp = din("xp", [2 * TP, D])
    condT = din("condT", [128, 16]); sel_d = din("sel", [128, 2]); st0_d = din("st0", [128, 16])
    w_ada = [din("w_ada_0", [D, 3 * D]), din("w_ada_1", [D, 3 * D])]
    badar_d = [din("badar_0", [1, 3072]), din("badar_1", [1, 3072])]
    ngr_d = [din("ngr_0", [1, 1024]), din("ngr_1", [1, 1024])]
    fgr_d = din("fgr", [1, 1024])
    w_in = [din("w_in_0", [D, 2 * DI]), din("w_in_1", [D, 2 * DI])]
    w_fmix = din("w_fmix", [8, 256, 256]); bfm_d = din("bfm", [128, 16])
    w_out = [din("w_out_0", [DI, D]), din("w_out_1", [DI, D])]
    convw_d = din("convw", [128, 80]); convb_d = din("convb", [128, 16])
    wr_d = din("wr", [2, 8, 256, 256]); wi_d = din("wi", [2, 8, 256, 256])
    br_d = din("br", [128, 32]); bi_d = din("bi", [128, 32]); lam_d = din("lam", [128, 32])
    ident_d = din("ident", [128, 128])
    s1s_d = din("s1s", [16, 128, 256]); s1p_d = din("s1p", [2, 128, 256])
    kmat_d = din("kmat", [6, 128, 128]); chtab_d = din("chtab", [2, 256, 512])
    ys = dout("ys", [TS, D]); yp = dout("yp", [2 * TP, D]); stp = dout("stp", [128, 64])
    rs_in = [dint(f"rs_in{g}", [512, TS], BF16) for g in range(8)]
    rs_out = [dint(f"rs_out{g}", [256, TS], BF16) for g in range(8)]
    halo_in = dint("halo_in", [128, 32], F32); halo_out = dint("halo_out", [256, 32], F32)
    car_in = [dint(f"car_in{h}", [128, 2], F32) for h in range(8)]
    car_out = [dint(f"car_out{h}", [256, 2], F32) for h in range(8)]
    RG = [[0, 1], [2, 3], [4, 5], [6, 7]]

    psb = [c.stack.enter_context(nc.psum_tensor(f"ps{i}", [128, 512], F32)) for i in range(6)]
    ptb = [c.stack.enter_context(nc.psum_tensor(f"pt{i}", [128, 1024], BF16)) for i in range(2)]

    def bank():
        b = c.bankrr
        c.bankrr = (b + 1) % 6
        return b, psb[b], ("ps", b)

    def tbank():
        b = c.ptrr
        c.ptrr = (b + 1) % 2
        return ptb[b], ("pt", b)

    evac_rr = [0]

    def evac_copy(out_ap, in_ap, reads, writes, flip=True):
        if flip:
            evac_rr[0] ^= 1
        if evac_rr[0]:
            c.op("act", lambda: act.copy(out=out_ap, in_=in_ap), reads, writes)
        else:
            c.op("dve", lambda: dve.tensor_copy(out=out_ap, in_=in_ap), reads, writes)

    x1 = c.alloc(16 * 1024)
    hT = c.alloc(8 * 1024)
    smalls = c.alloc(1280)
    so = [0]

    def S(n):
        r = smalls.sub(so[0], n)
        so[0] += n
        return r
    identb = S(64)
    ss = S(32); rstd = S(32)
    scond = S(16); selr = S(2); st0 = S(16); zcol = S(1)
    bfm = S(16); convw = S(80); convb = S(16)
    brc = S(32); bic = S(32); cl = S(32); cl2 = S(32)
    halo = S(32); hg = S(64); c1 = S(2); c2 = S(2); cg = S(4); ctmp = S(32)
    stpo = S(64)
    diagb = [S(64) for _ in range(5)]
    hTv = hT.bf().rearrange("p (k t) -> p k t", k=8)
    x1v = x1.f32().rearrange("p (t d) -> p t d", t=16)

    def x1t(t):
        return x1.sub(t * 1024, 1024)

    def hTk(kc, c0, n):
        return hT.sub(kc * 1024 + c0 // 2, (n + 1) // 2)

    def hT_cols(c0, n):
        return [hTk(kc, c0, n) for kc in range(8)]

    def ld(r, src, sem="small"):
        c.dma("sp", sem, lambda: sp.dma_start(out=r.f32(), in_=src), writes=[r])
    ld(scond, condT); ld(selr, sel_d); ld(st0, st0_d)
    ld(bfm, bfm_d); ld(convw, convw_d); ld(convb, convb_d)
    ld(brc, br_d); ld(bic, bi_d); ld(cl, lam_d)
    c.dma("pool", "identb", lambda: pool.dma_start(out=identb.bf(), in_=ident_d), writes=[identb])
    c.op("dve", lambda: dve.memset(zcol.f32(), 0.0), writes=[zcol])
    c.op("act", lambda: act.activation(out=scond.f32(), in_=scond.f32(), func=AF.Silu), reads=[scond], writes=[scond])
    c.op("act", lambda: act.activation(out=cl.f32(), in_=cl.f32(), func=AF.Exp, scale=-1.0), reads=[cl], writes=[cl])
    c.op("act", lambda: act.activation(out=cl.f32(), in_=cl.f32(), func=AF.Ln, bias=1.0), reads=[cl], writes=[cl])
    c.op("dve", lambda: dve.tensor_scalar(out=cl2.f32(), in0=cl.f32(), scalar1=-4.0, scalar2=None, op0=ALU.mult), reads=[cl], writes=[cl2])
    c.op("dve", lambda: dve.tensor_scalar(out=cl.f32(), in0=cl.f32(), scalar1=-8.0, scalar2=None, op0=ALU.mult), reads=[cl], writes=[cl])
    c.op("dve", lambda: dve.tensor_scalar(out=brc.f32(), in0=brc.f32(), scalar1=0.5, scalar2=None, op0=ALU.mult), reads=[brc], writes=[brc])
    c.op("dve", lambda: dve.tensor_scalar(out=bic.f32(), in0=bic.f32(), scalar1=0.5, scalar2=None, op0=ALU.mult), reads=[bic], writes=[bic])

    ada_dram = [dint(f"ada_rows{l}", [2, 3072], F32) for l in range(2)]

    def ada_layer(l):
        mk = c.mark()
        NS = 3
        wst = [c.alloc(3072) for _ in range(NS)]
        arow = c.alloc(3072); brow = c.alloc(3072); grow = c.alloc(1024)
        c.dma("sp", "brow", lambda: sp.dma_start(out=brow.f32()[0:2, :], in_=badar_d[l][0, :].partition_broadcast(2)), writes=[brow])
        c.dma("sp", "grow", lambda: sp.dma_start(out=grow.f32()[0:2, :], in_=ngr_d[l][0, :].partition_broadcast(2)), writes=[grow])
        for kc in range(8):
            r = wst[kc % NS]
            c.dma("sp", "wada", lambda kc=kc, r=r: sp.dma_start(out=r.f32(), in_=w_ada[l][kc * 128:(kc + 1) * 128, :]), writes=[r])
            for n in range(6):
                c.op("pe", lambda kc=kc, n=n, r=r: pe.matmul(psb[n][0:2, :], lhsT=scond.f32()[:, kc * 2:kc * 2 + 2], rhs=r.f32()[:, n * 512:(n + 1) * 512], start=(kc == 0), stop=(kc == 7)),
                     reads=[r, scond], writes=[("ps", n)])
        for n in range(6):
            c.op("dve", lambda n=n: dve.tensor_tensor(out=arow.f32()[0:2, n * 512:(n + 1) * 512], in0=psb[n][0:2, :], in1=brow.f32()[0:2, n * 512:(n + 1) * 512], op=ALU.add),
                 reads=[("ps", n), brow], writes=[arow.sub(n * 512, 512)])
        c.op("dve", lambda: dve.scalar_tensor_tensor(out=arow.f32()[0:2, 1024:2048], in0=arow.f32()[0:2, 1024:2048], scalar=1.0, in1=grow.f32()[0:2, :], op0=ALU.add, op1=ALU.mult),
             reads=[arow, grow], writes=[arow.sub(1024, 1024)])
        c.dma("sp", "adaout", lambda: sp.dma_start(out=ada_dram[l], in_=arow.f32()[0:2, :]), reads=[arow], writes=[("ada", l)])
        c.release(mk)

    WHICH = {"shift": 0, "m": 1, "gate": 2}

    def bcast_ada(l, j, which, dst):
        w = WHICH[which]
        c.dma("sp", "bc", lambda: sp.dma_start(out=dst.f32(), in_=ada_dram[l][j, w * 1024:(w + 1) * 1024].partition_broadcast(128)), reads=[("ada", l)], writes=[dst])

    ada_layer(0)

    def front_end(ntiles, col0, l, j, xsrc=None, possrc=None, xdst=None, order=None, after_first=None):
        mk = c.mark()
        m_bc = c.alloc(1024); sh_bc = c.alloc(1024)
        NSL = 3
        NLD = 4 if xdst is None else 5
        tmp = [c.alloc(1024) for _ in range(NSL)]
        hb = [c.alloc(512) for _ in range(NSL)]
        xt = [c.alloc(1024) for _ in range(NLD)] if xdst is None else None
        posb = [c.alloc(1024) for _ in range(NLD)] if possrc is not None else None
        bcast_ada(l, j, "m", m_bc)
        bcast_ada(l, j, "shift", sh_bc)
        tl = list(order if order is not None else range(ntiles))
        nt = len(tl)

        def slot(it):
            t = tl[it]
            s_ = it % NSL
            sl = it % NLD
            xr = xdst(t) if xdst is not None else xt[sl]
            return t, s_, sl, xr

        def stA(it):
            t, s, sl, xr = slot(it)
            if xsrc is not None:
                c.dma("sp", f"x{s}", lambda: sp.dma_start(out=xr.f32(), in_=xsrc[t * 128:(t + 1) * 128, :]), writes=[xr])
                if possrc is not None:
                    c.dma("sp", f"p{sl}", lambda: sp.dma_start(out=posb[sl].f32(), in_=possrc[t * 128:(t + 1) * 128, :]), writes=[posb[sl]])
                    c.op("dve", lambda: dve.tensor_tensor(out=xr.f32(), in0=xr.f32(), in1=posb[sl].f32(), op=ALU.add), reads=[xr, posb[sl]], writes=[xr])
            ssr = ss.sub(t, 1); rsr = rstd.sub(t, 1)
            c.op("act", lambda: act.activation(out=tmp[s].f32(), in_=xr.f32(), func=AF.Square, accum_out=ssr.f32()), reads=[xr], writes=[tmp[s], ssr])
            c.op("act", lambda: act.activation(out=rsr.f32(), in_=ssr.f32(), func=AF.Sqrt, scale=1.0 / D, bias=EPS), reads=[ssr], writes=[rsr])

        def stB(it):
            t, s, sl, xr = slot(it)
            rsr = rstd.sub(t, 1)
            c.op("dve", lambda: dve.reciprocal(out=rsr.f32(), in_=rsr.f32()), reads=[rsr], writes=[rsr])
            c.op("dve", lambda: dve.scalar_tensor_tensor(out=tmp[s].f32(), in0=xr.f32(), scalar=rsr.f32(), in1=m_bc.f32(), op0=ALU.mult, op1=ALU.mult),
                 reads=[xr, rsr, m_bc], writes=[tmp[s]])
            c.op("dve", lambda: dve.tensor_tensor(out=hb[s].bf(), in0=tmp[s].f32(), in1=sh_bc.f32(), op=ALU.add), reads=[tmp[s], sh_bc], writes=[hb[s]])

        ptbank = {}

        def stC(it):
            t, s, sl, xr = slot(it)
            pt, ptk = tbank()
            ptbank[it] = (pt, ptk)
            for kc in range(8):
                c.op("pe", lambda kc=kc: pe.transpose(pt[:, kc * 128:(kc + 1) * 128], hb[s].bf()[:, kc * 128:(kc + 1) * 128], identb.bf()),
                     reads=[hb[s], identb], writes=[ptk])

        def stD(it):
            t, s, sl, xr = slot(it)
            pt, ptk = ptbank.pop(it)
            cc0 = col0 + t * 128
            c.op("act", lambda: act.copy(out=hTv[:, :, cc0:cc0 + 128], in_=pt[:, :].rearrange("p (k t) -> p k t", k=8)),
                 reads=[ptk], writes=hT_cols(cc0, 128))

        for k in range(nt + 3):
            if k < nt:
                stA(k)
            if 0 <= k - 1 < nt:
                stB(k - 1)
            if 0 <= k - 2 < nt:
                stC(k - 2)
            if 0 <= k - 3 < nt:
                stD(k - 3)
        c.release(mk)

    def load_w_in(l, colstart, dst, sem):
        src = w_in[l].rearrange("(k p) n -> p k n", p=128)[:, :, colstart:colstart + 256]
        c.dma("pool", sem, lambda: pool.dma_start(out=dst.bf().rearrange("p (k n) -> p k n", k=8), in_=src), writes=[dst])

    def pieces_of(seqs):
        out = []
        for (c0, T) in seqs:
            for o in range(0, T, 512):
                out.append((c0 + o, min(512, T - o)))
        return out

    def l0_pre_seq(g, wu, col0, T, N2, s1r, slots, dest, u_tm, Yt, Qt):
        NI = 128 // N2
        wuv = wu.bf().rearrange("p (k n) -> p k n", k=8)
        utv = u_tm.bf().rearrange("p (m n) -> p m n", n=256)
        s1v = s1r.bf().rearrange("p (m n) -> p m n", n=256)
        Ytv = Yt.bf().rearrange("p (c q j m i) -> p c q j m i", c=2, q=2, j=N2, m=N2)
        for m2 in range(N2):
            b, ps, pk = bank()
            for kc in range(8):
                c.op("pe", lambda kc=kc, m2=m2: pe.matmul(ps[:, 0:256], lhsT=hTv[:, kc, col0 + m2:col0 + T:N2], rhs=wuv[:, kc, :], start=(kc == 0), stop=(kc == 7)),
                     reads=[hTk(kc, col0, T), wu], writes=[pk])
            ur = u_tm.sub(m2 * 128, 128)
            evac_copy(utv[:, m2, :], ps[:, 0:256], [pk], [ur])
        chk("pre_u")
        for m2 in range(N2):
            ur = u_tm.sub(m2 * 128, 128)
            for cc in range(2):
                b, ps, pk = bank()
                c.op("pe", lambda m2=m2, cc=cc: pe.matmul(ps[:, 0:256], lhsT=utv[:, m2, cc * 128:(cc + 1) * 128], rhs=s1v[:, m2, :], start=True, stop=True),
                     reads=[ur, s1r.sub(m2 * 128, 128)], writes=[pk])
                evac_copy(Ytv[:, cc, :, :, m2, :], ps[:, 0:256].rearrange("p (q j i) -> p q j i", q=2, j=N2), [pk], [Yt])
        chk("pre_s1")
        chv = chtab.bf().rearrange("p (q c n) -> p q c n", q=2, c=2)
        def s2_stage(j, q):
            b2, ps2, pk2 = bank()
            for si, (Kr, Ki, flip) in enumerate(slots):
                for lc in range(2):
                    o = ps2[:, (si * 2 + lc) * 128:(si * 2 + lc + 1) * 128]
                    c.op("pe", lambda o=o, lc=lc, Kr=Kr: pe.matmul(o, lhsT=q.bf()[:, lc * 128:(lc + 1) * 128], rhs=Kr.bf(), start=True, stop=False),
                         reads=[q, Kr], writes=[pk2])
                    c.op("pe", lambda o=o, lc=lc, Ki=Ki: pe.matmul(o, lhsT=q.bf()[:, 256 + lc * 128:256 + (lc + 1) * 128], rhs=Ki.bf(), start=False, stop=True),
                         reads=[q, Ki], writes=[pk2])
            for si, (Kr, Ki, flip) in enumerate(slots):
                jb = (N2 - 1 - j) if flip else j
                for lc in range(2):
                    o = ps2[:, (si * 2 + lc) * 128:(si * 2 + lc + 1) * 128].rearrange("p (k i) -> p k i", i=NI)
                    dap, dreg = dest(si, lc)
                    d = dap.rearrange("p (k c) -> p k c", c=128)[:, :, jb * NI:(jb + 1) * NI]
                    evac_copy(d, o, [pk2], [dreg], flip=(si == 0 and lc == 0))

        prev = None
        for j in range(N2):
            b, ps, pk = bank()
            n = 0
            for cc in range(2):
                for comp in range(2):
                    c.op("pe", lambda cc=cc, comp=comp, n=n: pe.matmul(ps[:, :], lhsT=Ytv[:, cc, comp, j, :, :].rearrange("p m i -> p (m i)"), rhs=chv[:, comp, cc, :], start=(n == 0), stop=(n == 3)),
                         reads=[Yt, chtab], writes=[pk])
                    n += 1
            q = Qt[j % len(Qt)]
            evac_copy(q.bf(), ps[:, :], [pk], [q])
            if prev is not None:
                s2_stage(*prev)
            prev = (j, q)
        s2_stage(*prev)

    def wout_load(l, hd, gate_bc, wstg, wob):
        src = w_out[l][hd * 256:(hd + 1) * 256, :].rearrange("(l p) d -> p l d", p=128)
        if wstg is not None:
            c.dma("sp", "wo", lambda: sp.dma_start(out=wstg.f32().rearrange("p (l d) -> p l d", l=2), in_=src), writes=[wstg])
            for lc in range(2):
                c.op("pool", lambda lc=lc: pool.tensor_tensor(out=wob.bf()[:, lc * 1024:(lc + 1) * 1024], in0=wstg.f32()[:, lc * 1024:(lc + 1) * 1024], in1=gate_bc.f32(), op=ALU.mult),
                     reads=[wstg.sub(lc * 1024, 1024), gate_bc], writes=[wob.sub(lc * 512, 512)])
        else:
            c.dma("pool", "wo", lambda: pool.dma_start(out=wob.bf().rearrange("p (l d) -> p l d", l=2), in_=src), writes=[wob])
            for lc in range(2):
                c.op("pool", lambda lc=lc: pool.tensor_tensor(out=wob.bf()[:, lc * 1024:(lc + 1) * 1024], in0=wob.bf()[:, lc * 1024:(lc + 1) * 1024], in1=gate_bc.f32(), op=ALU.mult),
                     reads=[wob.sub(lc * 512, 512), gate_bc], writes=[wob.sub(lc * 512, 512)])

    def wout_update(l, hd, gT, gate_bc, wstg, wob, seqs_tiles, load=True):
        if load:
            wout_load(l, hd, gate_bc, wstg, wob)
        gv = gT.bf().rearrange("p (l t) -> p l t", l=2)
        Tt = gT.n
        for (tile, col) in seqs_tiles:
            for half in range(2):
                b, ps, pk = bank()
                for lc in range(2):
                    c.op("pe", lambda lc=lc, col=col, half=half: pe.matmul(ps[:, :], lhsT=gv[:, lc, col:col + 128], rhs=wob.bf()[:, lc * 1024 + half * 512:lc * 1024 + (half + 1) * 512], start=(lc == 0), stop=(lc == 1)),
                         reads=[gT.sub(lc * (Tt // 2) + col // 2, 64), wob.sub(lc * 512 + half * 256, 256)], writes=[pk])
                xr = x1t(tile).sub(half * 512, 512)
                c.op("dve", lambda xr=xr: dve.tensor_tensor(out=xr.f32(), in0=ps[:, :], in1=xr.f32(), op=ALU.add), reads=[pk, xr], writes=[xr])

    def l0_post_group(g, Ttot, seqs, fT, wz, sz, wf, gT, gate_bc, wstg, wob, tile0):
        wzv = wz.bf().rearrange("p (k n) -> p k n", k=8)
        szv = sz.bf().rearrange("p (l t) -> p l t", l=2)
        fv = fT.bf().rearrange("p (l t) -> p l t", l=2)
        gv = gT.bf().rearrange("p (l t) -> p l t", l=2)
        wfv = wf.bf().rearrange("p (l n) -> p l n", l=2)
        pcs = pieces_of(seqs)
        for jc in range(2):
            for (c0, n) in pcs:
                b, ps, pk = bank()
                for kc in range(8):
                    c.op("pe", lambda kc=kc, jc=jc, c0=c0, n=n: pe.matmul(ps[:, 0:n], lhsT=wzv[:, kc, jc * 128:(jc + 1) * 128], rhs=hTv[:, kc, c0:c0 + n], start=(kc == 0), stop=(kc == 7)),
                         reads=[wz, hTk(kc, c0, n)], writes=[pk])
                r = sz.sub(jc * (Ttot // 2) + c0 // 2, n // 2)
                c.op("act", lambda jc=jc, c0=c0, n=n: act.activation(out=szv[:, jc, c0:c0 + n], in_=ps[:, 0:n], func=AF.Silu), reads=[pk], writes=[r])
        for jc in range(2):
            for (c0, n) in pcs:
                b, ps, pk = bank()
                for lc in range(2):
                    c.op("pe", lambda lc=lc, jc=jc, c0=c0, n=n: pe.matmul(ps[:, 0:n], lhsT=wfv[:, lc, jc * 128:(jc + 1) * 128], rhs=fv[:, lc, c0:c0 + n], start=(lc == 0), stop=(lc == 1)),
                         reads=[wf, fT.sub(lc * (Ttot // 2) + c0 // 2, n // 2)], writes=[pk])
                rs_ = sz.sub(jc * (Ttot // 2) + c0 // 2, n // 2)
                rg = gT.sub(jc * (Ttot // 2) + c0 // 2, n // 2)
                bcol = bfm.f32()[:, g * 2 + jc:g * 2 + jc + 1]
                c.op("dve", lambda jc=jc, c0=c0, n=n, bcol=bcol: dve.scalar_tensor_tensor(out=gv[:, jc, c0:c0 + n], in0=ps[:, 0:n], scalar=bcol, in1=szv[:, jc, c0:c0 + n], op0=ALU.add, op1=ALU.mult),
                     reads=[pk, bfm, rs_], writes=[rg])
        tiles = [(tile0 + i, i * 128) for i in range(Ttot // 128)]
        wout_update(0, g, gT, gate_bc, wstg, wob, tiles)

    def load_tables():
        t = {}
        t["s1s"] = c.alloc(16 * 128); t["s1p"] = c.alloc(2 * 128)
        t["k"] = [c.alloc(64) for _ in range(6)]
        t["ch"] = c.alloc(2 * 2 * 256)
        c.dma("pool", "tab", lambda: pool.dma_start(out=t["s1s"].bf().rearrange("p (m n) -> p m n", m=16), in_=s1s_d.rearrange("m p n -> p m n")), writes=[t["s1s"]])
        c.dma("pool", "tab", lambda: pool.dma_start(out=t["s1p"].bf().rearrange("p (m n) -> p m n", m=2), in_=s1p_d.rearrange("m p n -> p m n")), writes=[t["s1p"]])
        for i in range(6):
            c.dma("pool", "tab", lambda i=i: pool.dma_start(out=t["k"][i].bf(), in_=kmat_d[i]), writes=[t["k"][i]])
        c.dma("pool", "tab", lambda: pool.dma_start(out=t["ch"].bf().rearrange("p (q c n) -> p q c n", q=2, c=2), in_=chtab_d.rearrange("q (c p) n -> p q c n", p=128)), writes=[t["ch"]])
        return t

    chk("setup")
    mkL0 = c.mark()
    tabs = load_tables()
    chtab = tabs["ch"]
    front_end(16, 0, 0, 0, xsrc=xs_in, possrc=pos_in, xdst=None)
    ada_layer(1)
    dump("hT_in", lambda: hT.bf(), [128, 8 * 2048], [hT], BF16)
    chk("fe_in")
    mk = c.mark()
    wus = [c.alloc(1024), c.alloc(1024)]
    u_tm = c.alloc(16 * 128); Yt = c.alloc(4096); Qt = [c.alloc(256) for _ in range(3)]
    send = [c.alloc(4096), c.alloc(4096)]
    slotsS = [(tabs["k"][0], tabs["k"][1], False), (tabs["k"][2], tabs["k"][3], True)]
    load_w_in(0, 0, wus[0], "wu0")
    for t in range(16):
        xr = x1t(t)
        c.dma("sp", "xres", lambda xr=xr, t=t: sp.dma_start(out=xr.f32(), in_=xs_res[t * 128:(t + 1) * 128, :]), writes=[xr])
    for g in range(8):
        if g + 1 < 8:
            load_w_in(0, (g + 1) * 256, wus[(g + 1) % 2], f"wu{(g + 1) % 2}")
        sd = send[g % 2]
        sdv = sd.bf().rearrange("p (s l t) -> p s l t", s=2, l=2)

        def dest(si, lc, sd=sd, sdv=sdv):
            return sdv[:, si, lc, :], sd.sub((si * 2 + lc) * 1024, 1024)
        l0_pre_seq(g, wus[g % 2], 0, TS, 16, tabs["s1s"], slotsS, dest, u_tm, Yt, Qt)
        chk("pre_s2")
        c.dma("sp", f"send{g % 2}", lambda g=g, sdv=sdv: sp.dma_start(out=rs_in[g].rearrange("(s l p) t -> p s l t", s=2, l=2), in_=sdv),
              reads=[sd], writes=[("rs_in", g)])
        chk("pre_send")
        c.cc(f"rs{g}", lambda g=g: pool.collective_compute("ReduceScatter", ALU.add, replica_groups=RG, ins=[rs_in[g]], outs=[rs_out[g]]),
             reads=[("rs_in", g)], writes=[("rs_out", g)])
        for t in (2 * g, 2 * g + 1):
            xr = x1t(t)
            c.dma("pool", "xres", lambda xr=xr, t=t: pool.dma_start(out=xr.f32(), in_=pos_res[t * 128:(t + 1) * 128, :], accum_op=ALU.add), reads=[xr], writes=[xr])
        chk("pre_rs")
    c.release(mk)
    chk("l0pre")
    front_end(16, 0, 0, 0, xsrc=None, possrc=None, xdst=x1t)
    chk("fe_res")
    mk = c.mark()
    gate_bc = c.alloc(1024); wstg = c.alloc(2048); wob = c.alloc(1024)
    wzs = [c.alloc(1024), c.alloc(1024)]
    sz = c.alloc(2048); fTs = [c.alloc(2048), c.alloc(2048)]; wf = [c.alloc(256), c.alloc(256)]; gT = c.alloc(2048)
    bcast_ada(0, 0, "gate", gate_bc)

    def load_post(g):
        load_w_in(0, DI + g * 256, wzs[g % 2], f"wz{g % 2}")
        c.dma("pool", f"wf{g % 2}", lambda: pool.dma_start(out=wf[g % 2].bf().rearrange("p (l n) -> p l n", l=2), in_=w_fmix[g].rearrange("(l p) n -> p l n", p=128)), writes=[wf[g % 2]])
        c.dma("sp", f"ft{g % 2}", lambda: sp.dma_start(out=fTs[g % 2].bf().rearrange("p (l t) -> p l t", l=2), in_=rs_out[g].rearrange("(l p) t -> p l t", p=128)),
              reads=[("rs_out", g)], writes=[fTs[g % 2]])
    load_post(0)
    for g in range(8):
        if g + 1 < 8:
            load_post(g + 1)
        l0_post_group(g, TS, [(0, TS)], fTs[g % 2], wzs[g % 2], sz, wf[g % 2], gT, gate_bc, wstg, wob, 0)
    c.release(mk)
    c.release(mkL0)
    dump("x1_l0", lambda: x1.f32(), [128, 16 * 1024], [x1])
    chk("l0post")

    def layer1(Ttot, seqs, j, is_sample, tile0):
        ntiles = Ttot // 128
        front_end(ntiles, 0, 1, j, xdst=lambda t: x1t(tile0 + t))
        mk = c.mark()
        RL = (sum(T + 4 for (_, T) in seqs) + BLK - 1) // BLK * BLK
        Rall = c.alloc(5 * RL)
        row = [Rall.sub(i * RL, RL) for i in range(5)]
        RLB = (RL // 2 + BLK - 1) // BLK * BLK
        U = [c.alloc(RLB), c.alloc(RLB)]
        TA = [row[0].sub(0, Ttot), row[2].sub(0, Ttot)]
        TI = [row[1].sub(0, Ttot), row[3].sub(0, Ttot)]
        ts = row[4].sub(0, Ttot)
        h1 = c.alloc(Ttot)
        gate_bc = c.alloc(1024); wob = c.alloc(1024)
        wu = c.alloc(1024); wz = c.alloc(1024)
        gw = [[c.alloc(256) for _ in range(4)] for _ in range(2)]
        ucbf = c.alloc(Ttot); sz = c.alloc(Ttot); gT = c.alloc(Ttot)
        bcast_ada(1, j, "gate", gate_bc)
        pcs = pieces_of(seqs)
        pads = []
        po = 0
        for (c0, T) in seqs:
            pads.append(po)
            po += T + 4

        if is_sample:
            wuh = [wu, wz]
            b, ps, pk = bank()
            for hd in range(8):
                load_w_in(1, hd * 256, wuh[hd % 2], f"wu{hd % 2}")
                wv = wuh[hd % 2].bf().rearrange("p (k n) -> p k n", k=8)
                for jc in range(2):
                    ch = hd * 2 + jc
                    for kc in range(8):
                        c.op("pe", lambda kc=kc, jc=jc, ch=ch, wv=wv: pe.matmul(ps[:, ch * 2:ch * 2 + 2], lhsT=wv[:, kc, jc * 128:(jc + 1) * 128], rhs=hTv[:, kc, Ttot - 2:Ttot], start=(kc == 0), stop=(kc == 7)),
                             reads=[wuh[hd % 2], hTk(kc, Ttot - 2, 2)], writes=[pk])
            c.op("dve", lambda: dve.tensor_copy(out=halo.f32(), in_=ps[:, 0:32]), reads=[pk], writes=[halo])
            c.dma("sp", "halo", lambda: sp.dma_start(out=halo_in, in_=halo.f32()), reads=[halo], writes=[("halo_in",)])
            c.cc("halocc", lambda: pool.collective_compute("AllGather", ALU.bypass, replica_groups=RG, ins=[halo_in], outs=[halo_out]),
                 reads=[("halo_in",)], writes=[("halo_out",)])
            c.dma("sp", "halo", lambda: sp.dma_start(out=hg.f32().rearrange("p (r n) -> p r n", r=2), in_=halo_out.rearrange("(r p) n -> p r n", p=128)),
                  reads=[("halo_out",)], writes=[hg])
            c.op("dve", lambda: dve.tensor_scalar(out=ctmp.f32(), in0=hg.f32()[:, 0:32], scalar1=selr.f32()[:, 0:1], scalar2=None, op0=ALU.mult), reads=[hg, selr], writes=[ctmp])
            c.op("dve", lambda: dve.scalar_tensor_tensor(out=halo.f32(), in0=hg.f32()[:, 32:64], scalar=selr.f32()[:, 1:2], in1=ctmp.f32(), op0=ALU.mult, op1=ALU.add),
                 reads=[hg, selr, ctmp], writes=[halo])

        wuv = wu.bf().rearrange("p (k n) -> p k n", k=8)
        wzv = wz.bf().rearrange("p (k n) -> p k n", k=8)
        ucv = ucbf.bf().rearrange("p (l t) -> p l t", l=2)
        szv = sz.bf().rearrange("p (l t) -> p l t", l=2)
        gv = gT.bf().rearrange("p (l t) -> p l t", l=2)
        h1v = h1.bf().rearrange("p (l t) -> p l t", l=2)

        def h1r(jc, c0, n):
            return h1.sub(jc * (Ttot // 2) + c0 // 2, max(1, (n + 1) // 2))

        def load_wu(hd):
            load_w_in(1, hd * 256, wu, "wu1h")

        def loads(hd):
            load_w_in(1, DI + hd * 256, wz, "wz1h")
            gws = gw[hd % 2]
            for d in range(2):
                for gi, wsrc in enumerate((wr_d, wi_d)):
                    r = gws[d * 2 + gi]
                    c.dma("pool", f"gw{hd % 2}", lambda r=r, wsrc=wsrc, d=d: pool.dma_start(out=r.bf().rearrange("p (i n) -> p i n", i=2), in_=wsrc[d, hd].rearrange("(i p) n -> p i n", p=128)), writes=[r])

        def part_u(hd):
            for jc in range(2):
                for si, (c0, T) in enumerate(seqs):
                    for o in range(0, T, 512):
                        n = min(512, T - o)
                        b, ps, pk = bank()
                        for kc in range(8):
                            c.op("pe", lambda kc=kc, jc=jc, c0=c0, o=o, n=n: pe.matmul(ps[:, 0:n], lhsT=wuv[:, kc, jc * 128:(jc + 1) * 128], rhs=hTv[:, kc, c0 + o:c0 + o + n], start=(kc == 0), stop=(kc == 7)),
                                 reads=[wu, hTk(kc, c0 + o, n)], writes=[pk])
                        dcol = pads[si] + 2 + o
                        evac_copy(U[jc].bf()[:, dcol:dcol + n], ps[:, 0:n], [pk], [U[jc].sub(dcol // 2, (n + 1) // 2 + 1)])

        def part_z(hd):
            for jc in range(2):
                for (c0, n) in pcs:
                    b, ps, pk = bank()
                    for kc in range(8):
                        c.op("pe", lambda kc=kc, jc=jc, c0=c0, n=n: pe.matmul(ps[:, 0:n], lhsT=wzv[:, kc, jc * 128:(jc + 1) * 128], rhs=hTv[:, kc, c0:c0 + n], start=(kc == 0), stop=(kc == 7)),
                             reads=[wz, hTk(kc, c0, n)], writes=[pk])
                    c.op("act", lambda jc=jc, c0=c0, n=n: act.activation(out=szv[:, jc, c0:c0 + n], in_=ps[:, 0:n], func=AF.Silu), reads=[pk], writes=[sz.sub(jc * (Ttot // 2) + c0 // 2, n // 2)])

        def part_conv(hd):
            for jc in range(2):
                ch = hd * 2 + jc
                Ub = U[jc].bf()
                for si, (c0, T) in enumerate(seqs):
                    p0 = pads[si]
                    c.op("dve", lambda Ub=Ub, p0=p0: dve.memset(Ub[:, p0:p0 + 2], 0.0), writes=[U[jc].sub(p0 // 2, 2)])
                    if not is_sample:
                        c.op("dve", lambda Ub=Ub, p0=p0, T=T: dve.memset(Ub[:, p0 + T + 2:p0 + T + 4], 0.0), writes=[U[jc].sub((p0 + T + 2) // 2, 2)])
                if is_sample:
                    T = seqs[0][1]
                    for e in range(2):
                        c.op("dve", lambda Ub=Ub, ch=ch, e=e, T=T: dve.tensor_copy(out=Ub[:, T + 2 + e:T + 3 + e], in_=halo.f32()[:, ch * 2 + 1 - e:ch * 2 + 2 - e]),
                             reads=[halo], writes=[U[jc].sub((T + 2 + e) // 2, 1)])
                plist = []
                for si, (c0, T) in enumerate(seqs):
                    for o in range(0, T, 512):
                        plist.append((pads[si] + o, c0 + o, min(512, T - o)))
                banks = [bank() for _ in plist]
                for dd in range(5):
                    dg = diagb[dd]
                    c.op("dve", lambda dg=dg, ch=ch, dd=dd: dve.tensor_scalar(out=dg.bf(), in0=identb.bf(), scalar1=convw.f32()[:, ch * 5 + dd:ch * 5 + dd + 1], scalar2=None, op0=ALU.mult),
                         reads=[identb, convw], writes=[dg])
                    for (pcol, c0o, n), (b, ps, pk) in zip(plist, banks):
                        c.op("pe", lambda dg=dg, Ub=Ub, pcol=pcol, n=n, dd=dd, ps=ps: pe.matmul(ps[:, 0:n], lhsT=dg.bf(), rhs=Ub[:, pcol + dd:pcol + dd + n], start=(dd == 0), stop=(dd == 4)),
                             reads=[dg, U[jc].sub((pcol + dd) // 2, (n + 1) // 2 + 1)], writes=[pk])
                for (pcol, c0o, n), (b, ps, pk) in zip(plist, banks):
                    c.op("act", lambda ps=ps, c0o=c0o, n=n, ch=ch, jc=jc: act.activation(out=ucv[:, jc, c0o:c0o + n], in_=ps[:, 0:n], func=AF.Identity, bias=convb.f32()[:, ch:ch + 1]),
                         reads=[pk, convb], writes=[ucbf.sub(jc * (Ttot // 2) + c0o // 2, n // 2)])

        def stage(hd, d, jc, part="all"):
            gws = gw[hd % 2]
            ta, ti = TA[jc], TI[jc]
            ch = hd * 2 + jc
            colc = slice(d * 16 + ch, d * 16 + ch + 1)
            if part in ("all", "front"):
                stage_front(hd, d, jc, gws, ta, ti, ch, colc)
            if part in ("all", "back"):
                stage_back(hd, d, jc, ta, ti, ch)

        def stage_front(hd, d, jc, gws, ta, ti, ch, colc):
            for gi, (dst, bcol) in enumerate(((ta, brc), (ti, bic))):
                wv = gws[d * 2 + gi].bf().rearrange("p (i n) -> p i n", i=2)
                for (c0, n) in pcs:
                    b, ps, pk = bank()
                    for ic in range(2):
                        c.op("pe", lambda ic=ic, c0=c0, n=n, wv=wv: pe.matmul(ps[:, 0:n], lhsT=wv[:, ic, jc * 128:(jc + 1) * 128], rhs=ucv[:, ic, c0:c0 + n], start=(ic == 0), stop=(ic == 1)),
                             reads=[gws[d * 2 + gi], ucbf.sub(ic * (Ttot // 2) + c0 // 2, n // 2)], writes=[pk])
                    c.op("act", lambda c0=c0, n=n, dst=dst, bcol=bcol: act.activation(out=dst.f32()[:, c0:c0 + n], in_=ps[:, 0:n], func=AF.Tanh, scale=0.5, bias=bcol.f32()[:, d * 16 + ch:d * 16 + ch + 1]),
                         reads=[pk, bcol], writes=[dst.sub(c0, n)])
            c.op("act", lambda: act.activation(out=ts.f32(), in_=ta.f32(), func=AF.Exp, scale=cl.f32()[:, colc], bias=cl.f32()[:, colc]), reads=[ta, cl], writes=[ts])
            c.op("act", lambda: act.activation(out=ta.f32(), in_=ta.f32(), func=AF.Exp, scale=cl2.f32()[:, colc], bias=cl2.f32()[:, colc]), reads=[ta, cl2], writes=[ta])
            c.op("act", lambda: act.activation(out=ts.f32(), in_=ts.f32(), func=AF.Sqrt, scale=-1.0, bias=1.0), reads=[ts], writes=[ts])

        def stage_back(hd, d, jc, ta, ti, ch):
            c.op("dve", lambda: dve.scalar_tensor_tensor(out=ti.f32(), in0=ti.f32(), scalar=1.0, in1=ts.f32(), op0=ALU.add, op1=ALU.mult), reads=[ti, ts], writes=[ti])
            c.op("dve", lambda: dve.scalar_tensor_tensor(out=ti.f32(), in0=ti.f32(), scalar=0.5, in1=ucv[:, jc, :], op0=ALU.mult, op1=ALU.mult), reads=[ti, ucbf.sub(jc * (Ttot // 2), Ttot // 2)], writes=[ti])
            if d == 0:
                for si, (c0, T) in enumerate(seqs):
                    init = st0.f32()[:, ch:ch + 1] if is_sample else zcol.f32()
                    c.op("dve", lambda c0=c0, T=T, init=init: dve.tensor_tensor_scan(out=h1v[:, jc, c0:c0 + T], data0=ta.f32()[:, c0:c0 + T], data1=ti.f32()[:, c0:c0 + T], initial=init, op0=ALU.mult, op1=ALU.add),
                         reads=[ta, ti, st0, zcol], writes=[h1r(jc, c0, T)])
                    if is_sample:
                        c.op("dve", lambda c0=c0, T=T: dve.tensor_copy(out=c1.f32()[:, jc:jc + 1], in_=h1v[:, jc, c0 + T - 1:c0 + T]), reads=[h1r(jc, c0 + T - 2, 2)], writes=[c1])
                    else:
                        col = si * 32 + ch
                        c.op("dve", lambda c0=c0, T=T, col=col: dve.tensor_copy(out=stpo.f32()[:, col:col + 1], in_=h1v[:, jc, c0 + T - 1:c0 + T]), reads=[h1r(jc, c0 + T - 2, 2)], writes=[stpo.sub(col, 1)])
            else:
                if is_sample and jc == 0:
                    c.op("dve", lambda: dve.tensor_scalar(out=ctmp.f32()[:, 0:2], in0=cg.f32()[:, 0:2], scalar1=selr.f32()[:, 0:1], scalar2=None, op0=ALU.mult), reads=[cg, selr], writes=[ctmp])
                    c.op("dve", lambda: dve.scalar_tensor_tensor(out=c2.f32(), in0=cg.f32()[:, 2:4], scalar=selr.f32()[:, 1:2], in1=ctmp.f32()[:, 0:2], op0=ALU.mult, op1=ALU.add),
                         reads=[cg, selr, ctmp], writes=[c2])
                for si, (c0, T) in enumerate(seqs):
                    init = c2.f32()[:, jc:jc + 1] if is_sample else zcol.f32()
                    c.op("dve", lambda c0=c0, T=T, init=init: dve.tensor_tensor_scan(out=ti.f32()[:, c0:c0 + T][:, ::-1], data0=ta.f32()[:, c0:c0 + T][:, ::-1], data1=ti.f32()[:, c0:c0 + T][:, ::-1], initial=init, op0=ALU.mult, op1=ALU.add),
                         reads=[ta, ti, c2, zcol], writes=[ti])
                    if not is_sample:
                        col = si * 32 + 16 + ch
                        c.op("dve", lambda c0=c0, col=col: dve.tensor_copy(out=stpo.f32()[:, col:col + 1], in_=ti.f32()[:, c0:c0 + 1]), reads=[ti], writes=[stpo.sub(col, 1)])
                c.op("dve", lambda: dve.tensor_tensor(out=ti.f32(), in0=ti.f32(), in1=h1v[:, jc, :], op=ALU.add), reads=[ti, h1r(jc, 0, Ttot)], writes=[ti])
                c.op("dve", lambda: dve.tensor_tensor(out=gv[:, jc, :], in0=ti.f32(), in1=szv[:, jc, :], op=ALU.mult), reads=[ti, sz.sub(jc * (Ttot // 2), Ttot // 2)], writes=[gT.sub(jc * (Ttot // 2), Ttot // 2)])

        def exchange_send(hd):
            c.dma("sp", "car", lambda: sp.dma_start(out=car_in[hd], in_=c1.f32()), reads=[c1], writes=[("car_in", hd)])
            c.cc(f"carcc{hd}", lambda: pool.collective_compute("AllGather", ALU.bypass, replica_groups=RG, ins=[car_in[hd]], outs=[car_out[hd]]),
                 reads=[("car_in", hd)], writes=[("car_out", hd)])
            c.dma("sp", "car", lambda: sp.dma_start(out=cg.f32().rearrange("p (r n) -> p r n", r=2), in_=car_out[hd].rearrange("(r p) n -> p r n", p=128)),
                  reads=[("car_out", hd)], writes=[cg])

        tiles = [(tile0 + i, i * 128) for i in range(ntiles)]
        load_wu(0)
        loads(0)
        part_u(0)
        part_z(0)
        part_conv(0)
        for hd in range(8):
            if hd + 1 < 8:
                load_wu(hd + 1)
            stage(hd, 0, 0, "front")
            if hd > 0:
                wout_update(1, hd - 1, gT, gate_bc, None, wob, tiles, load=False)
            stage(hd, 0, 0, "back")
            stage(hd, 0, 1)
            if hd + 1 < 8:
                part_u(hd + 1)
            if is_sample:
                exchange_send(hd)
            wout_load(1, hd, gate_bc, None, wob)
            stage(hd, 1, 0)
            if hd + 1 < 8:
                loads(hd + 1)
            stage(hd, 1, 1)
            if hd + 1 < 8:
                part_z(hd + 1)
                part_conv(hd + 1)
        wout_update(1, 7, gT, gate_bc, None, wob, tiles, load=False)
        c.release(mk)

    def final_out(ntiles, tile0, dst):
        mk = c.mark()
        fg_bc = c.alloc(1024)
        tmp = [c.alloc(1024), c.alloc(1024)]
        c.dma("sp", "bc", lambda: sp.dma_start(out=fg_bc.f32(), in_=fgr_d[0, :].partition_broadcast(128)), writes=[fg_bc])
        for t in range(ntiles):
            s = t % 2
            xr = x1t(tile0 + t)
            ssr = ss.sub(t, 1); rsr = rstd.sub(t, 1)
            c.op("act", lambda xr=xr, s=s, ssr=ssr: act.activation(out=tmp[s].f32(), in_=xr.f32(), func=AF.Square, accum_out=ssr.f32()), reads=[xr], writes=[tmp[s], ssr])
            c.op("act", lambda ssr=ssr, rsr=rsr: act.activation(out=rsr.f32(), in_=ssr.f32(), func=AF.Sqrt, scale=1.0 / D, bias=EPS), reads=[ssr], writes=[rsr])
            c.op("dve", lambda rsr=rsr: dve.reciprocal(out=rsr.f32(), in_=rsr.f32()), reads=[rsr], writes=[rsr])
            c.op("dve", lambda xr=xr, s=s, rsr=rsr: dve.scalar_tensor_tensor(out=tmp[s].f32(), in0=xr.f32(), scalar=rsr.f32(), in1=fg_bc.f32(), op0=ALU.mult, op1=ALU.mult),
                 reads=[xr, rsr, fg_bc], writes=[tmp[s]])
            c.dma("sp", f"out{s}", lambda s=s, t=t: sp.dma_start(out=dst[t * 128:(t + 1) * 128, :], in_=tmp[s].f32()), reads=[tmp[s]], writes=[("dout", id(dst), t)])
        c.release(mk)

    layer1(TS, [(0, TS)], 0, True, 0)
    dump("x1_l1", lambda: x1.f32(), [128, 16 * 1024], [x1])
    chk("l1s")
    final_out(16, 0, ys)
    chk("sample")

    mkP = c.mark()
    tabs = load_tables()
    chtab = tabs["ch"]
    front_end(4, 0, 0, 1, xsrc=xp, possrc=None, xdst=x1t)
    mk = c.mark()
    TT = 2 * TP
    wus = [c.alloc(1024), c.alloc(1024)]
    u_tm = c.alloc(2 * 128); Yt = c.alloc(512); Qt = [c.alloc(256) for _ in range(3)]
    gate_bc = c.alloc(1024); wstg = c.alloc(2048); wob = c.alloc(1024)
    wzs = [c.alloc(1024), c.alloc(1024)]
    sz = c.alloc(TT); fT = c.alloc(TT); wf = [c.alloc(256), c.alloc(256)]; gT = c.alloc(TT)
    bcast_ada(0, 1, "gate", gate_bc)
    slotsP = [(tabs["k"][4], tabs["k"][5], False)]
    fv = fT.bf().rearrange("p (l t) -> p l t", l=2)
    for g in range(8):
        load_w_in(0, g * 256, wus[g % 2], f"wu{g % 2}")
        load_w_in(0, DI + g * 256, wzs[g % 2], f"wz{g % 2}")
        c.dma("pool", f"wf{g % 2}", lambda g=g: pool.dma_start(out=wf[g % 2].bf().rearrange("p (l n) -> p l n", l=2), in_=w_fmix[g].rearrange("(l p) n -> p l n", p=128)), writes=[wf[g % 2]])
        for q in range(2):
            def dest(si, lc, q=q):
                return fv[:, lc, q * TP:(q + 1) * TP], fT.sub(lc * (TT // 2) + q * TP // 2, TP // 2)
            l0_pre_seq(g, wus[g % 2], q * TP, TP, 2, tabs["s1p"], slotsP, dest, u_tm, Yt, Qt)
        l0_post_group(g, TT, [(0, TP), (TP, TP)], fT, wzs[g % 2], sz, wf[g % 2], gT, gate_bc, wstg, wob, 0)
    c.release(mk)
    c.release(mkP)
    layer1(TT, [(0, TP), (TP, TP)], 1, False, 0)
    c.dma("sp", "stpo", lambda: sp.dma_start(out=stp, in_=stpo.f32()), reads=[stpo], writes=[("stp",)])
    final_out(4, 0, yp)


def build_nc():
    nc0 = bass.Bass("TRN2", target_bir_lowering=False)
    with ExitStack() as st:
        c0 = Ctx(nc0, st, needed=None)
        _emit(nc0, c0)
        needed = c0.need_out
    nc = bass.Bass("TRN2", target_bir_lowering=False)
    with ExitStack() as st:
        c = Ctx(nc, st, needed=needed)
        _emit(nc, c)
        print(f"[kernel] insts={c.ninst} waits={c.nwaits} incs={ {e: len(needed[e]) for e in needed} } arena_top={c.top}")
    return nc


def _col(v, nchunks):
    return np.ascontiguousarray(np.asarray(v, np.float32).reshape(nchunks, 128).T)


_NC_CACHE = {}
_RETURN_MAPS = False


def kernel(x_prompt, x_sample, state_lru_1, c, c_ctx,
           norm_g_0, w_ada_0, b_ada_0, w_in_0, w_fmix_0, b_fmix_0, w_out_0,
           norm_g_1, w_ada_1, b_ada_1, w_in_1, conv_w_1, conv_b_1,
           w_rgate_1, b_rgate_1, w_igate_1, b_igate_1, lam_1, w_out_1,
           final_g):
    f = lambda a: np.ascontiguousarray(np.asarray(a, np.float32))
    x_prompt, x_sample, state_lru_1, c, c_ctx = map(f, (x_prompt, x_sample, state_lru_1, c, c_ctx))
    pos = pos_embed(4096, D)
    ident = np.eye(128, dtype=np.float32)
    cht = ch_tables().astype(np.float32)
    tabs_s = [dft_tables_sample(s) for s in (0, 1)]
    tabs_p = [dft_tables_prompt(bool(s)) for s in (0, 1)]
    common = {
        "w_ada_0": f(w_ada_0), "w_ada_1": f(w_ada_1), "badar_0": f(b_ada_0).reshape(1, 3072), "badar_1": f(b_ada_1).reshape(1, 3072),
        "ngr_0": f(norm_g_0).reshape(1, 1024), "ngr_1": f(norm_g_1).reshape(1, 1024), "fgr": f(final_g).reshape(1, 1024),
        "w_in_0": f(w_in_0), "w_in_1": f(w_in_1), "w_fmix": f(w_fmix_0), "bfm": _col(b_fmix_0, 16),
        "w_out_0": f(w_out_0), "w_out_1": f(w_out_1), "convb": _col(conv_b_1, 16),
        "ident": ident, "chtab": cht,
    }
    cw = f(conv_w_1)
    zero = np.zeros_like(cw[0])
    in_maps = []
    for cid in range(NCORES):
        p, s = cid // 2, cid % 2
        m = dict(common)
        xs = x_sample[p]
        m["xs_in"] = np.ascontiguousarray(xs[s::2]); m["pos_in"] = np.ascontiguousarray(pos[s::2])
        if s == 0:
            m["xs_res"] = np.ascontiguousarray(xs[:TS]); m["pos_res"] = np.ascontiguousarray(pos[:TS])
        else:
            m["xs_res"] = np.ascontiguousarray(xs[::-1][:TS]); m["pos_res"] = np.ascontiguousarray(pos[::-1][:TS])
        xpp = x_prompt[2 * cid:2 * cid + 2]
        if s == 1:
            xpp = xpp[:, ::-1]
        m["xp"] = np.ascontiguousarray(xpp.reshape(2 * TP, D))
        condT = np.zeros((128, 16), np.float32)
        condT[:, 0::2] = _col(c[p], 8); condT[:, 1::2] = _col(c_ctx, 8)
        m["condT"] = condT
        sel = np.zeros((128, 2), np.float32); sel[:, 1 - s] = 1.0
        m["sel"] = sel
        m["st0"] = _col(state_lru_1[p, s], 16)
        taps = [cw[0], cw[1], cw[2], cw[3], zero] if s == 0 else [zero, cw[3], cw[2], cw[1], cw[0]]
        m["convw"] = np.ascontiguousarray(np.stack([_col(t, 16) for t in taps], axis=-1).reshape(128, 80))
        d1, d2 = (0, 1) if s == 0 else (1, 0)
        m["wr"] = np.ascontiguousarray(f(w_rgate_1)[[d1, d2]]); m["wi"] = np.ascontiguousarray(f(w_igate_1)[[d1, d2]])
        m["br"] = np.concatenate([_col(b_rgate_1[d1], 16), _col(b_rgate_1[d2], 16)], 1)
        m["bi"] = np.concatenate([_col(b_igate_1[d1], 16), _col(b_igate_1[d2], 16)], 1)
        m["lam"] = np.concatenate([_col(lam_1[d1], 16), _col(lam_1[d2], 16)], 1)
        s1, KrA, KiA, KrB, KiB = tabs_s[s]
        s1p, KrP, KiP = tabs_p[s]
        m["s1s"] = s1.astype(np.float32); m["s1p"] = s1p.astype(np.float32)
        m["kmat"] = np.stack([KrA, KiA, KrB, KiB, KrP, KiP]).astype(np.float32)
        in_maps.append({k: np.ascontiguousarray(v, dtype=np.float32) for k, v in m.items()})

    if _RETURN_MAPS:
        return in_maps
    if "nc" not in _NC_CACHE:
        _NC_CACHE["nc"] = build_nc()
    res = run_bass_kernel_spmd(_NC_CACHE["nc"], in_maps, core_ids=list(range(NCORES)))
    outs = res.results

    y_prompt = np.zeros((16, TP, D), np.float32)
    y_sample = np.zeros((4, 4096, D), np.float32)
    new_state = np.zeros((16, 2, DI), np.float32)
    for cid in range(NCORES):
        p, s = cid // 2, cid % 2
        ysv = np.asarray(outs[cid]["ys"], np.float32)
        ypv = np.asarray(outs[cid]["yp"], np.float32).reshape(2, TP, D)
        stv = np.asarray(outs[cid]["stp"], np.float32).reshape(128, 2, 2, 16)
        stv = stv.transpose(1, 2, 3, 0).reshape(2, 2, DI)
        if s == 0:
            y_sample[p, :TS] = ysv
            y_prompt[2 * cid:2 * cid + 2] = ypv
            new_state[2 * cid:2 * cid + 2] = stv
        else:
            y_sample[p, TS:] = ysv[::-1]
            y_prompt[2 * cid:2 * cid + 2] = ypv[:, ::-1]
            new_state[2 * cid:2 * cid + 2] = stv[:, ::-1]
    return (y_prompt, y_sample, new_state)
```
